# Optimizing a Trainium2 kernel written in Bass

```python
import math
import jax, jax.numpy as jnp
from jax import lax
import numpy as np

D_MODEL = 1024
BATCH = 4
SEQ = 8192
DEPTH = 1

HEAD_DIM = 64
A_Q_HEADS = 8
A_KV_HEADS = 2
A_GROUP = A_Q_HEADS // A_KV_HEADS
B_HEADS = 4
B_V_DIM = 2 * HEAD_DIM
A_Q = A_Q_HEADS * HEAD_DIM
A_KV = A_KV_HEADS * HEAD_DIM
B_QK = B_HEADS * 2 * HEAD_DIM
B_V = B_HEADS * B_V_DIM
MIX_WIDTH = A_Q + B_V
IN_WIDTH = A_Q + 2 * A_KV + 2 * B_QK + B_V
D_FF = 4 * D_MODEL
GRID_W = 64
ROPE_THETA = 10000.0
REL_BUCKETS = 32
REL_MAX_DIST = 128
Q_BLOCK = 128
EPS = 1e-6

kernel_name = "hymba_style_gqa_diffattn_encoder_layer"


def rmsnorm(x, g):
    xf = x.astype(jnp.float32)
    y = xf * lax.rsqrt(jnp.mean(xf * xf, axis=-1, keepdims=True) + EPS)
    return (y * g.astype(jnp.float32)).astype(x.dtype)


def axial_rope_tables(seq_len, dtype):
    rows = seq_len // GRID_W
    row = jnp.broadcast_to(jnp.arange(rows)[:, None], (rows, GRID_W)).reshape(-1).astype(jnp.float32)
    col = jnp.broadcast_to(jnp.arange(GRID_W)[None, :], (rows, GRID_W)).reshape(-1).astype(jnp.float32)
    half = HEAD_DIM // 2
    inv_freq = 1.0 / (ROPE_THETA ** (jnp.arange(0, half, 2, dtype=jnp.float32) / half))
    ang_r = row[:, None] * inv_freq[None, :]
    ang_c = col[:, None] * inv_freq[None, :]
    return tuple(t.astype(dtype) for t in (jnp.cos(ang_r), jnp.sin(ang_r), jnp.cos(ang_c), jnp.sin(ang_c)))


def rope_rotate(x, cos, sin):
    x1, x2 = jnp.split(x, 2, axis=-1)
    c = cos[None, :, None, :]
    s = sin[None, :, None, :]
    return jnp.concatenate([x1 * c - x2 * s, x2 * c + x1 * s], axis=-1)


def apply_axial_rope(x, tables):
    cr, sr, cc, sc = tables
    xr, xc = jnp.split(x, 2, axis=-1)
    return jnp.concatenate([rope_rotate(xr, cr, sr), rope_rotate(xc, cc, sc)], axis=-1)


def t5_bucket(rel):
    nb = REL_BUCKETS // 2
    max_exact = nb // 2
    ret = (rel > 0).astype(jnp.int32) * nb
    n = jnp.abs(rel)
    nf = jnp.maximum(n, 1).astype(jnp.float32)
    large = max_exact + (jnp.log(nf / max_exact) / math.log(REL_MAX_DIST / max_exact) * (nb - max_exact)).astype(jnp.int32)
    large = jnp.minimum(large, nb - 1)
    return ret + jnp.where(n < max_exact, n, large)


def gqa_axial_attention(q, k, v):
    B, S = q.shape[0], q.shape[1]
    nblk = S // Q_BLOCK
    scale = HEAD_DIM ** -0.5
    qb = q.reshape(B, nblk, Q_BLOCK, A_KV_HEADS, A_GROUP, HEAD_DIM).transpose(1, 0, 3, 4, 2, 5)
    kt = k.transpose(0, 2, 1, 3)
    vt = v.transpose(0, 2, 1, 3)

    def block(qblk):
        s = jnp.einsum('bkgqd,bksd->bkgqs', qblk, kt).astype(jnp.float32) * scale
        p = jax.nn.softmax(s, axis=-1)
        return jnp.einsum('bkgqs,bksd->bkgqd', p.astype(vt.dtype), vt)

    o = lax.map(block, qb)
    return o.transpose(1, 0, 4, 2, 3, 5).reshape(B, S, A_Q)


def differential_attention(q, k, v, lam, rel_bias):
    B, S = q.shape[0], q.shape[1]
    nblk = S // Q_BLOCK
    scale = HEAD_DIM ** -0.5
    qb = q.reshape(B, nblk, Q_BLOCK, B_HEADS, 2, HEAD_DIM).transpose(1, 0, 3, 4, 2, 5)
    kt = k.transpose(0, 2, 3, 1, 4)
    vt = v.transpose(0, 2, 1, 3)
    starts = jnp.arange(nblk, dtype=jnp.int32) * Q_BLOCK
    k_pos = jnp.arange(S, dtype=jnp.int32)
    lam32 = lam.astype(jnp.float32)

    def block(args):
        qblk, start = args
        q_pos = start + jnp.arange(Q_BLOCK, dtype=jnp.int32)
        bucket = t5_bucket(k_pos[None, :] - q_pos[:, None])
        bias = rel_bias[bucket].astype(jnp.float32).transpose(2, 0, 1)
        s = jnp.einsum('bhcqd,bhcsd->bhcqs', qblk, kt).astype(jnp.float32) * scale + bias[None, :, None]
        p = jax.nn.softmax(s, axis=-1)
        attn = p[:, :, 0] - lam32 * p[:, :, 1]
        return jnp.einsum('bhqs,bhse->bhqe', attn.astype(vt.dtype), vt)

    o = lax.map(block, (qb, starts))
    return o.transpose(1, 0, 3, 2, 4).reshape(B, S, B_HEADS, B_V_DIM)


def setup_inputs(seed: int = 0) -> dict:
    key = jax.random.key(seed)
    ks = jax.random.split(key, 20)
    f32 = jnp.float32
    nrm = lambda k, shape, s: jax.random.normal(k, shape, f32) * s
    return {
        "x": nrm(ks[0], (BATCH, SEQ, D_MODEL), 1.0),
        "attn_norm_g": 1.0 + nrm(ks[1], (DEPTH, D_MODEL), 0.05),
        "w_in": nrm(ks[2], (DEPTH, D_MODEL, IN_WIDTH), D_MODEL ** -0.5),
        "a_q_norm_g": 1.0 + nrm(ks[3], (DEPTH, HEAD_DIM), 0.05),
        "a_k_norm_g": 1.0 + nrm(ks[4], (DEPTH, HEAD_DIM), 0.05),
        "b_q_norm_g": 1.0 + nrm(ks[5], (DEPTH, HEAD_DIM), 0.05),
        "b_k_norm_g": 1.0 + nrm(ks[6], (DEPTH, HEAD_DIM), 0.05),
        "lambda_q1": nrm(ks[7], (DEPTH, HEAD_DIM), 0.1),
        "lambda_k1": nrm(ks[8], (DEPTH, HEAD_DIM), 0.1),
        "lambda_q2": nrm(ks[9], (DEPTH, HEAD_DIM), 0.1),
        "lambda_k2": nrm(ks[10], (DEPTH, HEAD_DIM), 0.1),
        "b_subln_g": 1.0 + nrm(ks[11], (DEPTH, B_V_DIM), 0.05),
        "rel_bias": nrm(ks[12], (REL_BUCKETS, B_HEADS), 0.5),
        "w_out": nrm(ks[13], (DEPTH, MIX_WIDTH, D_MODEL), MIX_WIDTH ** -0.5),
        "mlp_norm_g": 1.0 + nrm(ks[14], (DEPTH, D_MODEL), 0.05),
        "w_up": nrm(ks[15], (DEPTH, D_MODEL, D_FF), D_MODEL ** -0.5),
        "w_down": nrm(ks[16], (DEPTH, D_FF, D_MODEL), D_FF ** -0.5),
    }


def reference(x, attn_norm_g, w_in, a_q_norm_g, a_k_norm_g, b_q_norm_g, b_k_norm_g,
              lambda_q1, lambda_k1, lambda_q2, lambda_k2, b_subln_g, rel_bias,
              w_out, mlp_norm_g, w_up, w_down):
    B, S, _ = x.shape
    rope_tables = axial_rope_tables(S, x.dtype)
    split_pts = np.cumsum([A_Q, A_KV, A_KV, B_QK, B_QK])[:].tolist()
    for l in range(DEPTH):
        h = rmsnorm(x, attn_norm_g[l])
        proj = jnp.einsum('bsd,de->bse', h, w_in[l])
        qa, ka, va, qb, kb, vb = jnp.split(proj, split_pts, axis=-1)

        qa = rmsnorm(qa.reshape(B, S, A_Q_HEADS, HEAD_DIM), a_q_norm_g[l])
        ka = rmsnorm(ka.reshape(B, S, A_KV_HEADS, HEAD_DIM), a_k_norm_g[l])
        qa = apply_axial_rope(qa, rope_tables).reshape(B, S, A_KV_HEADS, A_GROUP, HEAD_DIM)
        ka = apply_axial_rope(ka, rope_tables)
        va = va.reshape(B, S, A_KV_HEADS, HEAD_DIM)
        out_a = gqa_axial_attention(qa, ka, va)

        lambda_init = 0.8 - 0.6 * math.exp(-0.3 * l)
        lam = (jnp.exp(jnp.sum(lambda_q1[l].astype(jnp.float32) * lambda_k1[l].astype(jnp.float32)))
               - jnp.exp(jnp.sum(lambda_q2[l].astype(jnp.float32) * lambda_k2[l].astype(jnp.float32)))
               + lambda_init)
        qb = rmsnorm(qb.reshape(B, S, B_HEADS, 2, HEAD_DIM), b_q_norm_g[l])
        kb = rmsnorm(kb.reshape(B, S, B_HEADS, 2, HEAD_DIM), b_k_norm_g[l])
        vb = vb.reshape(B, S, B_HEADS, B_V_DIM)
        ob = differential_attention(qb, kb, vb, lam, rel_bias)
        out_b = (rmsnorm(ob, b_subln_g[l]) * (1.0 - lambda_init)).reshape(B, S, B_V)

        mixed = jnp.concatenate([out_a, out_b], axis=-1)
        x = x + jnp.einsum('bse,ed->bsd', mixed, w_out[l])

        h = rmsnorm(x, mlp_norm_g[l])
        u = jnp.einsum('bsd,df->bsf', h, w_up[l])
        u = jnp.square(jax.nn.relu(u))
        x = x + jnp.einsum('bsf,fd->bsd', u, w_down[l])
    return x
```

```python
import contextlib
import math

import numpy as np
import concourse.bass as bass
import concourse.mybir as mybir
from concourse.bass_utils import run_bass_kernel_spmd

F32 = mybir.dt.float32
BF16 = mybir.dt.bfloat16
ALU = mybir.AluOpType
AF = mybir.ActivationFunctionType
AX = mybir.AxisListType

D_MODEL = 1024
SEQ = 8192
NQ = 4096
HD = 64
EPS = 1e-6
SCALE = HD ** -0.5
TT = 512
NT = SEQ // TT
NQB = NQ // TT
NKT = SEQ // 128
LAMBDA_INIT = 0.8 - 0.6 * math.exp(-0.3 * 0)


class Sched:
    EPOCH = 16000

    def __init__(self, nc, stack):
        self.nc = nc
        self.stack = stack
        self.names = ["pe", "act", "dve", "pool", "sp"]
        self.streams = {e: [] for e in self.names}
        self.esem = {}
        self.ecnt = {e: 0 for e in self.names}
        self.semobj = []
        self.slots = {}
        self.last_w = {}
        self.readers = {}
        self.waited = {e: {} for e in self.names}

    def _newsem(self, name):
        h = self.stack.enter_context(self.nc.semaphore(name))
        self.semobj.append(h)
        return len(self.semobj) - 1

    def _deps(self, eng, r, w):
        need = {}
        wt = self.waited[eng]

        def add(sid, val, teng):
            if eng == "pe" and teng == "pe":
                return
            if wt.get(sid, 0) >= val:
                return
            if need.get(sid, 0) < val:
                need[sid] = val

        for k in r:
            t = self.last_w.get(k)
            if t is not None:
                add(*t)
        for k in w:
            t = self.last_w.get(k)
            if t is not None:
                add(*t)
            rd = self.readers.get(k)
            if rd:
                for sid, (val, teng) in rd.items():
                    add(sid, val, teng)
        for sid, val in need.items():
            wt[sid] = val
            self.streams[eng].append(("w", sid, val))

    def _record(self, tok, r, w):
        for k in r:
            d = self.readers.setdefault(k, {})
            if d.get(tok[0], (0, None))[0] < tok[1]:
                d[tok[0]] = (tok[1], tok[2])
        for k in w:
            self.last_w[k] = tok
            self.readers[k] = {}

    def op(self, eng, fn, r=(), w=(), sig=True):
        self._deps(eng, r, w)
        if eng not in self.esem:
            self.esem[eng] = self._newsem("e_%s_%d" % (eng, len(self.semobj)))
        if sig:
            self.ecnt[eng] += 1
            tok = (self.esem[eng], self.ecnt[eng], eng)
            self.streams[eng].append(("o", fn, tok[0]))
            if self.ecnt[eng] >= self.EPOCH:
                self.esem[eng] = self._newsem("e_%s_%d" % (eng, len(self.semobj)))
                self.ecnt[eng] = 0
        else:
            tok = (self.esem[eng], self.ecnt[eng] + 1, eng)
            self.streams[eng].append(("o", fn, None))
        self._record(tok, r, w)

    def dma(self, q, fn, slot, r=(), w=()):
        self._deps(q, r, w)
        if slot not in self.slots or self.slots[slot][1] >= 30000:
            self.slots[slot] = [self._newsem("d_%d" % len(self.semobj)), 0]
        s = self.slots[slot]
        s[1] += 16
        tok = (s[0], s[1], None)
        self.streams[q].append(("d", fn, s[0]))
        self._record(tok, r, w)

    def barrier(self):
        toks = []
        for e, sid in self.esem.items():
            if self.ecnt[e] > 0:
                toks.append((sid, self.ecnt[e], e))
        for s in self.slots.values():
            if s[1] > 0:
                toks.append((s[0], s[1], None))
        for e in self.names:
            wt = self.waited[e]
            for sid, val, teng in toks:
                if e == "pe" and teng == "pe":
                    continue
                if wt.get(sid, 0) >= val:
                    continue
                wt[sid] = val
                self.streams[e].append(("w", sid, val))
        self.last_w.clear()
        self.readers.clear()

    def wait_slots(self, eng, slots):
        for sl in slots:
            s = self.slots[sl]
            self.streams[eng].append(("w", s[0], s[1]))

    def replay(self):
        nc = self.nc
        semobj = self.semobj
        streams = self.streams

        def run(name, e):
            for it in streams[name]:
                if it[0] == "w":
                    e.wait_ge(semobj[it[1]], it[2])
                elif it[0] == "o":
                    ins = it[1](e)
                    if it[2] is not None:
                        ins.then_inc(semobj[it[2]], 1)
                else:
                    it[1](e).then_inc(semobj[it[2]], 16)

        with nc.Block() as block:
            @block.tensor
            def _(e):
                run("pe", e)

            @block.scalar
            def _(e):
                run("act", e)

            @block.vector
            def _(e):
                run("dve", e)

            @block.gpsimd
            def _(e):
                run("pool", e)

            @block.sync
            def _(e):
                run("sp", e)
        for k in self.names:
            self.streams[k] = []


def bias_info(qb, kt):
    if kt < 32:
        o = 128 * kt - 512 * qb
        if o <= -256:
            return ("far", 0)
        if o >= 640:
            return ("far", 1)
        return ("toep", 512 - o)
    if qb == 7 and kt == 32:
        return ("cross", 0)
    if qb == 0 and kt == 63:
        return ("cross", 1)
    return ("far", 2)


def build_program():
    nc = bass.Bass("TRN2", target_bir_lowering=False)

    def din(name, shape):
        return nc.dram_tensor(name, list(shape), F32, kind="ExternalInput").ap()

    xT = din("xT", [1024, SEQ])
    wA = din("wA", [2, 128, 8, 896])
    wB = din("wB", [4, 128, 8, 384])
    gcol = din("gcol", [128, 24])
    lamv = din("lamv", [128, 256])
    ropeC = din("ropeC", [128, SEQ])
    ropeS = din("ropeS", [128, SEQ])
    btoep = din("btoep", [128, 4, 1152])
    bcross = din("bcross", [128, 4, 2, 512])
    cfar = din("cfar", [128, 12])
    cmat = din("cmat", [128, 4, 128])
    wout = din("wout", [128, 8, 1024])
    wup = din("wup", [8, 128, 8, 512])
    wdn = din("wdn", [8, 128, 32, 128])
    outT = nc.dram_tensor("outT", [1024, NQ], F32, kind="ExternalOutput").ap()
    wup_s = nc.dram_tensor("wup_s", [8, 128, 4096], BF16).ap()
    wdn_s = nc.dram_tensor("wdn_s", [8, 128, 4096], BF16).ap()
    mix_s = nc.dram_tensor("mix_s", [8, 128, NQ], BF16).ap()

    xT3 = xT.rearrange("(c p) t -> p c t", p=128)
    outT3 = outT.rearrange("(c p) t -> p c t", p=128)
    mix3 = mix_s.rearrange("c p t -> p c t")

    with contextlib.ExitStack() as top:
        S = Sched(nc, top)

        def sb(stack, name, shape, dt):
            return stack.enter_context(nc.sbuf_tensor(name, list(shape), dt))

        ps = top.enter_context(nc.psum_tensor("ps", [128, 8, 512], F32))
        ps7b = ps[:, 7, :].bitcast(BF16)

        def PS(b):
            return ("ps", b)

        identb = sb(top, "identb", [128, 128], BF16)
        swapf = sb(top, "swapf", [128, 128], F32)
        selA = sb(top, "selA", [128, 128], F32)
        selB = sb(top, "selB", [128, 128], F32)
        ones1024 = sb(top, "ones1024", [128, 128], BF16)
        blk64 = sb(top, "blk64", [128, 128], BF16)
        ones128 = sb(top, "ones128", [128, 128], BF16)
        onesb = sb(top, "onesb", [128, 128], BF16)
        gc = sb(top, "gc", [128, 24], F32)
        cfar_t = sb(top, "cfar_t", [128, 12], F32)
        epsc = sb(top, "epsc", [128, 1], F32)
        zeroc = sb(top, "zeroc", [128, 1], F32)
        neglam = sb(top, "neglam", [128, 1], F32)
        gsub = sb(top, "gsub", [128, 1], F32)
        lamt = sb(top, "lamt", [128, 256], F32)
        lamp = sb(top, "lamp", [128, 128], F32)
        lams = sb(top, "lams", [128, 4], F32)

        S.dma("pool", lambda e: e.dma_start(out=identb[:], in_=cmat[:, 0, :]), "c0", w=["identb"])
        S.dma("sp", lambda e: e.dma_start(out=swapf[:], in_=cmat[:, 1, :]), "c1", w=["swapf"])
        S.dma("sp", lambda e: e.dma_start(out=selA[:], in_=cmat[:, 2, :]), "c5", w=["selA"])
        S.dma("sp", lambda e: e.dma_start(out=selB[:], in_=cmat[:, 3, :]), "c6", w=["selB"])
        S.dma("sp", lambda e: e.dma_start(out=gc[:], in_=gcol), "c2", w=["gc"])
        S.dma("sp", lambda e: e.dma_start(out=cfar_t[:], in_=cfar), "c3", w=["cfar_t"])
        S.dma("sp", lambda e: e.dma_start(out=lamt[:], in_=lamv), "c4", w=["lamt"])
        S.op("pool", lambda e: e.memset(ones1024[:], 1.0 / 1024.0), w=["ones1024"])
        S.op("pool", lambda e: e.memset(ones128[:], 1.0 / 128.0), w=["ones128"])
        S.op("pool", lambda e: e.memset(onesb[:], 1.0), w=["onesb"])
        S.op("pool", lambda e: e.memset(blk64[:], 0.0), w=["blk64"])
        S.op("pool", lambda e: e.memset(blk64[0:64, 0:64], 1.0 / 64.0), w=["blk64"])
        S.op("pool", lambda e: e.memset(blk64[64:128, 64:128], 1.0 / 64.0), w=["blk64"])
        S.op("pool", lambda e: e.memset(epsc[:], EPS), w=["epsc"])
        S.op("pool", lambda e: e.memset(zeroc[:], 0.0), w=["zeroc"])

        S.op("dve", lambda e: e.tensor_tensor(out=lamp[:, 0:64], in0=lamt[:, 0:64], in1=lamt[:, 64:128], op=ALU.mult),
             r=["lamt"], w=["lamp"])
        S.op("dve", lambda e: e.tensor_tensor(out=lamp[:, 64:128], in0=lamt[:, 128:192], in1=lamt[:, 192:256], op=ALU.mult),
             r=["lamt"], w=["lamp"])
        S.op("dve", lambda e: e.tensor_reduce(out=lams[:, 0:1], in_=lamp[:, 0:64], axis=AX.X, op=ALU.add),
             r=["lamp"], w=["lams"])
        S.op("dve", lambda e: e.tensor_reduce(out=lams[:, 1:2], in_=lamp[:, 64:128], axis=AX.X, op=ALU.add),
             r=["lamp"], w=["lams"])
        S.op("act", lambda e: e.activation(out=lams[:, 2:4], in_=lams[:, 0:2], func=AF.Exp, bias=zeroc[:], scale=1.0),
             r=["lams", "zeroc"], w=["lams2"])
        S.op("dve", lambda e: e.tensor_tensor(out=lams[:, 0:1], in0=lams[:, 3:4], in1=lams[:, 2:3], op=ALU.subtract),
             r=["lams2"], w=["lams"])
        S.op("dve", lambda e: e.tensor_scalar(out=neglam[:], in0=lams[:, 0:1], scalar1=-LAMBDA_INIT, scalar2=None, op0=ALU.add),
             r=["lams"], w=["neglam"])
        S.op("dve", lambda e: e.tensor_scalar(out=gsub[:], in0=gc[:, 22:23], scalar1=1.0 - LAMBDA_INIT, scalar2=None, op0=ALU.mult),
             r=["gc"], w=["gsub"])

        with contextlib.ExitStack() as pst:
            wst = [sb(pst, "pw_st%d" % i, [128, 8, 512], F32) for i in range(2)]
            wbo = [sb(pst, "pw_bo%d" % i, [128, 4096], BF16) for i in range(2)]
            wb2 = [sb(pst, "pw_b2%d" % i, [128, 4096], BF16) for i in range(2)]
            for g in range(8):
                i = g % 2
                S.dma("sp", lambda e, g=g, i=i: e.dma_start(out=wst[i][:], in_=wup[g]), ("pwst", i), w=[("pwst", i)])
                for c in range(8):
                    S.op("dve", lambda e, c=c, i=i: e.tensor_scalar(
                        out=wbo[i][:, c * 512:(c + 1) * 512], in0=wst[i][:, c, :], scalar1=gc[:, 8 + c:9 + c],
                        scalar2=None, op0=ALU.mult), r=[("pwst", i), "gc"], w=[("pwbo", i)])
                S.dma("sp", lambda e, g=g, i=i: e.dma_start(out=wup_s[g], in_=wbo[i][:]), ("pwo", i),
                      r=[("pwbo", i)], w=[("wup_s", g)])
            for m in range(8):
                i = m % 2
                S.dma("pool", lambda e, m=m, i=i: e.dma_start(out=wb2[i][:], in_=wdn[m].rearrange("p j e -> p (j e)")),
                      ("pwb2", i), w=[("pwb2", i)])
                S.dma("sp", lambda e, m=m, i=i: e.dma_start(out=wdn_s[m], in_=wb2[i][:]), ("pwo2", i),
                      r=[("pwb2", i)], w=[("wdn_s", m)])
            S.barrier()
            S.replay()

        with contextlib.ExitStack() as ast:
            kTs = [sb(ast, "kT%d" % i, [128, SEQ], BF16) for i in range(2)]
            vaugs = [sb(ast, "vaug0", [128, NKT, 192], BF16), sb(ast, "vaug1", [128, NKT, 128], BF16)]
            qTs = [sb(ast, "qT0", [128, 2, NQ], BF16), sb(ast, "qT1", [128, 1, NQ], BF16)]
            pT = [sb(ast, "pT%d" % i, [128, 2, 512], BF16) for i in range(2)]
            wbf = sb(ast, "wbf", [128, 8, 896], BF16)
            wstage = sb(ast, "wstage", [128, 896], F32)
            xbf = [sb(ast, "xbf%d" % i, [128, 8, 512], BF16) for i in range(2)]
            xsq = sb(ast, "xsq", [128, 8, 512], BF16)
            rC = sb(ast, "rC", [128, 512], F32)
            rS = sb(ast, "rS", [128, 512], F32)
            ptm = [sb(ast, "ptm%d" % i, [128, 512], F32) for i in range(8)]
            et = [sb(ast, "et%d" % i, [128, 512], F32) for i in range(6)]
            sqt_p = sb(ast, "sqt_p", [128, 512], BF16)
            sqt_e = sb(ast, "sqt_e", [128, 512], BF16)
            vtmp = sb(ast, "vtmp", [128, 512], BF16)
            bt = sb(ast, "bt", [128, 1152], F32)
            bx = sb(ast, "bx", [128, 2, 512], F32)
            mst = [sb(ast, "mst%d" % i, [128, 512], BF16) for i in range(2)]

            def P(i):
                return ("ptm", i)

            def E(i):
                return ("et", i)

            S.op("pool", lambda e: e.memset(vaugs[0][:, :, 64:128], 1.0), w=[("vaug", 0)])

            def proj_gen(kind, idx, ub):
                kT, vaug, qT = kTs[ub], vaugs[ub], qTs[ub]
                KT, VA, QT = ("kT", ub), ("vaug", ub), ("qT", ub)
                if kind == "A":
                    wsrc, ncol = wA[idx], 896
                    tiles = [("q", 0, 128, 16, 17, 0), ("q", 256, 384, 16, 17, 1),
                             ("k", 512, 640, 18, 19, None), ("v", 768, None, None, None, None)]
                else:
                    wsrc, ncol = wB[idx], 384
                    tiles = [("q", 0, None, 20, None, 0), ("k", 128, None, 21, None, None),
                             ("v", 256, None, None, None, None)]
                for c in range(8):
                    S.dma("sp", lambda e, c=c: e.dma_start(out=wstage[:, 0:ncol], in_=wsrc[:, c, :]), "wstage",
                          w=["wstage"])
                    S.op("dve", lambda e, c=c: e.tensor_scalar(out=wbf[:, c, 0:ncol], in0=wstage[:, 0:ncol],
                                                               scalar1=gc[:, c:c + 1], scalar2=None, op0=ALU.mult),
                         r=["wstage", "gc"], w=["wbf"])
                    yield

                def load_x(t):
                    i = t % 2
                    S.dma("pool", lambda e: e.dma_start(out=xbf[i][:], in_=xT3[:, :, t * TT:(t + 1) * TT]),
                          ("xbf", i), w=[("xbf", i)])

                def mm8(c0, xb, XB):
                    for c in range(8):
                        S.op("pe", lambda e, c=c: e.matmul(ps[:, 7, :], lhsT=wbf[:, c, c0:c0 + 128], rhs=xb[:, c, :],
                                                           start=(c == 0), stop=(c == 7)),
                             r=["wbf", XB], w=[PS(7)], sig=(c == 7))

                load_x(0)
                for t in range(NT):
                    own = t < NQB
                    xb = xbf[t % 2]
                    XB = ("xbf", t % 2)
                    if t + 1 < NT:
                        load_x(t + 1)
                    tsl = slice(t * TT, (t + 1) * TT)
                    S.op("pool", lambda e, xb=xb: e.tensor_tensor(out=xsq[:], in0=xb[:], in1=xb[:], op=ALU.mult),
                         r=[XB], w=["xsq"])
                    yield
                    for c in range(8):
                        S.op("pe", lambda e, c=c: e.matmul(ps[:, 7, :], lhsT=ones1024[:], rhs=xsq[:, c, :],
                                                           start=(c == 0), stop=(c == 7)),
                             r=["ones1024", "xsq"], w=[PS(7)], sig=(c == 7))
                    yield
                    S.op("act", lambda e: e.activation(out=ptm[2][:], in_=ps[:, 7, :], func=AF.Ln, bias=epsc[:], scale=1.0),
                         r=["epsc"], w=[PS(7), P(2)])
                    S.op("act", lambda e: e.activation(out=ptm[0][:], in_=ptm[2][:], func=AF.Exp, bias=zeroc[:], scale=-0.5),
                         r=[P(2), "zeroc"], w=[P(0)])
                    yield
                    S.op("dve", lambda e: e.tensor_tensor(out=ptm[1][:], in0=ptm[0][:], in1=ptm[0][:], op=ALU.mult),
                         r=[P(0)], w=[P(1)])
                    if kind == "A":
                        S.dma("sp", lambda e, tsl=tsl: e.dma_start(out=rC[:], in_=ropeC[:, tsl]), "rC", w=["rC"])
                        S.dma("sp", lambda e, tsl=tsl: e.dma_start(out=rS[:], in_=ropeS[:, tsl]), "rS", w=["rS"])
                    yield
                    for (ty, c0, sc0, gi, sgi, qi) in tiles:
                        if ty == "q" and not own:
                            continue
                        mm8(c0, xb, XB)
                        yield
                        if ty == "v":
                            S.op("dve", lambda e: e.tensor_tensor(out=vtmp[:], in0=ps[:, 7, :], in1=ptm[0][:], op=ALU.mult),
                                 r=[P(0)], w=[PS(7), "vtmp"])
                            yield
                            for j in range(4):
                                S.op("pe", lambda e, j=j: e.transpose(out=ps7b[:, j * 128:(j + 1) * 128],
                                                                      in_=vtmp[:, j * 128:(j + 1) * 128],
                                                                      identity=identb[:]),
                                     r=["vtmp", "identb"], w=[PS(7)], sig=(j == 3))
                            yield
                            src = ps7b[:, 0:512].rearrange("p (j e) -> p j e", e=128)
                            if kind == "B":
                                S.op("dve", lambda e, t=t, src=src: e.tensor_copy(out=vaug[:, 4 * t:4 * t + 4, 0:128], in_=src),
                                     w=[PS(7), VA])
                            else:
                                S.op("dve", lambda e, t=t, src=src: e.tensor_copy(out=vaug[:, 4 * t:4 * t + 4, 0:64],
                                                                                  in_=src[:, :, 0:64]),
                                     w=[PS(7), VA])
                                S.op("dve", lambda e, t=t, src=src: e.tensor_copy(out=vaug[:, 4 * t:4 * t + 4, 128:192],
                                                                                  in_=src[:, :, 0:64]),
                                     w=[PS(7), VA])
                            yield
                            continue
                        S.op("dve", lambda e: e.tensor_copy(out=ptm[5][:], in_=ps[:, 7, :]), w=[PS(7), P(5)])
                        S.op("pool", lambda e: e.tensor_tensor(out=sqt_p[:], in0=ptm[5][:], in1=ptm[5][:], op=ALU.mult),
                             r=[P(5)], w=["sqt_p"])
                        yield
                        S.op("pe", lambda e: e.matmul(ps[:, 7, :], lhsT=blk64[:], rhs=sqt_p[:], start=True, stop=True),
                             r=["blk64", "sqt_p"], w=[PS(7)])
                        yield
                        S.op("dve", lambda e: e.tensor_tensor(out=ptm[3][:], in0=ps[:, 7, :], in1=ptm[1][:], op=ALU.mult),
                             r=[P(1)], w=[PS(7), P(3)])
                        yield
                        S.op("act", lambda e: e.activation(out=ptm[4][:], in_=ptm[3][:], func=AF.Ln, bias=epsc[:], scale=1.0),
                             r=[P(3), "epsc"], w=[P(4)])
                        S.op("act", lambda e: e.activation(out=ptm[3][:], in_=ptm[4][:], func=AF.Exp, bias=zeroc[:], scale=-0.5),
                             r=[P(4), "zeroc"], w=[P(3)])
                        yield
                        S.op("dve", lambda e: e.tensor_tensor(out=ptm[4][:], in0=ptm[3][:], in1=ptm[0][:], op=ALU.mult),
                             r=[P(3), P(0)], w=[P(4)])
                        if ty == "q":
                            dst = qT[:, qi, tsl]
                            dkey = QT
                        else:
                            dst = kT[:, tsl]
                            dkey = KT
                        if sc0 is None:
                            S.op("dve", lambda e, gi=gi, dst=dst: e.scalar_tensor_tensor(
                                out=dst, in0=ptm[5][:], scalar=gc[:, gi:gi + 1], in1=ptm[4][:], op0=ALU.mult, op1=ALU.mult),
                                r=["gc", P(4), P(5)], w=[dkey])
                            yield
                        else:
                            S.op("dve", lambda e, gi=gi: e.scalar_tensor_tensor(
                                out=ptm[6][:], in0=ptm[5][:], scalar=gc[:, gi:gi + 1], in1=rC[:], op0=ALU.mult, op1=ALU.mult),
                                r=["gc", "rC", P(5)], w=[P(6)])
                            yield
                            mm8(sc0, xb, XB)
                            yield
                            S.op("dve", lambda e, sgi=sgi: e.scalar_tensor_tensor(
                                out=ptm[7][:], in0=ps[:, 7, :], scalar=gc[:, sgi:sgi + 1], in1=rS[:], op0=ALU.mult, op1=ALU.mult),
                                r=["gc", "rS"], w=[PS(7), P(7)])
                            yield
                            S.op("pool", lambda e: e.tensor_tensor(out=ptm[6][:], in0=ptm[6][:], in1=ptm[7][:], op=ALU.add),
                                 r=[P(7)], w=[P(6)])
                            S.op("pool", lambda e, dst=dst: e.tensor_tensor(out=dst, in0=ptm[6][:], in1=ptm[4][:], op=ALU.mult),
                                 r=[P(6), P(4)], w=[dkey])
                            yield

            epi_count = [0]

            def attention(kind, idx, ub, bg, every):
                kT, vaug, qT = kTs[ub], vaugs[ub], qTs[ub]
                KT, VA, QT = ("kT", ub), ("vaug", ub), ("qT", ub)
                if kind == "A":
                    passes = [(0, 2 * idx), (1, 2 * idx + 1)]
                else:
                    passes = [(0, 4 + idx)]
                    S.dma("sp", lambda e: e.dma_start(out=bt[:], in_=btoep[:, idx, :]), "bt", w=["bt"])
                    S.dma("sp", lambda e: e.dma_start(out=bx[:], in_=bcross[:, idx, :, :]), "bx", w=["bx"])
                h = idx
                its = []
                for (qi, chunk) in passes:
                    for qb in range(NQB):
                        for kt in range(NKT):
                            its.append((qi, chunk, qb, kt))

                def emit_qk(n):
                    qi, chunk, qb, kt = its[n]
                    b0 = 2 * (n % 2)
                    ksl = slice(kt * 128, (kt + 1) * 128)
                    qsl = slice(qb * TT, (qb + 1) * TT)
                    S.op("pe", lambda e: e.matmul(ps[:, b0, :], lhsT=kT[0:64, ksl], rhs=qT[0:64, qi, qsl], start=True,
                                                  stop=True, tile_position=(0, 0)),
                         r=[KT, QT], w=[PS(b0)], sig=False)
                    S.op("pe", lambda e: e.matmul(ps[:, b0 + 1, :], lhsT=kT[64:128, ksl], rhs=qT[64:128, qi, qsl],
                                                  start=True, stop=True, tile_position=(64, 0)),
                         r=[KT, QT], w=[PS(b0 + 1)])

                emit_qk(0)
                for n in range(len(its)):
                    qi, chunk, qb, kt = its[n]
                    b0 = 2 * (n % 2)
                    pt = pT[n % 2]
                    PT = ("pT", n % 2)
                    if n + 1 < len(its):
                        emit_qk(n + 1)
                    scale = SCALE
                    bias_ap = zeroc[:]
                    bias_key = "zeroc"
                    if kind == "B":
                        info = bias_info(qb, kt)
                        if info[0] == "far":
                            ci = 3 * h + info[1]
                            bias_ap = cfar_t[:, ci:ci + 1]
                            bias_key = "cfar_t"
                        else:
                            scale = 1.0
                            if info[0] == "toep":
                                s0 = info[1]
                                bsrc = bt[:, s0:s0 + 512]
                                bkey = "bt"
                            else:
                                bsrc = bx[:, info[1], :]
                                bkey = "bx"
                            for bb in (b0, b0 + 1):
                                S.op("dve", lambda e, bb=bb, bsrc=bsrc: e.scalar_tensor_tensor(
                                    out=ps[:, bb, :], in0=ps[:, bb, :], scalar=SCALE, in1=bsrc, op0=ALU.mult, op1=ALU.add),
                                    r=[bkey], w=[PS(bb)])
                    S.op("act", lambda e, pt=pt, b0=b0, bias_ap=bias_ap, scale=scale: e.activation(
                        out=pt[:], in_=ps[:, b0:b0 + 2, :], func=AF.Exp, bias=bias_ap, scale=scale),
                        r=[bias_key], w=[PS(b0), PS(b0 + 1), PT])
                    st, sp_ = (kt == 0), (kt == NKT - 1)
                    if kind == "B":
                        for mp, bank in ((0, 4), (1, 5)):
                            for half in (0, 1):
                                lo = 64 * half
                                S.op("pe", lambda e, pt=pt, kt=kt, st=st, sp_=sp_, mp=mp, bank=bank, lo=lo: e.matmul(
                                    ps[lo:lo + 64, bank, :], lhsT=vaug[:, kt, lo:lo + 64], rhs=pt[:, mp, :], start=st,
                                    stop=sp_, tile_position=(0, lo)), r=[VA, PT], w=[PS(bank)], sig=False)
                        S.op("pe", lambda e, pt=pt, st=st, sp_=sp_: e.matmul(
                            ps[0:64, 6, :], lhsT=onesb[:, 0:64], rhs=pt[:, 0, :], start=st, stop=sp_, tile_position=(0, 0)),
                            r=["onesb", PT], w=[PS(6)], sig=False)
                        S.op("pe", lambda e, pt=pt, st=st, sp_=sp_: e.matmul(
                            ps[64:128, 6, :], lhsT=onesb[:, 0:64], rhs=pt[:, 1, :], start=st, stop=sp_, tile_position=(0, 64)),
                            r=["onesb", PT], w=[PS(6)])
                    else:
                        S.op("pe", lambda e, pt=pt, kt=kt, st=st, sp_=sp_: e.matmul(
                            ps[:, 4, :], lhsT=vaug[:, kt, 0:128], rhs=pt[:, 0, :], start=st, stop=sp_),
                            r=[VA, PT], w=[PS(4)], sig=False)
                        S.op("pe", lambda e, pt=pt, kt=kt, st=st, sp_=sp_: e.matmul(
                            ps[:, 5, :], lhsT=vaug[:, kt, 64:192], rhs=pt[:, 1, :], start=st, stop=sp_),
                            r=[VA, PT], w=[PS(5)])
                    if sp_:
                        qsl = slice(qb * TT, (qb + 1) * TT)
                        mi = epi_count[0] % 2
                        epi_count[0] += 1
                        ms = mst[mi]
                        MS = ("mst", mi)
                        if kind == "B":
                            S.op("dve", lambda e: e.reciprocal(out=et[0][:], in_=ps[:, 6, :]), w=[PS(6), E(0)])
                            S.op("pe", lambda e: e.matmul(ps[:, 6, :], lhsT=selA[:], rhs=et[0][:], start=True, stop=True),
                                 r=["selA", E(0)], w=[PS(6)])
                            S.op("dve", lambda e: e.tensor_copy(out=et[1][:], in_=ps[:, 6, :]), w=[PS(6), E(1)])
                            S.op("pe", lambda e: e.matmul(ps[:, 6, :], lhsT=selB[:], rhs=et[0][:], start=True, stop=True),
                                 r=["selB", E(0)], w=[PS(6)])
                            S.op("dve", lambda e: e.tensor_tensor(out=et[2][:], in0=ps[:, 4, :], in1=et[1][:], op=ALU.mult),
                                 r=[E(1)], w=[PS(4), E(2)])
                            S.op("dve", lambda e: e.tensor_copy(out=et[1][:], in_=ps[:, 6, :]), w=[PS(6), E(1)])
                            S.op("dve", lambda e: e.tensor_tensor(out=et[3][:], in0=ps[:, 5, :], in1=et[1][:], op=ALU.mult),
                                 r=[E(1)], w=[PS(5), E(3)])
                            S.op("dve", lambda e: e.scalar_tensor_tensor(out=et[4][:], in0=et[3][:], scalar=neglam[:, 0:1],
                                                                         in1=et[2][:], op0=ALU.mult, op1=ALU.add),
                                 r=[E(3), E(2), "neglam"], w=[E(4)])
                            S.op("pool", lambda e: e.tensor_tensor(out=sqt_e[:], in0=et[4][:], in1=et[4][:], op=ALU.mult),
                                 r=[E(4)], w=["sqt_e"])
                            S.op("pe", lambda e: e.matmul(ps[:, 6, :], lhsT=ones128[:], rhs=sqt_e[:], start=True, stop=True),
                                 r=["ones128", "sqt_e"], w=[PS(6)])
                            S.op("act", lambda e: e.activation(out=et[5][:], in_=ps[:, 6, :], func=AF.Ln, bias=epsc[:],
                                                               scale=1.0), r=["epsc"], w=[PS(6), E(5)])
                            S.op("act", lambda e: e.activation(out=et[0][:], in_=et[5][:], func=AF.Exp, bias=zeroc[:],
                                                               scale=-0.5), r=[E(5), "zeroc"], w=[E(0)])
                            S.op("dve", lambda e, ms=ms: e.scalar_tensor_tensor(
                                out=ms[:], in0=et[4][:], scalar=gsub[:, 0:1], in1=et[0][:], op0=ALU.mult, op1=ALU.mult),
                                r=[E(4), E(0), "gsub"], w=[MS])
                        else:
                            for s in (0, 1):
                                lo, hi = (0, 64) if s == 0 else (64, 128)
                                S.op("dve", lambda e, s=s: e.tensor_copy(out=et[s][:], in_=ps[:, 4 + s, :]),
                                     w=[PS(4 + s), E(s)])
                                S.op("pe", lambda e, s=s: e.matmul(ps[:, 6, :], lhsT=swapf[:], rhs=et[s][:], start=True,
                                                                   stop=True), r=["swapf", E(s)], w=[PS(6)])
                                S.op("dve", lambda e, s=s, lo=lo, hi=hi: e.reciprocal(out=et[2 + s][lo:hi, :],
                                                                                      in_=ps[lo:hi, 6, :]),
                                     w=[PS(6), E(2 + s)])
                                S.op("dve", lambda e, s=s, lo=lo, hi=hi, ms=ms: e.tensor_tensor(
                                    out=ms[lo:hi, :], in0=et[s][lo:hi, :], in1=et[2 + s][lo:hi, :], op=ALU.mult),
                                    r=[E(s), E(2 + s)], w=[MS])
                        S.dma("sp", lambda e, ms=ms, chunk=chunk, qsl=qsl: e.dma_start(out=mix_s[chunk][:, qsl], in_=ms[:]),
                              ("mso", mi), r=[MS], w=[("mix", chunk, qb)])
                    if bg is not None and (n % every) == 0:
                        next(bg, None)
                if bg is not None:
                    for _ in bg:
                        pass

            units = [("A", 0, 0), ("B", 0, 1), ("A", 1, 0), ("B", 1, 1), ("B", 2, 0), ("B", 3, 1)]
            est = {"A": 440, "B": 300}
            for _ in proj_gen(*units[0]):
                pass
            for ui, (kind, idx, ub) in enumerate(units):
                nits = 1024 if kind == "A" else 512
                if ui + 1 < len(units):
                    nk = units[ui + 1][0]
                    bg = proj_gen(*units[ui + 1])
                    every = max(1, nits // est[nk])
                else:
                    bg, every = None, 1
                attention(kind, idx, ub, bg, every)
            S.barrier()
            S.replay()

        with contextlib.ExitStack() as fst:
            woutb = sb(fst, "woutb", [128, 8, 1024], BF16)
            xts = [sb(fst, "xt%d" % i, [128, 8, 512], F32) for i in range(2)]
            mxs = [sb(fst, "mx%d" % i, [128, 8, 512], BF16) for i in range(2)]
            h2 = sb(fst, "h2", [128, 8, 512], BF16)
            uT = sb(fst, "uT", [128, 32, 512], BF16)
            wsb = [sb(fst, "wsb%d" % i, [128, 4096], BF16) for i in range(4)]
            lnv = sb(fst, "lnv", [128, 512], F32)
            rstd = sb(fst, "rstd", [128, 512], F32)
            rl = [sb(fst, "rl%d" % i, [128, 512], F32) for i in range(2)]

            S.dma("pool", lambda e: e.dma_start(out=woutb[:], in_=wout), "woutb", w=["woutb"])

            loads = []
            for i in range(NQB):
                for g in range(8):
                    loads.append(("u", g))
                for m in range(8):
                    loads.append(("d", m))
            issued = [0]

            def issue_loads(upto):
                while issued[0] < min(upto, len(loads)):
                    k = issued[0]
                    ty, j = loads[k]
                    src = wup_s[j] if ty == "u" else wdn_s[j]
                    bi = k % 4
                    S.dma("sp", lambda e, src=src, bi=bi: e.dma_start(out=wsb[bi][:], in_=src), ("wsb", bi),
                          w=[("wsb", bi)])
                    issued[0] += 1

            def load_xt(i):
                S.dma("sp", lambda e: e.dma_start(out=xts[i % 2][:], in_=xT3[:, :, i * TT:(i + 1) * TT]), ("xt", i % 2),
                      w=[("xt", i % 2)])
                S.dma("sp", lambda e: e.dma_start(out=mxs[i % 2][:], in_=mix3[:, :, i * TT:(i + 1) * TT]), ("mx", i % 2),
                      w=[("mx", i % 2)])

            bankc = [0]

            def nb():
                b = bankc[0] % 8
                bankc[0] += 1
                return b

            load_xt(0)
            lk = 0
            for i in range(NQB):
                xt = xts[i % 2]
                XT = ("xt", i % 2)
                mx = mxs[i % 2]
                MX = ("mx", i % 2)
                tsl = slice(i * TT, (i + 1) * TT)
                issue_loads(lk + 2)
                for m in range(8):
                    b = nb()
                    for c in range(8):
                        S.op("pe", lambda e, b=b, c=c, m=m, mx=mx: e.matmul(
                            ps[:, b, :], lhsT=woutb[:, c, m * 128:(m + 1) * 128], rhs=mx[:, c, :], start=(c == 0),
                            stop=(c == 7)), r=["woutb", MX], w=[PS(b)], sig=(c == 7))
                    S.op("dve", lambda e, b=b, m=m, xt=xt: e.tensor_tensor(out=xt[:, m, :], in0=ps[:, b, :], in1=xt[:, m, :],
                                                                            op=ALU.add), w=[PS(b), XT])
                if i + 1 < NQB:
                    load_xt(i + 1)
                S.op("act", lambda e, xt=xt: e.activation(out=h2[:], in_=xt[:], func=AF.Square), r=[XT], w=["h2"])
                b = nb()
                for c in range(8):
                    S.op("pe", lambda e, b=b, c=c: e.matmul(ps[:, b, :], lhsT=ones1024[:], rhs=h2[:, c, :], start=(c == 0),
                                                            stop=(c == 7)), r=["ones1024", "h2"], w=[PS(b)], sig=(c == 7))
                S.op("act", lambda e, b=b: e.activation(out=lnv[:], in_=ps[:, b, :], func=AF.Ln, bias=epsc[:], scale=1.0),
                     r=["epsc"], w=[PS(b), "lnv"])
                S.op("act", lambda e: e.activation(out=rstd[:], in_=lnv[:], func=AF.Exp, bias=zeroc[:], scale=-0.5),
                     r=["lnv", "zeroc"], w=["rstd"])
                for c in range(8):
                    S.op("dve", lambda e, c=c, xt=xt: e.tensor_tensor(out=h2[:, c, :], in0=xt[:, c, :], in1=rstd[:],
                                                                      op=ALU.mult), r=[XT, "rstd"], w=["h2"])
                for g in range(8):
                    bi = lk % 4
                    issue_loads(lk + 3)
                    for jj in range(4):
                        j = 4 * g + jj
                        b = nb()
                        for c in range(8):
                            S.op("pe", lambda e, b=b, c=c, jj=jj, bi=bi: e.matmul(
                                ps[:, b, :], lhsT=wsb[bi][:, c * 512 + jj * 128:c * 512 + (jj + 1) * 128], rhs=h2[:, c, :],
                                start=(c == 0), stop=(c == 7)), r=[("wsb", bi), "h2"], w=[PS(b)], sig=(c == 7))
                        ri = j % 2
                        S.op("act", lambda e, b=b, ri=ri: e.activation(out=rl[ri][:], in_=ps[:, b, :], func=AF.Relu),
                             w=[PS(b), ("rl", ri)])
                        S.op("pool", lambda e, j=j, ri=ri: e.tensor_tensor(out=uT[:, j, :], in0=rl[ri][:], in1=rl[ri][:],
                                                                           op=ALU.mult), r=[("rl", ri)], w=["uT"])
                    lk += 1
                for m in range(8):
                    bi = lk % 4
                    issue_loads(lk + 3)
                    b = nb()
                    for j in range(32):
                        S.op("pe", lambda e, b=b, j=j, bi=bi: e.matmul(
                            ps[:, b, :], lhsT=wsb[bi][:, j * 128:(j + 1) * 128], rhs=uT[:, j, :], start=(j == 0),
                            stop=(j == 31)), r=[("wsb", bi), "uT"], w=[PS(b)], sig=(j == 31))
                    S.op("dve", lambda e, b=b, m=m, xt=xt: e.tensor_tensor(out=xt[:, m, :], in0=ps[:, b, :], in1=xt[:, m, :],
                                                                            op=ALU.add), w=[PS(b), XT])
                    lk += 1
                S.dma("sp", lambda e, xt=xt, tsl=tsl: e.dma_start(out=outT3[:, :, tsl], in_=xt[:]), ("out", i % 2),
                      r=[XT], w=[("outT", i)])
            S.wait_slots("sp", [("out", 0), ("out", 1)])
            S.barrier()
            S.replay()
    return nc


def _host_tables():
    import jax
    import jax.numpy as jnp
    cpu = jax.devices("cpu")[0]
    with jax.default_device(cpu):
        rows = SEQ // 64
        row = jnp.broadcast_to(jnp.arange(rows)[:, None], (rows, 64)).reshape(-1).astype(jnp.float32)
        col = jnp.broadcast_to(jnp.arange(64)[None, :], (rows, 64)).reshape(-1).astype(jnp.float32)
        half = HD // 2
        inv_freq = 1.0 / (10000.0 ** (jnp.arange(0, half, 2, dtype=jnp.float32) / half))
        ang_r = row[:, None] * inv_freq[None, :]
        ang_c = col[:, None] * inv_freq[None, :]
        cr, sr, cc, sc = (np.asarray(t.astype(jnp.float32)) for t in
                          (jnp.cos(ang_r), jnp.sin(ang_r), jnp.cos(ang_c), jnp.sin(ang_c)))

        def t5_bucket(rel):
            nb = 16
            max_exact = 8
            ret = (rel > 0).astype(jnp.int32) * nb
            n = jnp.abs(rel)
            nf = jnp.maximum(n, 1).astype(jnp.float32)
            large = max_exact + (jnp.log(nf / max_exact) / math.log(128 / max_exact) * (nb - max_exact)).astype(jnp.int32)
            large = jnp.minimum(large, nb - 1)
            return ret + jnp.where(n < max_exact, n, large)

        rel = jnp.arange(-SEQ, SEQ + 1, dtype=jnp.int32)
        bucket = np.asarray(t5_bucket(rel))
    C = np.empty((64, SEQ), np.float32)
    Sg = np.empty((64, SEQ), np.float32)
    for d in range(64):
        j = d % 16
        first = (d % 32) < 16
        if d < 32:
            c_, s_ = cr[:, j], sr[:, j]
        else:
            c_, s_ = cc[:, j], sc[:, j]
        C[d] = c_
        Sg[d] = -s_ if first else s_
    return C, Sg, bucket


def _partner(d):
    return d + 16 if (d % 32) < 16 else d - 16


def _prep(inputs):
    f = lambda k: np.asarray(inputs[k], dtype=np.float32)
    x = f("x")
    w_in = f("w_in")[0]
    C, Sg, bucket = _host_tables()
    rel_bias = f("rel_bias")
    dd = np.arange(64)
    pd = np.array([_partner(d) for d in range(64)])

    def chunked(w):
        return np.ascontiguousarray(w.reshape(8, 128, -1).transpose(1, 0, 2))

    wA = np.empty((2, 128, 8, 896), np.float32)
    for kv in range(2):
        cols = []
        for pair in range(2):
            hs = [4 * kv + 2 * pair, 4 * kv + 2 * pair + 1]
            cols.append(np.concatenate([h * 64 + dd for h in hs]))
            cols.append(np.concatenate([h * 64 + pd for h in hs]))
        kc = 512 + kv * 64
        cols.append(np.concatenate([kc + dd, kc + dd]))
        cols.append(np.concatenate([kc + pd, kc + pd]))
        vc = 640 + kv * 64
        cols.append(np.concatenate([vc + dd, vc + dd]))
        wA[kv] = chunked(w_in[:, np.concatenate(cols)])
    wB = np.empty((4, 128, 8, 384), np.float32)
    for h in range(4):
        cols = np.concatenate([768 + h * 128 + np.arange(128), 1280 + h * 128 + np.arange(128),
                               1792 + h * 128 + np.arange(128)])
        wB[h] = chunked(w_in[:, cols])

    gcol = np.zeros((128, 24), np.float32)
    gcol[:, 0:8] = f("attn_norm_g")[0].reshape(8, 128).T
    gcol[:, 8:16] = f("mlp_norm_g")[0].reshape(8, 128).T
    p64 = np.arange(128) % 64
    aq, ak, bq, bk = f("a_q_norm_g")[0], f("a_k_norm_g")[0], f("b_q_norm_g")[0], f("b_k_norm_g")[0]
    gcol[:, 16] = aq[p64]
    gcol[:, 17] = aq[pd[p64]]
    gcol[:, 18] = ak[p64]
    gcol[:, 19] = ak[pd[p64]]
    gcol[:, 20] = bq[p64]
    gcol[:, 21] = bk[p64]
    gcol[:, 22] = f("b_subln_g")[0]

    lamv = np.empty((128, 256), np.float32)
    lamv[:, 0:64] = f("lambda_q1")[0][None, :]
    lamv[:, 64:128] = f("lambda_k1")[0][None, :]
    lamv[:, 128:192] = f("lambda_q2")[0][None, :]
    lamv[:, 192:256] = f("lambda_k2")[0][None, :]

    cmat = np.zeros((128, 4, 128), np.float32)
    cmat[:, 0, :] = np.eye(128, dtype=np.float32)
    for m in range(128):
        cmat[(m + 64) % 128, 1, m] = 1.0
    cmat[0:64, 2, :] = 1.0 / 64.0
    cmat[64:128, 3, :] = 1.0 / 64.0

    wout = chunked(f("w_out")[0])
    w_up = f("w_up")[0]
    wup = np.ascontiguousarray(w_up.reshape(8, 128, 8, 512).transpose(2, 1, 0, 3))
    w_down = f("w_down")[0]
    wdn = np.ascontiguousarray(w_down.reshape(32, 128, 8, 128).transpose(2, 1, 0, 3))

    ii = np.arange(128)[:, None]
    uu = np.arange(1152)[None, :]
    btoep = np.ascontiguousarray(rel_bias[bucket[(ii - uu + 512) + SEQ]].transpose(0, 2, 1))

    common = dict(wA=wA, wB=wB, gcol=gcol, lamv=lamv, btoep=btoep, cmat=cmat, wout=wout, wup=wup, wdn=wdn)
    in_maps = []
    for core in range(8):
        b, qh = core // 2, core % 2
        order = np.concatenate([np.arange(qh * NQ, (qh + 1) * NQ), np.arange((1 - qh) * NQ, (2 - qh) * NQ)])
        xT = np.ascontiguousarray(x[b].T[:, order])
        ropeC = np.ascontiguousarray(np.concatenate([C, C], axis=0)[:, order])
        ropeS = np.ascontiguousarray(np.concatenate([Sg, Sg], axis=0)[:, order])
        pos = order
        jj = np.arange(512)[None, :]
        bcross = np.empty((128, 4, 2, 512), np.float32)
        r0 = pos[4096 + np.arange(128)][:, None] - pos[3584 + np.arange(512)][None, :]
        r1 = pos[8064 + np.arange(128)][:, None] - pos[0 + np.arange(512)][None, :]
        bcross[:, :, 0, :] = rel_bias[bucket[r0 + SEQ]].transpose(0, 2, 1)
        bcross[:, :, 1, :] = rel_bias[bucket[r1 + SEQ]].transpose(0, 2, 1)
        cfar = np.empty((128, 12), np.float32)
        for h in range(4):
            cfar[:, 3 * h + 0] = rel_bias[15, h]
            cfar[:, 3 * h + 1] = rel_bias[31, h]
            cfar[:, 3 * h + 2] = rel_bias[31, h] if qh == 0 else rel_bias[15, h]
        m = dict(common)
        m.update(xT=xT, ropeC=ropeC, ropeS=ropeS, bcross=bcross, cfar=cfar)
        in_maps.append(m)
    return in_maps


_NC_CACHE = {}


def kernel(**inputs):
    in_maps = _prep(inputs)
    if "nc" not in _NC_CACHE:
        _NC_CACHE["nc"] = build_program()
    nc = _NC_CACHE["nc"]
    res = run_bass_kernel_spmd(nc, in_maps, core_ids=list(range(8)))
    out = np.empty((4, SEQ, D_MODEL), np.float32)
    for core in range(8):
        b, qh = core // 2, core % 2
        out[b, qh * NQ:(qh + 1) * NQ, :] = res.results[core]["outT"].T
    return out
```

```python
import contextlib
import itertools
import math

import numpy as np
import concourse.bass as bass
import concourse.mybir as mybir
from concourse.bass_utils import run_bass_kernel_spmd

F32 = mybir.dt.float32
BF16 = mybir.dt.bfloat16
ALU = mybir.AluOpType
AF = mybir.ActivationFunctionType
AX = mybir.AxisListType

D_MODEL = 1024
SEQ = 8192
NQ = 4096
HD = 64
EPS = 1e-6
SCALE = HD ** -0.5
TT = 512
NT = SEQ // TT
NQB = NQ // TT
NKT = SEQ // 128
LAMBDA_INIT = 0.8 - 0.6 * math.exp(-0.3 * 0)


class Sched:
    EPOCH = 16000

    def __init__(self, nc, stack):
        self.nc = nc
        self.stack = stack
        self.names = ["pe", "act", "dve", "pool", "sp"]
        self.streams = {e: [] for e in self.names}
        self.esem = {}
        self.ecnt = {e: 0 for e in self.names}
        self.semobj = []
        self.slots = {}
        self.last_w = {}
        self.readers = {}
        self.waited = {e: {} for e in self.names}

    def _newsem(self, name):
        h = self.stack.enter_context(self.nc.semaphore(name))
        self.semobj.append(h)
        return len(self.semobj) - 1

    def _deps(self, eng, r, w):
        need = {}
        wt = self.waited[eng]

        def add(sid, val, teng):
            if eng == "pe" and teng == "pe":
                return
            if wt.get(sid, 0) >= val:
                return
            if need.get(sid, 0) < val:
                need[sid] = val

        for k in r:
            t = self.last_w.get(k)
            if t is not None:
                add(*t)
        for k in w:
            t = self.last_w.get(k)
            if t is not None:
                add(*t)
            rd = self.readers.get(k)
            if rd:
                for sid, (val, teng) in rd.items():
                    add(sid, val, teng)
        for sid, val in need.items():
            wt[sid] = val
            self.streams[eng].append(("w", sid, val))

    def _record(self, tok, r, w):
        for k in r:
            d = self.readers.setdefault(k, {})
            if d.get(tok[0], (0, None))[0] < tok[1]:
                d[tok[0]] = (tok[1], tok[2])
        for k in w:
            self.last_w[k] = tok
            self.readers[k] = {}

    def op(self, eng, fn, r=(), w=(), sig=True):
        self._deps(eng, r, w)
        if eng not in self.esem:
            self.esem[eng] = self._newsem("e_%s_%d" % (eng, len(self.semobj)))
        if sig:
            self.ecnt[eng] += 1
            tok = (self.esem[eng], self.ecnt[eng], eng)
            self.streams[eng].append(("o", fn, tok[0]))
            if self.ecnt[eng] >= self.EPOCH:
                self.esem[eng] = self._newsem("e_%s_%d" % (eng, len(self.semobj)))
                self.ecnt[eng] = 0
        else:
            tok = (self.esem[eng], self.ecnt[eng] + 1, eng)
            self.streams[eng].append(("o", fn, None))
        self._record(tok, r, w)

    def dma(self, q, fn, slot, r=(), w=()):
        self._deps(q, r, w)
        if slot not in self.slots or self.slots[slot][1] >= 30000:
            self.slots[slot] = [self._newsem("d_%d" % len(self.semobj)), 0]
        s = self.slots[slot]
        s[1] += 16
        tok = (s[0], s[1], None)
        self.streams[q].append(("d", fn, s[0]))
        self._record(tok, r, w)

    def barrier(self):
        toks = []
        for e, sid in self.esem.items():
            if self.ecnt[e] > 0:
                toks.append((sid, self.ecnt[e], e))
        for s in self.slots.values():
            if s[1] > 0:
                toks.append((s[0], s[1], None))
        for e in self.names:
            wt = self.waited[e]
            for sid, val, teng in toks:
                if e == "pe" and teng == "pe":
                    continue
                if wt.get(sid, 0) >= val:
                    continue
                wt[sid] = val
                self.streams[e].append(("w", sid, val))
        self.last_w.clear()
        self.readers.clear()

    def wait_slots(self, eng, slots):
        for sl in slots:
            s = self.slots[sl]
            self.streams[eng].append(("w", s[0], s[1]))

    def replay(self):
        nc = self.nc
        semobj = self.semobj
        streams = self.streams

        def run(name, e):
            for it in streams[name]:
                if it[0] == "w":
                    e.wait_ge(semobj[it[1]], it[2])
                elif it[0] == "o":
                    ins = it[1](e)
                    if it[2] is not None:
                        ins.then_inc(semobj[it[2]], 1)
                else:
                    it[1](e).then_inc(semobj[it[2]], 16)

        with nc.Block() as block:
            @block.tensor
            def _(e):
                run("pe", e)

            @block.scalar
            def _(e):
                run("act", e)

            @block.vector
            def _(e):
                run("dve", e)

            @block.gpsimd
            def _(e):
                run("pool", e)

            @block.sync
            def _(e):
                run("sp", e)
        for k in self.names:
            self.streams[k] = []


def bias_info(qb, kt):
    if kt < 32:
        o = 128 * kt - 512 * qb
        if o <= -256:
            return ("far", 0)
        if o >= 640:
            return ("far", 1)
        return ("toep", 512 - o)
    if qb == 7 and kt == 32:
        return ("cross", 0)
    if qb == 0 and kt == 63:
        return ("cross", 1)
    return ("far", 2)


def build_program():
    nc = bass.Bass("TRN2", target_bir_lowering=False)

    def din(name, shape):
        return nc.dram_tensor(name, list(shape), F32, kind="ExternalInput").ap()

    xT = din("xT", [1024, SEQ])
    wA = din("wA", [2, 128, 8, 896])
    wB = din("wB", [4, 128, 8, 384])
    gcol = din("gcol", [128, 24])
    lamv = din("lamv", [128, 256])
    ropeC = din("ropeC", [128, SEQ])
    ropeS = din("ropeS", [128, SEQ])
    btoep = din("btoep", [128, 4, 1152])
    bcross = din("bcross", [128, 4, 2, 512])
    cfar = din("cfar", [128, 12])
    cmat = din("cmat", [128, 4, 128])
    wout = din("wout", [128, 8, 1024])
    wup = din("wup", [8, 128, 8, 512])
    wdn = din("wdn", [8, 128, 32, 128])
    outT = nc.dram_tensor("outT", [1024, NQ], F32, kind="ExternalOutput").ap()
    wup_s = nc.dram_tensor("wup_s", [8, 128, 4096], BF16).ap()
    wdn_s = nc.dram_tensor("wdn_s", [8, 128, 4096], BF16).ap()
    mix_s = nc.dram_tensor("mix_s", [8, 128, NQ], BF16).ap()

    xT3 = xT.rearrange("(c p) t -> p c t", p=128)
    outT3 = outT.rearrange("(c p) t -> p c t", p=128)
    mix3 = mix_s.rearrange("c p t -> p c t")

    with contextlib.ExitStack() as top:
        S = Sched(nc, top)

        def sb(stack, name, shape, dt):
            return stack.enter_context(nc.sbuf_tensor(name, list(shape), dt))

        ps = top.enter_context(nc.psum_tensor("ps", [128, 8, 512], F32))
        ps7b = ps[:, 7, :].bitcast(BF16)

        def PS(b):
            return ("ps", b)

        identb = sb(top, "identb", [128, 128], BF16)
        swapf = sb(top, "swapf", [128, 128], F32)
        selA = sb(top, "selA", [128, 128], F32)
        selB = sb(top, "selB", [128, 128], F32)
        ones1024 = sb(top, "ones1024", [128, 128], BF16)
        blk64 = sb(top, "blk64", [128, 128], BF16)
        ones128 = sb(top, "ones128", [128, 128], BF16)
        onesb = sb(top, "onesb", [128, 128], BF16)
        gc = sb(top, "gc", [128, 24], F32)
        cfar_t = sb(top, "cfar_t", [128, 12], F32)
        epsc = sb(top, "epsc", [128, 1], F32)
        zeroc = sb(top, "zeroc", [128, 1], F32)
        neglam = sb(top, "neglam", [128, 1], F32)
        gsub = sb(top, "gsub", [128, 1], F32)
        lamt = sb(top, "lamt", [128, 256], F32)
        lamp = sb(top, "lamp", [128, 128], F32)
        lams = sb(top, "lams", [128, 4], F32)

        S.dma("pool", lambda e: e.dma_start(out=identb[:], in_=cmat[:, 0, :]), "c0", w=["identb"])
        S.dma("sp", lambda e: e.dma_start(out=swapf[:], in_=cmat[:, 1, :]), "c1", w=["swapf"])
        S.dma("sp", lambda e: e.dma_start(out=selA[:], in_=cmat[:, 2, :]), "c5", w=["selA"])
        S.dma("sp", lambda e: e.dma_start(out=selB[:], in_=cmat[:, 3, :]), "c6", w=["selB"])
        S.dma("sp", lambda e: e.dma_start(out=gc[:], in_=gcol), "c2", w=["gc"])
        S.dma("sp", lambda e: e.dma_start(out=cfar_t[:], in_=cfar), "c3", w=["cfar_t"])
        S.dma("sp", lambda e: e.dma_start(out=lamt[:], in_=lamv), "c4", w=["lamt"])
        S.op("pool", lambda e: e.memset(ones1024[:], 1.0 / 1024.0), w=["ones1024"])
        S.op("pool", lambda e: e.memset(ones128[:], 1.0 / 128.0), w=["ones128"])
        S.op("pool", lambda e: e.memset(onesb[:], 1.0), w=["onesb"])
        S.op("pool", lambda e: e.memset(blk64[:], 0.0), w=["blk64"])
        S.op("pool", lambda e: e.memset(blk64[0:64, 0:64], 1.0 / 64.0), w=["blk64"])
        S.op("pool", lambda e: e.memset(blk64[64:128, 64:128], 1.0 / 64.0), w=["blk64"])
        S.op("pool", lambda e: e.memset(epsc[:], EPS), w=["epsc"])
        S.op("pool", lambda e: e.memset(zeroc[:], 0.0), w=["zeroc"])

        S.op("dve", lambda e: e.tensor_tensor(out=lamp[:, 0:64], in0=lamt[:, 0:64], in1=lamt[:, 64:128], op=ALU.mult),
             r=["lamt"], w=["lamp"])
        S.op("dve", lambda e: e.tensor_tensor(out=lamp[:, 64:128], in0=lamt[:, 128:192], in1=lamt[:, 192:256], op=ALU.mult),
             r=["lamt"], w=["lamp"])
        S.op("dve", lambda e: e.tensor_reduce(out=lams[:, 0:1], in_=lamp[:, 0:64], axis=AX.X, op=ALU.add),
             r=["lamp"], w=["lams"])
        S.op("dve", lambda e: e.tensor_reduce(out=lams[:, 1:2], in_=lamp[:, 64:128], axis=AX.X, op=ALU.add),
             r=["lamp"], w=["lams"])
        S.op("act", lambda e: e.activation(out=lams[:, 2:4], in_=lams[:, 0:2], func=AF.Exp, bias=zeroc[:], scale=1.0),
             r=["lams", "zeroc"], w=["lams2"])
        S.op("dve", lambda e: e.tensor_tensor(out=lams[:, 0:1], in0=lams[:, 3:4], in1=lams[:, 2:3], op=ALU.subtract),
             r=["lams2"], w=["lams"])
        S.op("dve", lambda e: e.tensor_scalar(out=neglam[:], in0=lams[:, 0:1], scalar1=-LAMBDA_INIT, scalar2=None, op0=ALU.add),
             r=["lams"], w=["neglam"])
        S.op("dve", lambda e: e.tensor_scalar(out=gsub[:], in0=gc[:, 22:23], scalar1=1.0 - LAMBDA_INIT, scalar2=None, op0=ALU.mult),
             r=["gc"], w=["gsub"])

        with contextlib.ExitStack() as ast:
            kTs = [sb(ast, "kT%d" % i, [128, SEQ], BF16) for i in range(2)]
            vaugs = [sb(ast, "vaug0", [128, NKT, 192], BF16), sb(ast, "vaug1", [128, NKT, 128], BF16)]
            qTs = [sb(ast, "qT0", [128, 2, NQ], BF16), sb(ast, "qT1", [128, 1, NQ], BF16)]
            pT = [sb(ast, "pT%d" % i, [128, 2, 512], BF16) for i in range(3)]
            wbf = sb(ast, "wbf", [128, 8, 896], BF16)
            wstage = sb(ast, "wstage", [128, 896], F32)
            xbf = [sb(ast, "xbf%d" % i, [128, 8, 512], BF16) for i in range(2)]
            xsq = sb(ast, "xsq", [128, 8, 512], BF16)
            rC = sb(ast, "rC", [128, 512], F32)
            rS = sb(ast, "rS", [128, 512], F32)
            ptm = [sb(ast, "ptm%d" % i, [128, 512], F32) for i in range(8)]
            et = [sb(ast, "et%d" % i, [128, 512], F32) for i in range(6)]
            sqt_p = sb(ast, "sqt_p", [128, 512], BF16)
            sqt_e = sb(ast, "sqt_e", [128, 512], BF16)
            vtmp = sb(ast, "vtmp", [128, 512], BF16)
            bt = sb(ast, "bt", [128, 1152], F32)
            bx = sb(ast, "bx", [128, 2, 512], F32)
            mst = [sb(ast, "mst%d" % i, [128, 512], BF16) for i in range(2)]
            pst_f = [sb(ast, "pst_f%d" % i, [128, 512], F32) for i in range(2)]
            pst_b = [sb(ast, "pst_b%d" % i, [128, 512], BF16) for i in range(2)]
            pst_d = sb(ast, "pst_d", [128, 2048], BF16)

            def prologue_gen():
                for g in range(8):
                    for c in range(8):
                        i = c % 2
                        S.dma("sp", lambda e, g=g, c=c, i=i: e.dma_start(out=pst_f[i][:], in_=wup[g][:, c, :]),
                              ("pwst", i), w=[("pwst", i)])
                        S.op("dve", lambda e, c=c, i=i: e.tensor_scalar(
                            out=pst_b[i][:], in0=pst_f[i][:], scalar1=gc[:, 8 + c:9 + c], scalar2=None, op0=ALU.mult),
                            r=[("pwst", i), "gc"], w=[("pwbo", i)])
                        S.dma("sp", lambda e, g=g, c=c, i=i: e.dma_start(out=wup_s[g][:, c * 512:(c + 1) * 512],
                                                                         in_=pst_b[i][:]),
                              ("pwo", i), r=[("pwbo", i)], w=[("wup_s", g, c)])
                        yield
                for m in range(8):
                    for hh in range(2):
                        src = wdn[m].rearrange("p j e -> p (j e)")[:, hh * 2048:(hh + 1) * 2048]
                        S.dma("pool", lambda e, src=src: e.dma_start(out=pst_d[:], in_=src), "pwb2", w=["pwb2"])
                        S.dma("sp", lambda e, m=m, hh=hh: e.dma_start(out=wdn_s[m][:, hh * 2048:(hh + 1) * 2048],
                                                                      in_=pst_d[:]),
                              "pwo2", r=["pwb2"], w=[("wdn_s", m, hh)])
                        yield

            def P(i):
                return ("ptm", i)

            def E(i):
                return ("et", i)

            S.op("pool", lambda e: e.memset(vaugs[0][:, :, 64:128], 1.0), w=[("vaug", 0)])

            def proj_gen(kind, idx, ub):
                kT, vaug, qT = kTs[ub], vaugs[ub], qTs[ub]
                KT, VA, QT = ("kT", ub), ("vaug", ub), ("qT", ub)
                if kind == "A":
                    wsrc, ncol = wA[idx], 896
                    tiles = [("q", 0, 128, 16, 17, 0), ("q", 256, 384, 16, 17, 1),
                             ("k", 512, 640, 18, 19, None), ("v", 768, None, None, None, None)]
                else:
                    wsrc, ncol = wB[idx], 384
                    tiles = [("q", 0, None, 20, None, 0), ("k", 128, None, 21, None, None),
                             ("v", 256, None, None, None, None)]
                for c in range(8):
                    S.dma("sp", lambda e, c=c: e.dma_start(out=wstage[:, 0:ncol], in_=wsrc[:, c, :]), "wstage",
                          w=["wstage"])
                    S.op("dve", lambda e, c=c: e.tensor_scalar(out=wbf[:, c, 0:ncol], in0=wstage[:, 0:ncol],
                                                               scalar1=gc[:, c:c + 1], scalar2=None, op0=ALU.mult),
                         r=["wstage", "gc"], w=["wbf"])
                    yield

                def load_x(t):
                    i = t % 2
                    S.dma("pool", lambda e: e.dma_start(out=xbf[i][:], in_=xT3[:, :, t * TT:(t + 1) * TT]),
                          ("xbf", i), w=[("xbf", i)])

                def mm8(c0, xb, XB):
                    for c in range(8):
                        S.op("pe", lambda e, c=c: e.matmul(ps[:, 7, :], lhsT=wbf[:, c, c0:c0 + 128], rhs=xb[:, c, :],
                                                           start=(c == 0), stop=(c == 7)),
                             r=["wbf", XB], w=[PS(7)], sig=(c == 7))

                load_x(0)
                for t in range(NT):
                    own = t < NQB
                    xb = xbf[t % 2]
                    XB = ("xbf", t % 2)
                    if t + 1 < NT:
                        load_x(t + 1)
                    tsl = slice(t * TT, (t + 1) * TT)
                    S.op("pool", lambda e, xb=xb: e.tensor_tensor(out=xsq[:], in0=xb[:], in1=xb[:], op=ALU.mult),
                         r=[XB], w=["xsq"])
                    yield
                    for c in range(8):
                        S.op("pe", lambda e, c=c: e.matmul(ps[:, 7, :], lhsT=ones1024[:], rhs=xsq[:, c, :],
                                                           start=(c == 0), stop=(c == 7)),
                             r=["ones1024", "xsq"], w=[PS(7)], sig=(c == 7))
                    yield
                    S.op("act", lambda e: e.activation(out=ptm[2][:], in_=ps[:, 7, :], func=AF.Ln, bias=epsc[:], scale=1.0),
                         r=["epsc"], w=[PS(7), P(2)])
                    S.op("act", lambda e: e.activation(out=ptm[0][:], in_=ptm[2][:], func=AF.Exp, bias=zeroc[:], scale=-0.5),
                         r=[P(2), "zeroc"], w=[P(0)])
                    yield
                    S.op("dve", lambda e: e.tensor_tensor(out=ptm[1][:], in0=ptm[0][:], in1=ptm[0][:], op=ALU.mult),
                         r=[P(0)], w=[P(1)])
                    if kind == "A":
                        S.dma("sp", lambda e, tsl=tsl: e.dma_start(out=rC[:], in_=ropeC[:, tsl]), "rC", w=["rC"])
                        S.dma("sp", lambda e, tsl=tsl: e.dma_start(out=rS[:], in_=ropeS[:, tsl]), "rS", w=["rS"])
                    yield
                    for (ty, c0, sc0, gi, sgi, qi) in tiles:
                        if ty == "q" and not own:
                            continue
                        mm8(c0, xb, XB)
                        yield
                        if ty == "v":
                            S.op("dve", lambda e: e.tensor_tensor(out=vtmp[:], in0=ps[:, 7, :], in1=ptm[0][:], op=ALU.mult),
                                 r=[P(0)], w=[PS(7), "vtmp"])
                            yield
                            for j in range(4):
                                S.op("pe", lambda e, j=j: e.transpose(out=ps7b[:, j * 128:(j + 1) * 128],
                                                                      in_=vtmp[:, j * 128:(j + 1) * 128],
                                                                      identity=identb[:]),
                                     r=["vtmp", "identb"], w=[PS(7)], sig=(j == 3))
                            yield
                            src = ps7b[:, 0:512].rearrange("p (j e) -> p j e", e=128)
                            if kind == "B":
                                S.op("dve", lambda e, t=t, src=src: e.tensor_copy(out=vaug[:, 4 * t:4 * t + 4, 0:128], in_=src),
                                     w=[PS(7), VA])
                            else:
                                S.op("dve", lambda e, t=t, src=src: e.tensor_copy(out=vaug[:, 4 * t:4 * t + 4, 0:64],
                                                                                  in_=src[:, :, 0:64]),
                                     w=[PS(7), VA])
                                S.op("dve", lambda e, t=t, src=src: e.tensor_copy(out=vaug[:, 4 * t:4 * t + 4, 128:192],
                                                                                  in_=src[:, :, 0:64]),
                                     w=[PS(7), VA])
                            yield
                            continue
                        S.op("dve", lambda e: e.tensor_copy(out=ptm[5][:], in_=ps[:, 7, :]), w=[PS(7), P(5)])
                        S.op("pool", lambda e: e.tensor_tensor(out=sqt_p[:], in0=ptm[5][:], in1=ptm[5][:], op=ALU.mult),
                             r=[P(5)], w=["sqt_p"])
                        yield
                        S.op("pe", lambda e: e.matmul(ps[:, 7, :], lhsT=blk64[:], rhs=sqt_p[:], start=True, stop=True),
                             r=["blk64", "sqt_p"], w=[PS(7)])
                        yield
                        S.op("dve", lambda e: e.tensor_tensor(out=ptm[3][:], in0=ps[:, 7, :], in1=ptm[1][:], op=ALU.mult),
                             r=[P(1)], w=[PS(7), P(3)])
                        yield
                        S.op("act", lambda e: e.activation(out=ptm[4][:], in_=ptm[3][:], func=AF.Ln, bias=epsc[:], scale=1.0),
                             r=[P(3), "epsc"], w=[P(4)])
                        S.op("act", lambda e: e.activation(out=ptm[3][:], in_=ptm[4][:], func=AF.Exp, bias=zeroc[:], scale=-0.5),
                             r=[P(4), "zeroc"], w=[P(3)])
                        yield
                        S.op("dve", lambda e: e.tensor_tensor(out=ptm[4][:], in0=ptm[3][:], in1=ptm[0][:], op=ALU.mult),
                             r=[P(3), P(0)], w=[P(4)])
                        if ty == "q":
                            dst = qT[:, qi, tsl]
                            dkey = QT
                        else:
                            dst = kT[:, tsl]
                            dkey = KT
                        if sc0 is None:
                            S.op("dve", lambda e, gi=gi, dst=dst: e.scalar_tensor_tensor(
                                out=dst, in0=ptm[5][:], scalar=gc[:, gi:gi + 1], in1=ptm[4][:], op0=ALU.mult, op1=ALU.mult),
                                r=["gc", P(4), P(5)], w=[dkey])
                            yield
                        else:
                            S.op("dve", lambda e, gi=gi: e.scalar_tensor_tensor(
                                out=ptm[6][:], in0=ptm[5][:], scalar=gc[:, gi:gi + 1], in1=rC[:], op0=ALU.mult, op1=ALU.mult),
                                r=["gc", "rC", P(5)], w=[P(6)])
                            yield
                            mm8(sc0, xb, XB)
                            yield
                            S.op("dve", lambda e, sgi=sgi: e.scalar_tensor_tensor(
                                out=ptm[7][:], in0=ps[:, 7, :], scalar=gc[:, sgi:sgi + 1], in1=rS[:], op0=ALU.mult, op1=ALU.mult),
                                r=["gc", "rS"], w=[PS(7), P(7)])
                            yield
                            S.op("pool", lambda e: e.tensor_tensor(out=ptm[6][:], in0=ptm[6][:], in1=ptm[7][:], op=ALU.add),
                                 r=[P(7)], w=[P(6)])
                            S.op("pool", lambda e, dst=dst: e.tensor_tensor(out=dst, in0=ptm[6][:], in1=ptm[4][:], op=ALU.mult),
                                 r=[P(6), P(4)], w=[dkey])
                            yield

            epi_count = [0]

            def attention(kind, idx, ub, bg, every):
                kT, vaug, qT = kTs[ub], vaugs[ub], qTs[ub]
                KT, VA, QT = ("kT", ub), ("vaug", ub), ("qT", ub)
                if kind == "A":
                    passes = [(0, 2 * idx), (1, 2 * idx + 1)]
                else:
                    passes = [(0, 4 + idx)]
                    S.dma("sp", lambda e: e.dma_start(out=bt[:], in_=btoep[:, idx, :]), "bt", w=["bt"])
                    S.dma("sp", lambda e: e.dma_start(out=bx[:], in_=bcross[:, idx, :, :]), "bx", w=["bx"])
                h = idx
                its = []
                for (qi, chunk) in passes:
                    for qb in range(NQB):
                        for kt in range(NKT):
                            its.append((qi, chunk, qb, kt))

                def emit_qk(n):
                    qi, chunk, qb, kt = its[n]
                    b0 = 2 * (n % 2)
                    ksl = slice(kt * 128, (kt + 1) * 128)
                    qsl = slice(qb * TT, (qb + 1) * TT)
                    S.op("pe", lambda e: e.matmul(ps[:, b0, :], lhsT=kT[0:64, ksl], rhs=qT[0:64, qi, qsl], start=True,
                                                  stop=True, tile_position=(0, 0)),
                         r=[KT, QT], w=[PS(b0)], sig=False)
                    S.op("pe", lambda e: e.matmul(ps[:, b0 + 1, :], lhsT=kT[64:128, ksl], rhs=qT[64:128, qi, qsl],
                                                  start=True, stop=True, tile_position=(64, 0)),
                         r=[KT, QT], w=[PS(b0 + 1)])

                emit_qk(0)
                emit_qk(1)
                for n in range(len(its)):
                    qi, chunk, qb, kt = its[n]
                    b0 = 2 * (n % 2)
                    pt = pT[n % 3]
                    PT = ("pT", n % 3)
                    scale = SCALE
                    bias_ap = zeroc[:]
                    bias_key = "zeroc"
                    if kind == "B":
                        info = bias_info(qb, kt)
                        if info[0] == "far":
                            ci = 3 * h + info[1]
                            bias_ap = cfar_t[:, ci:ci + 1]
                            bias_key = "cfar_t"
                        else:
                            scale = 1.0
                            if info[0] == "toep":
                                s0 = info[1]
                                bsrc = bt[:, s0:s0 + 512]
                                bkey = "bt"
                            else:
                                bsrc = bx[:, info[1], :]
                                bkey = "bx"
                            for bb in (b0, b0 + 1):
                                S.op("dve", lambda e, bb=bb, bsrc=bsrc: e.scalar_tensor_tensor(
                                    out=ps[:, bb, :], in0=ps[:, bb, :], scalar=SCALE, in1=bsrc, op0=ALU.mult, op1=ALU.add),
                                    r=[bkey], w=[PS(bb)])
                    S.op("act", lambda e, pt=pt, b0=b0, bias_ap=bias_ap, scale=scale: e.activation(
                        out=pt[:], in_=ps[:, b0:b0 + 2, :], func=AF.Exp, bias=bias_ap, scale=scale),
                        r=[bias_key], w=[PS(b0), PS(b0 + 1), PT])
                    if n + 2 < len(its):
                        emit_qk(n + 2)
                    st, sp_ = (kt == 0), (kt == NKT - 1)
                    if kind == "B":
                        for mp, bank in ((0, 4), (1, 5)):
                            for half in (0, 1):
                                lo = 64 * half
                                S.op("pe", lambda e, pt=pt, kt=kt, st=st, sp_=sp_, mp=mp, bank=bank, lo=lo: e.matmul(
                                    ps[lo:lo + 64, bank, :], lhsT=vaug[:, kt, lo:lo + 64], rhs=pt[:, mp, :], start=st,
                                    stop=sp_, tile_position=(0, lo)), r=[VA, PT], w=[PS(bank)], sig=False)
                        S.op("pe", lambda e, pt=pt, st=st, sp_=sp_: e.matmul(
                            ps[0:64, 6, :], lhsT=onesb[:, 0:64], rhs=pt[:, 0, :], start=st, stop=sp_, tile_position=(0, 0)),
                            r=["onesb", PT], w=[PS(6)], sig=False)
                        S.op("pe", lambda e, pt=pt, st=st, sp_=sp_: e.matmul(
                            ps[64:128, 6, :], lhsT=onesb[:, 0:64], rhs=pt[:, 1, :], start=st, stop=sp_, tile_position=(0, 64)),
                            r=["onesb", PT], w=[PS(6)])
                    else:
                        S.op("pe", lambda e, pt=pt, kt=kt, st=st, sp_=sp_: e.matmul(
                            ps[:, 4, :], lhsT=vaug[:, kt, 0:128], rhs=pt[:, 0, :], start=st, stop=sp_),
                            r=[VA, PT], w=[PS(4)], sig=False)
                        S.op("pe", lambda e, pt=pt, kt=kt, st=st, sp_=sp_: e.matmul(
                            ps[:, 5, :], lhsT=vaug[:, kt, 64:192], rhs=pt[:, 1, :], start=st, stop=sp_),
                            r=[VA, PT], w=[PS(5)])
                    if sp_:
                        qsl = slice(qb * TT, (qb + 1) * TT)
                        mi = epi_count[0] % 2
                        epi_count[0] += 1
                        ms = mst[mi]
                        MS = ("mst", mi)
                        if kind == "B":
                            S.op("dve", lambda e: e.reciprocal(out=et[0][:], in_=ps[:, 6, :]), w=[PS(6), E(0)])
                            S.op("pe", lambda e: e.matmul(ps[:, 6, :], lhsT=selA[:], rhs=et[0][:], start=True, stop=True),
                                 r=["selA", E(0)], w=[PS(6)])
                            S.op("dve", lambda e: e.tensor_copy(out=et[1][:], in_=ps[:, 6, :]), w=[PS(6), E(1)])
                            S.op("pe", lambda e: e.matmul(ps[:, 6, :], lhsT=selB[:], rhs=et[0][:], start=True, stop=True),
                                 r=["selB", E(0)], w=[PS(6)])
                            S.op("dve", lambda e: e.tensor_tensor(out=et[2][:], in0=ps[:, 4, :], in1=et[1][:], op=ALU.mult),
                                 r=[E(1)], w=[PS(4), E(2)])
                            S.op("dve", lambda e: e.tensor_copy(out=et[1][:], in_=ps[:, 6, :]), w=[PS(6), E(1)])
                            S.op("dve", lambda e: e.tensor_tensor(out=et[3][:], in0=ps[:, 5, :], in1=et[1][:], op=ALU.mult),
                                 r=[E(1)], w=[PS(5), E(3)])
                            S.op("dve", lambda e: e.scalar_tensor_tensor(out=et[4][:], in0=et[3][:], scalar=neglam[:, 0:1],
                                                                         in1=et[2][:], op0=ALU.mult, op1=ALU.add),
                                 r=[E(3), E(2), "neglam"], w=[E(4)])
                            S.op("pool", lambda e: e.tensor_tensor(out=sqt_e[:], in0=et[4][:], in1=et[4][:], op=ALU.mult),
                                 r=[E(4)], w=["sqt_e"])
                            S.op("pe", lambda e: e.matmul(ps[:, 6, :], lhsT=ones128[:], rhs=sqt_e[:], start=True, stop=True),
                                 r=["ones128", "sqt_e"], w=[PS(6)])
                            S.op("act", lambda e: e.activation(out=et[5][:], in_=ps[:, 6, :], func=AF.Ln, bias=epsc[:],
                                                               scale=1.0), r=["epsc"], w=[PS(6), E(5)])
                            S.op("act", lambda e: e.activation(out=et[0][:], in_=et[5][:], func=AF.Exp, bias=zeroc[:],
                                                               scale=-0.5), r=[E(5), "zeroc"], w=[E(0)])
                            S.op("dve", lambda e, ms=ms: e.scalar_tensor_tensor(
                                out=ms[:], in0=et[4][:], scalar=gsub[:, 0:1], in1=et[0][:], op0=ALU.mult, op1=ALU.mult),
                                r=[E(4), E(0), "gsub"], w=[MS])
                        else:
                            for s in (0, 1):
                                lo, hi = (0, 64) if s == 0 else (64, 128)
                                S.op("dve", lambda e, s=s: e.tensor_copy(out=et[s][:], in_=ps[:, 4 + s, :]),
                                     w=[PS(4 + s), E(s)])
                                S.op("pe", lambda e, s=s: e.matmul(ps[:, 6, :], lhsT=swapf[:], rhs=et[s][:], start=True,
                                                                   stop=True), r=["swapf", E(s)], w=[PS(6)])
                                S.op("dve", lambda e, s=s, lo=lo, hi=hi: e.reciprocal(out=et[2 + s][lo:hi, :],
                                                                                      in_=ps[lo:hi, 6, :]),
                                     w=[PS(6), E(2 + s)])
                                S.op("dve", lambda e, s=s, lo=lo, hi=hi, ms=ms: e.tensor_tensor(
                                    out=ms[lo:hi, :], in0=et[s][lo:hi, :], in1=et[2 + s][lo:hi, :], op=ALU.mult),
                                    r=[E(s), E(2 + s)], w=[MS])
                        S.dma("sp", lambda e, ms=ms, chunk=chunk, qsl=qsl: e.dma_start(out=mix_s[chunk][:, qsl], in_=ms[:]),
                              ("mso", mi), r=[MS], w=[("mix", chunk, qb)])
                    if bg is not None and (n % every) == 0:
                        next(bg, None)
                if bg is not None:
                    for _ in bg:
                        pass

            units = [("B", 0, 1), ("A", 0, 0), ("B", 1, 1), ("A", 1, 0), ("B", 2, 1), ("B", 3, 0)]
            est = {"A": 440, "B": 300}
            pg = prologue_gen()
            for _ in proj_gen(*units[0]):
                next(pg, None)
            for ui, (kind, idx, ub) in enumerate(units):
                nits = 1024 if kind == "A" else 512
                if ui + 1 < len(units):
                    nk = units[ui + 1][0]
                    bg = itertools.chain(pg, proj_gen(*units[ui + 1]))
                    every = max(1, nits // est[nk])
                else:
                    bg, every = None, 1
                attention(kind, idx, ub, bg, every)
            S.barrier()
            S.replay()

        with contextlib.ExitStack() as fst:
            woutb = sb(fst, "woutb", [128, 8, 1024], BF16)
            xts = [sb(fst, "xt%d" % i, [128, 8, 512], F32) for i in range(2)]
            mxs = [sb(fst, "mx%d" % i, [128, 8, 512], BF16) for i in range(2)]
            h2 = sb(fst, "h2", [128, 8, 512], BF16)
            uT = sb(fst, "uT", [128, 32, 512], BF16)
            wsb = [sb(fst, "wsb%d" % i, [128, 4096], BF16) for i in range(4)]
            lnv = sb(fst, "lnv", [128, 512], F32)
            rstd = sb(fst, "rstd", [128, 512], F32)
            rl = [sb(fst, "rl%d" % i, [128, 512], F32) for i in range(2)]

            S.dma("pool", lambda e: e.dma_start(out=woutb[:], in_=wout), "woutb", w=["woutb"])

            loads = []
            for i in range(NQB):
                for g in range(8):
                    loads.append(("u", g))
                for m in range(8):
                    loads.append(("d", m))
            issued = [0]

            def issue_loads(upto):
                while issued[0] < min(upto, len(loads)):
                    k = issued[0]
                    ty, j = loads[k]
                    src = wup_s[j] if ty == "u" else wdn_s[j]
                    bi = k % 4
                    S.dma("sp", lambda e, src=src, bi=bi: e.dma_start(out=wsb[bi][:], in_=src), ("wsb", bi),
                          w=[("wsb", bi)])
                    issued[0] += 1

            def load_xt(i):
                S.dma("sp", lambda e: e.dma_start(out=xts[i % 2][:], in_=xT3[:, :, i * TT:(i + 1) * TT]), ("xt", i % 2),
                      w=[("xt", i % 2)])
                S.dma("sp", lambda e: e.dma_start(out=mxs[i % 2][:], in_=mix3[:, :, i * TT:(i + 1) * TT]), ("mx", i % 2),
                      w=[("mx", i % 2)])

            bankc = [0]

            def nb():
                b = bankc[0] % 8
                bankc[0] += 1
                return b

            load_xt(0)
            lk = 0
            for i in range(NQB):
                xt = xts[i % 2]
                XT = ("xt", i % 2)
                mx = mxs[i % 2]
                MX = ("mx", i % 2)
                tsl = slice(i * TT, (i + 1) * TT)
                issue_loads(lk + 2)
                for m in range(8):
                    b = nb()
                    for c in range(8):
                        S.op("pe", lambda e, b=b, c=c, m=m, mx=mx: e.matmul(
                            ps[:, b, :], lhsT=woutb[:, c, m * 128:(m + 1) * 128], rhs=mx[:, c, :], start=(c == 0),
                            stop=(c == 7)), r=["woutb", MX], w=[PS(b)], sig=(c == 7))
                    S.op("dve", lambda e, b=b, m=m, xt=xt: e.tensor_tensor(out=xt[:, m, :], in0=ps[:, b, :], in1=xt[:, m, :],
                                                                            op=ALU.add), w=[PS(b), XT])
                if i + 1 < NQB:
                    load_xt(i + 1)
                S.op("act", lambda e, xt=xt: e.activation(out=h2[:], in_=xt[:], func=AF.Square), r=[XT], w=["h2"])
                b = nb()
                for c in range(8):
                    S.op("pe", lambda e, b=b, c=c: e.matmul(ps[:, b, :], lhsT=ones1024[:], rhs=h2[:, c, :], start=(c == 0),
                                                            stop=(c == 7)), r=["ones1024", "h2"], w=[PS(b)], sig=(c == 7))
                S.op("act", lambda e, b=b: e.activation(out=lnv[:], in_=ps[:, b, :], func=AF.Ln, bias=epsc[:], scale=1.0),
                     r=["epsc"], w=[PS(b), "lnv"])
                S.op("act", lambda e: e.activation(out=rstd[:], in_=lnv[:], func=AF.Exp, bias=zeroc[:], scale=-0.5),
                     r=["lnv", "zeroc"], w=["rstd"])
                for c in range(8):
                    S.op("dve", lambda e, c=c, xt=xt: e.tensor_tensor(out=h2[:, c, :], in0=xt[:, c, :], in1=rstd[:],
                                                                      op=ALU.mult), r=[XT, "rstd"], w=["h2"])
                for g in range(8):
                    bi = lk % 4
                    issue_loads(lk + 3)
                    for jj in range(4):
                        j = 4 * g + jj
                        b = nb()
                        for c in range(8):
                            S.op("pe", lambda e, b=b, c=c, jj=jj, bi=bi: e.matmul(
                                ps[:, b, :], lhsT=wsb[bi][:, c * 512 + jj * 128:c * 512 + (jj + 1) * 128], rhs=h2[:, c, :],
                                start=(c == 0), stop=(c == 7)), r=[("wsb", bi), "h2"], w=[PS(b)], sig=(c == 7))
                        ri = j % 2
                        S.op("act", lambda e, b=b, ri=ri: e.activation(out=rl[ri][:], in_=ps[:, b, :], func=AF.Relu),
                             w=[PS(b), ("rl", ri)])
                        S.op("pool", lambda e, j=j, ri=ri: e.tensor_tensor(out=uT[:, j, :], in0=rl[ri][:], in1=rl[ri][:],
                                                                           op=ALU.mult), r=[("rl", ri)], w=["uT"])
                    lk += 1
                for m in range(8):
                    bi = lk % 4
                    issue_loads(lk + 3)
                    b = nb()
                    for j in range(32):
                        S.op("pe", lambda e, b=b, j=j, bi=bi: e.matmul(
                            ps[:, b, :], lhsT=wsb[bi][:, j * 128:(j + 1) * 128], rhs=uT[:, j, :], start=(j == 0),
                            stop=(j == 31)), r=[("wsb", bi), "uT"], w=[PS(b)], sig=(j == 31))
                    S.op("dve", lambda e, b=b, m=m, xt=xt: e.tensor_tensor(out=xt[:, m, :], in0=ps[:, b, :], in1=xt[:, m, :],
                                                                            op=ALU.add), w=[PS(b), XT])
                    lk += 1
                S.dma("sp", lambda e, xt=xt, tsl=tsl: e.dma_start(out=outT3[:, :, tsl], in_=xt[:]), ("out", i % 2),
                      r=[XT], w=[("outT", i)])
            S.wait_slots("sp", [("out", 0), ("out", 1)])
            S.barrier()
            S.replay()
    return nc


def _host_tables():
    import jax
    import jax.numpy as jnp
    cpu = jax.devices("cpu")[0]
    with jax.default_device(cpu):
        rows = SEQ // 64
        row = jnp.broadcast_to(jnp.arange(rows)[:, None], (rows, 64)).reshape(-1).astype(jnp.float32)
        col = jnp.broadcast_to(jnp.arange(64)[None, :], (rows, 64)).reshape(-1).astype(jnp.float32)
        half = HD // 2
        inv_freq = 1.0 / (10000.0 ** (jnp.arange(0, half, 2, dtype=jnp.float32) / half))
        ang_r = row[:, None] * inv_freq[None, :]
        ang_c = col[:, None] * inv_freq[None, :]
        cr, sr, cc, sc = (np.asarray(t.astype(jnp.float32)) for t in
                          (jnp.cos(ang_r), jnp.sin(ang_r), jnp.cos(ang_c), jnp.sin(ang_c)))

        def t5_bucket(rel):
            nb = 16
            max_exact = 8
            ret = (rel > 0).astype(jnp.int32) * nb
            n = jnp.abs(rel)
            nf = jnp.maximum(n, 1).astype(jnp.float32)
            large = max_exact + (jnp.log(nf / max_exact) / math.log(128 / max_exact) * (nb - max_exact)).astype(jnp.int32)
            large = jnp.minimum(large, nb - 1)
            return ret + jnp.where(n < max_exact, n, large)

        rel = jnp.arange(-SEQ, SEQ + 1, dtype=jnp.int32)
        bucket = np.asarray(t5_bucket(rel))
    C = np.empty((64, SEQ), np.float32)
    Sg = np.empty((64, SEQ), np.float32)
    for d in range(64):
        j = d % 16
        first = (d % 32) < 16
        if d < 32:
            c_, s_ = cr[:, j], sr[:, j]
        else:
            c_, s_ = cc[:, j], sc[:, j]
        C[d] = c_
        Sg[d] = -s_ if first else s_
    return C, Sg, bucket


def _partner(d):
    return d + 16 if (d % 32) < 16 else d - 16


def _prep(inputs):
    f = lambda k: np.asarray(inputs[k], dtype=np.float32)
    x = f("x")
    w_in = f("w_in")[0]
    C, Sg, bucket = _host_tables()
    rel_bias = f("rel_bias")
    dd = np.arange(64)
    pd = np.array([_partner(d) for d in range(64)])

    def chunked(w):
        return np.ascontiguousarray(w.reshape(8, 128, -1).transpose(1, 0, 2))

    wA = np.empty((2, 128, 8, 896), np.float32)
    for kv in range(2):
        cols = []
        for pair in range(2):
            hs = [4 * kv + 2 * pair, 4 * kv + 2 * pair + 1]
            cols.append(np.concatenate([h * 64 + dd for h in hs]))
            cols.append(np.concatenate([h * 64 + pd for h in hs]))
        kc = 512 + kv * 64
        cols.append(np.concatenate([kc + dd, kc + dd]))
        cols.append(np.concatenate([kc + pd, kc + pd]))
        vc = 640 + kv * 64
        cols.append(np.concatenate([vc + dd, vc + dd]))
        wA[kv] = chunked(w_in[:, np.concatenate(cols)])
    wB = np.empty((4, 128, 8, 384), np.float32)
    for h in range(4):
        cols = np.concatenate([768 + h * 128 + np.arange(128), 1280 + h * 128 + np.arange(128),
                               1792 + h * 128 + np.arange(128)])
        wB[h] = chunked(w_in[:, cols])

    gcol = np.zeros((128, 24), np.float32)
    gcol[:, 0:8] = f("attn_norm_g")[0].reshape(8, 128).T
    gcol[:, 8:16] = f("mlp_norm_g")[0].reshape(8, 128).T
    p64 = np.arange(128) % 64
    aq, ak, bq, bk = f("a_q_norm_g")[0], f("a_k_norm_g")[0], f("b_q_norm_g")[0], f("b_k_norm_g")[0]
    gcol[:, 16] = aq[p64]
    gcol[:, 17] = aq[pd[p64]]
    gcol[:, 18] = ak[p64]
    gcol[:, 19] = ak[pd[p64]]
    gcol[:, 20] = bq[p64]
    gcol[:, 21] = bk[p64]
    gcol[:, 22] = f("b_subln_g")[0]

    lamv = np.empty((128, 256), np.float32)
    lamv[:, 0:64] = f("lambda_q1")[0][None, :]
    lamv[:, 64:128] = f("lambda_k1")[0][None, :]
    lamv[:, 128:192] = f("lambda_q2")[0][None, :]
    lamv[:, 192:256] = f("lambda_k2")[0][None, :]

    cmat = np.zeros((128, 4, 128), np.float32)
    cmat[:, 0, :] = np.eye(128, dtype=np.float32)
    for m in range(128):
        cmat[(m + 64) % 128, 1, m] = 1.0
    cmat[0:64, 2, :] = 1.0 / 64.0
    cmat[64:128, 3, :] = 1.0 / 64.0

    wout = chunked(f("w_out")[0])
    w_up = f("w_up")[0]
    wup = np.ascontiguousarray(w_up.reshape(8, 128, 8, 512).transpose(2, 1, 0, 3))
    w_down = f("w_down")[0]
    wdn = np.ascontiguousarray(w_down.reshape(32, 128, 8, 128).transpose(2, 1, 0, 3))

    ii = np.arange(128)[:, None]
    uu = np.arange(1152)[None, :]
    btoep = np.ascontiguousarray(rel_bias[bucket[(ii - uu + 512) + SEQ]].transpose(0, 2, 1))

    common = dict(wA=wA, wB=wB, gcol=gcol, lamv=lamv, btoep=btoep, cmat=cmat, wout=wout, wup=wup, wdn=wdn)
    in_maps = []
    for core in range(8):
        b, qh = core // 2, core % 2
        order = np.concatenate([np.arange(qh * NQ, (qh + 1) * NQ), np.arange((1 - qh) * NQ, (2 - qh) * NQ)])
        xT = np.ascontiguousarray(x[b].T[:, order])
        ropeC = np.ascontiguousarray(np.concatenate([C, C], axis=0)[:, order])
        ropeS = np.ascontiguousarray(np.concatenate([Sg, Sg], axis=0)[:, order])
        pos = order
        jj = np.arange(512)[None, :]
        bcross = np.empty((128, 4, 2, 512), np.float32)
        r0 = pos[4096 + np.arange(128)][:, None] - pos[3584 + np.arange(512)][None, :]
        r1 = pos[8064 + np.arange(128)][:, None] - pos[0 + np.arange(512)][None, :]
        bcross[:, :, 0, :] = rel_bias[bucket[r0 + SEQ]].transpose(0, 2, 1)
        bcross[:, :, 1, :] = rel_bias[bucket[r1 + SEQ]].transpose(0, 2, 1)
        cfar = np.empty((128, 12), np.float32)
        for h in range(4):
            cfar[:, 3 * h + 0] = rel_bias[15, h]
            cfar[:, 3 * h + 1] = rel_bias[31, h]
            cfar[:, 3 * h + 2] = rel_bias[31, h] if qh == 0 else rel_bias[15, h]
        m = dict(common)
        m.update(xT=xT, ropeC=ropeC, ropeS=ropeS, bcross=bcross, cfar=cfar)
        in_maps.append(m)
    return in_maps


_NC_CACHE = {}


def kernel(**inputs):
    in_maps = _prep(inputs)
    if "nc" not in _NC_CACHE:
        _NC_CACHE["nc"] = build_program()
    nc = _NC_CACHE["nc"]
    res = run_bass_kernel_spmd(nc, in_maps, core_ids=list(range(8)))
    out = np.empty((4, SEQ, D_MODEL), np.float32)
    for core in range(8):
        b, qh = core // 2, core % 2
        out[b, qh * NQ:(qh + 1) * NQ, :] = res.results[core]["outT"].T
    return out
```

```python
import contextlib
import itertools
import math

import numpy as np
import concourse.bass as bass
import concourse.mybir as mybir
from concourse.bass_utils import run_bass_kernel_spmd

F32 = mybir.dt.float32
BF16 = mybir.dt.bfloat16
ALU = mybir.AluOpType
AF = mybir.ActivationFunctionType
AX = mybir.AxisListType

D_MODEL = 1024
SEQ = 8192
NQ = 4096
HD = 64
EPS = 1e-6
SCALE = HD ** -0.5
TT = 512
NT = SEQ // TT
NQB = NQ // TT
NKT = SEQ // 128
LAMBDA_INIT = 0.8 - 0.6 * math.exp(-0.3 * 0)


class Sched:
    EPOCH = 16000

    def __init__(self, nc, stack):
        self.nc = nc
        self.stack = stack
        self.names = ["pe", "act", "dve", "pool", "sp"]
        self.streams = {e: [] for e in self.names}
        self.esem = {}
        self.ecnt = {e: 0 for e in self.names}
        self.semobj = []
        self.slots = {}
        self.last_w = {}
        self.readers = {}
        self.waited = {e: {} for e in self.names}

    def _newsem(self, name):
        h = self.stack.enter_context(self.nc.semaphore(name))
        self.semobj.append(h)
        return len(self.semobj) - 1

    def _deps(self, eng, r, w):
        need = {}
        wt = self.waited[eng]

        def add(sid, val, teng):
            if eng == "pe" and teng == "pe":
                return
            if wt.get(sid, 0) >= val:
                return
            if need.get(sid, 0) < val:
                need[sid] = val

        for k in r:
            t = self.last_w.get(k)
            if t is not None:
                add(*t)
        for k in w:
            t = self.last_w.get(k)
            if t is not None:
                add(*t)
            rd = self.readers.get(k)
            if rd:
                for sid, (val, teng) in rd.items():
                    add(sid, val, teng)
        for sid, val in need.items():
            wt[sid] = val
            self.streams[eng].append(("w", sid, val))

    def _record(self, tok, r, w):
        for k in r:
            d = self.readers.setdefault(k, {})
            if d.get(tok[0], (0, None))[0] < tok[1]:
                d[tok[0]] = (tok[1], tok[2])
        for k in w:
            self.last_w[k] = tok
            self.readers[k] = {}

    def op(self, eng, fn, r=(), w=(), sig=True):
        self._deps(eng, r, w)
        if eng not in self.esem:
            self.esem[eng] = self._newsem("e_%s_%d" % (eng, len(self.semobj)))
        if sig:
            self.ecnt[eng] += 1
            tok = (self.esem[eng], self.ecnt[eng], eng)
            self.streams[eng].append(("o", fn, tok[0]))
            if self.ecnt[eng] >= self.EPOCH:
                self.esem[eng] = self._newsem("e_%s_%d" % (eng, len(self.semobj)))
                self.ecnt[eng] = 0
        else:
            tok = (self.esem[eng], self.ecnt[eng] + 1, eng)
            self.streams[eng].append(("o", fn, None))
        self._record(tok, r, w)

    def dma(self, q, fn, slot, r=(), w=()):
        self._deps(q, r, w)
        if slot not in self.slots or self.slots[slot][1] >= 30000:
            self.slots[slot] = [self._newsem("d_%d" % len(self.semobj)), 0]
        s = self.slots[slot]
        s[1] += 16
        tok = (s[0], s[1], None)
        self.streams[q].append(("d", fn, s[0]))
        self._record(tok, r, w)

    def barrier(self):
        toks = []
        for e, sid in self.esem.items():
            if self.ecnt[e] > 0:
                toks.append((sid, self.ecnt[e], e))
        for s in self.slots.values():
            if s[1] > 0:
                toks.append((s[0], s[1], None))
        for e in self.names:
            wt = self.waited[e]
            for sid, val, teng in toks:
                if e == "pe" and teng == "pe":
                    continue
                if wt.get(sid, 0) >= val:
                    continue
                wt[sid] = val
                self.streams[e].append(("w", sid, val))
        self.last_w.clear()
        self.readers.clear()

    def wait_slots(self, eng, slots):
        for sl in slots:
            s = self.slots[sl]
            self.streams[eng].append(("w", s[0], s[1]))

    def replay(self):
        nc = self.nc
        semobj = self.semobj
        streams = self.streams

        def run(name, e):
            for it in streams[name]:
                if it[0] == "w":
                    e.wait_ge(semobj[it[1]], it[2])
                elif it[0] == "o":
                    ins = it[1](e)
                    if it[2] is not None:
                        ins.then_inc(semobj[it[2]], 1)
                else:
                    it[1](e).then_inc(semobj[it[2]], 16)

        with nc.Block() as block:
            @block.tensor
            def _(e):
                run("pe", e)

            @block.scalar
            def _(e):
                run("act", e)

            @block.vector
            def _(e):
                run("dve", e)

            @block.gpsimd
            def _(e):
                run("pool", e)

            @block.sync
            def _(e):
                run("sp", e)
        for k in self.names:
            self.streams[k] = []


def bias_info(qb, kt):
    if kt < 32:
        o = 128 * kt - 512 * qb
        if o <= -256:
            return ("far", 0)
        if o >= 640:
            return ("far", 1)
        return ("toep", 512 - o)
    if qb == 7 and kt == 32:
        return ("cross", 0)
    if qb == 0 and kt == 63:
        return ("cross", 1)
    return ("far", 2)


def build_program():
    nc = bass.Bass("TRN2", target_bir_lowering=False)

    def din(name, shape):
        return nc.dram_tensor(name, list(shape), F32, kind="ExternalInput").ap()

    xT = din("xT", [1024, SEQ])
    wA = din("wA", [2, 128, 8, 896])
    wB = din("wB", [4, 128, 8, 384])
    gcol = din("gcol", [128, 24])
    lamv = din("lamv", [128, 256])
    ropeC = din("ropeC", [128, SEQ])
    ropeS = din("ropeS", [128, SEQ])
    btoep = din("btoep", [128, 4, 1152])
    bcross = din("bcross", [128, 4, 2, 512])
    cfar = din("cfar", [128, 12])
    cmat = din("cmat", [128, 4, 128])
    wout = din("wout", [128, 8, 1024])
    wup = din("wup", [8, 128, 8, 512])
    wdn = din("wdn", [8, 128, 32, 128])
    outT = nc.dram_tensor("outT", [1024, NQ], F32, kind="ExternalOutput").ap()
    wup_s = nc.dram_tensor("wup_s", [8, 128, 4096], BF16).ap()
    wdn_s = nc.dram_tensor("wdn_s", [8, 128, 4096], BF16).ap()
    mix_s = nc.dram_tensor("mix_s", [8, 128, NQ], BF16).ap()

    xT3 = xT.rearrange("(c p) t -> p c t", p=128)
    outT3 = outT.rearrange("(c p) t -> p c t", p=128)
    mix3 = mix_s.rearrange("c p t -> p c t")

    with contextlib.ExitStack() as top:
        S = Sched(nc, top)

        def sb(stack, name, shape, dt):
            return stack.enter_context(nc.sbuf_tensor(name, list(shape), dt))

        ps = top.enter_context(nc.psum_tensor("ps", [128, 8, 512], F32))
        ps7b = ps[:, 7, :].bitcast(BF16)

        def PS(b):
            return ("ps", b)

        identb = sb(top, "identb", [128, 128], BF16)
        swapf = sb(top, "swapf", [128, 128], F32)
        selA = sb(top, "selA", [128, 128], F32)
        selB = sb(top, "selB", [128, 128], F32)
        ones1024 = sb(top, "ones1024", [128, 128], BF16)
        blk64 = sb(top, "blk64", [128, 128], BF16)
        ones128 = sb(top, "ones128", [128, 128], BF16)
        onesb = sb(top, "onesb", [128, 128], BF16)
        gc = sb(top, "gc", [128, 24], F32)
        cfar_t = sb(top, "cfar_t", [128, 12], F32)
        epsc = sb(top, "epsc", [128, 1], F32)
        zeroc = sb(top, "zeroc", [128, 1], F32)
        neglam = sb(top, "neglam", [128, 1], F32)
        gsub = sb(top, "gsub", [128, 1], F32)
        lamt = sb(top, "lamt", [128, 256], F32)
        lamp = sb(top, "lamp", [128, 128], F32)
        lams = sb(top, "lams", [128, 4], F32)

        S.dma("pool", lambda e: e.dma_start(out=identb[:], in_=cmat[:, 0, :]), "c0", w=["identb"])
        S.dma("sp", lambda e: e.dma_start(out=swapf[:], in_=cmat[:, 1, :]), "c1", w=["swapf"])
        S.dma("sp", lambda e: e.dma_start(out=selA[:], in_=cmat[:, 2, :]), "c5", w=["selA"])
        S.dma("sp", lambda e: e.dma_start(out=selB[:], in_=cmat[:, 3, :]), "c6", w=["selB"])
        S.dma("sp", lambda e: e.dma_start(out=gc[:], in_=gcol), "c2", w=["gc"])
        S.dma("sp", lambda e: e.dma_start(out=cfar_t[:], in_=cfar), "c3", w=["cfar_t"])
        S.dma("sp", lambda e: e.dma_start(out=lamt[:], in_=lamv), "c4", w=["lamt"])
        S.op("pool", lambda e: e.memset(ones1024[:], 1.0 / 1024.0), w=["ones1024"])
        S.op("pool", lambda e: e.memset(ones128[:], 1.0 / 128.0), w=["ones128"])
        S.op("pool", lambda e: e.memset(onesb[:], 1.0), w=["onesb"])
        S.op("pool", lambda e: e.memset(blk64[:], 0.0), w=["blk64"])
        S.op("pool", lambda e: e.memset(blk64[0:64, 0:64], 1.0 / 64.0), w=["blk64"])
        S.op("pool", lambda e: e.memset(blk64[64:128, 64:128], 1.0 / 64.0), w=["blk64"])
        S.op("pool", lambda e: e.memset(epsc[:], EPS), w=["epsc"])
        S.op("pool", lambda e: e.memset(zeroc[:], 0.0), w=["zeroc"])

        S.op("dve", lambda e: e.tensor_tensor(out=lamp[:, 0:64], in0=lamt[:, 0:64], in1=lamt[:, 64:128], op=ALU.mult),
             r=["lamt"], w=["lamp"])
        S.op("dve", lambda e: e.tensor_tensor(out=lamp[:, 64:128], in0=lamt[:, 128:192], in1=lamt[:, 192:256], op=ALU.mult),
             r=["lamt"], w=["lamp"])
        S.op("dve", lambda e: e.tensor_reduce(out=lams[:, 0:1], in_=lamp[:, 0:64], axis=AX.X, op=ALU.add),
             r=["lamp"], w=["lams"])
        S.op("dve", lambda e: e.tensor_reduce(out=lams[:, 1:2], in_=lamp[:, 64:128], axis=AX.X, op=ALU.add),
             r=["lamp"], w=["lams"])
        S.op("act", lambda e: e.activation(out=lams[:, 2:4], in_=lams[:, 0:2], func=AF.Exp, bias=zeroc[:], scale=1.0),
             r=["lams", "zeroc"], w=["lams2"])
        S.op("dve", lambda e: e.tensor_tensor(out=lams[:, 0:1], in0=lams[:, 3:4], in1=lams[:, 2:3], op=ALU.subtract),
             r=["lams2"], w=["lams"])
        S.op("dve", lambda e: e.tensor_scalar(out=neglam[:], in0=lams[:, 0:1], scalar1=-LAMBDA_INIT, scalar2=None, op0=ALU.add),
             r=["lams"], w=["neglam"])
        S.op("dve", lambda e: e.tensor_scalar(out=gsub[:], in0=gc[:, 22:23], scalar1=1.0 - LAMBDA_INIT, scalar2=None, op0=ALU.mult),
             r=["gc"], w=["gsub"])

        with contextlib.ExitStack() as ast:
            kTs = [sb(ast, "kT%d" % i, [128, SEQ], BF16) for i in range(2)]
            vaugs = [sb(ast, "vaug0", [128, NKT, 192], BF16), sb(ast, "vaug1", [128, NKT, 128], BF16)]
            qTs = [sb(ast, "qT0", [128, 2, NQ], BF16), sb(ast, "qT1", [128, 1, NQ], BF16)]
            pT = [sb(ast, "pT%d" % i, [128, 2, 512], BF16) for i in range(3)]
            wbf = sb(ast, "wbf", [128, 8, 896], BF16)
            wstage = sb(ast, "wstage", [128, 896], F32)
            xbf = [sb(ast, "xbf%d" % i, [128, 8, 512], BF16) for i in range(2)]
            xsq = sb(ast, "xsq", [128, 8, 512], BF16)
            rC = sb(ast, "rC", [128, 512], F32)
            rS = sb(ast, "rS", [128, 512], F32)
            ptm = [sb(ast, "ptm%d" % i, [128, 512], F32) for i in range(8)]
            et = [sb(ast, "et%d" % i, [128, 512], F32) for i in range(6)]
            sqt_p = sb(ast, "sqt_p", [128, 512], BF16)
            sqt_e = sb(ast, "sqt_e", [128, 512], BF16)
            vtmp = sb(ast, "vtmp", [128, 512], BF16)
            bt = sb(ast, "bt", [128, 1152], F32)
            bx = sb(ast, "bx", [128, 2, 512], F32)
            mst = [sb(ast, "mst%d" % i, [128, 512], BF16) for i in range(2)]
            pst_f = [sb(ast, "pst_f%d" % i, [128, 512], F32) for i in range(2)]
            pst_b = [sb(ast, "pst_b%d" % i, [128, 512], BF16) for i in range(2)]
            pst_d = sb(ast, "pst_d", [128, 1024], BF16)

            def prologue_gen():
                for g in range(8):
                    for c in range(8):
                        i = c % 2
                        S.dma("sp", lambda e, g=g, c=c, i=i: e.dma_start(out=pst_f[i][:], in_=wup[g][:, c, :]),
                              ("pwst", i), w=[("pwst", i)])
                        S.op("dve", lambda e, c=c, i=i: e.tensor_scalar(
                            out=pst_b[i][:], in0=pst_f[i][:], scalar1=gc[:, 8 + c:9 + c], scalar2=None, op0=ALU.mult),
                            r=[("pwst", i), "gc"], w=[("pwbo", i)])
                        S.dma("sp", lambda e, g=g, c=c, i=i: e.dma_start(out=wup_s[g][:, c * 512:(c + 1) * 512],
                                                                         in_=pst_b[i][:]),
                              ("pwo", i), r=[("pwbo", i)], w=[("wup_s", g, c)])
                        yield
                for m in range(8):
                    for hh in range(4):
                        src = wdn[m].rearrange("p j e -> p (j e)")[:, hh * 1024:(hh + 1) * 1024]
                        S.dma("pool", lambda e, src=src: e.dma_start(out=pst_d[:], in_=src), "pwb2", w=["pwb2"])
                        S.dma("sp", lambda e, m=m, hh=hh: e.dma_start(out=wdn_s[m][:, hh * 1024:(hh + 1) * 1024],
                                                                      in_=pst_d[:]),
                              "pwo2", r=["pwb2"], w=[("wdn_s", m, hh)])
                        yield

            def P(i):
                return ("ptm", i)

            def E(i):
                return ("et", i)

            S.op("pool", lambda e: e.memset(vaugs[0][:, :, 64:128], 1.0), w=[("vaug", 0)])

            def proj_gen(kind, idx, ub):
                kT, vaug, qT = kTs[ub], vaugs[ub], qTs[ub]
                KT, VA, QT = ("kT", ub), ("vaug", ub), ("qT", ub)
                if kind == "A":
                    wsrc, ncol = wA[idx], 896
                    tiles = [("q", 0, 128, 16, 17, 0), ("q", 256, 384, 16, 17, 1),
                             ("k", 512, 640, 18, 19, None), ("v", 768, None, None, None, None)]
                else:
                    wsrc, ncol = wB[idx], 384
                    tiles = [("q", 0, None, 20, None, 0), ("k", 128, None, 21, None, None),
                             ("v", 256, None, None, None, None)]
                for c in range(8):
                    S.dma("sp", lambda e, c=c: e.dma_start(out=wstage[:, 0:ncol], in_=wsrc[:, c, :]), "wstage",
                          w=["wstage"])
                    S.op("dve", lambda e, c=c: e.tensor_scalar(out=wbf[:, c, 0:ncol], in0=wstage[:, 0:ncol],
                                                               scalar1=gc[:, c:c + 1], scalar2=None, op0=ALU.mult),
                         r=["wstage", "gc"], w=["wbf"])
                    yield

                def load_x(t):
                    i = t % 2
                    S.dma("pool", lambda e: e.dma_start(out=xbf[i][:], in_=xT3[:, :, t * TT:(t + 1) * TT]),
                          ("xbf", i), w=[("xbf", i)])

                def mm8(c0, xb, XB):
                    for c in range(8):
                        S.op("pe", lambda e, c=c: e.matmul(ps[:, 7, :], lhsT=wbf[:, c, c0:c0 + 128], rhs=xb[:, c, :],
                                                           start=(c == 0), stop=(c == 7)),
                             r=["wbf", XB], w=[PS(7)], sig=(c == 7))

                load_x(0)
                for t in range(NT):
                    own = t < NQB
                    xb = xbf[t % 2]
                    XB = ("xbf", t % 2)
                    if t + 1 < NT:
                        load_x(t + 1)
                    tsl = slice(t * TT, (t + 1) * TT)
                    for hh in range(2):
                        S.op("dve", lambda e, xb=xb, hh=hh: e.tensor_tensor(
                            out=xsq[:, 4 * hh:4 * hh + 4, :], in0=xb[:, 4 * hh:4 * hh + 4, :], in1=xb[:, 4 * hh:4 * hh + 4, :],
                            op=ALU.mult), r=[XB], w=[("xsq", hh)])
                        yield 0
                    for hh in range(2):
                        for c in range(4 * hh, 4 * hh + 4):
                            S.op("pe", lambda e, c=c: e.matmul(ps[:, 7, :], lhsT=ones1024[:], rhs=xsq[:, c, :],
                                                               start=(c == 0), stop=(c == 7)),
                                 r=["ones1024", ("xsq", hh)], w=[PS(7)], sig=(c == 7 or c == 3))
                        yield 1
                    S.op("act", lambda e: e.activation(out=ptm[2][:], in_=ps[:, 7, :], func=AF.Ln, bias=epsc[:], scale=1.0),
                         r=["epsc"], w=[PS(7), P(2)])
                    S.op("act", lambda e: e.activation(out=ptm[0][:], in_=ptm[2][:], func=AF.Exp, bias=zeroc[:], scale=-0.5),
                         r=[P(2), "zeroc"], w=[P(0)])
                    yield 0
                    S.op("dve", lambda e: e.tensor_tensor(out=ptm[1][:], in0=ptm[0][:], in1=ptm[0][:], op=ALU.mult),
                         r=[P(0)], w=[P(1)])
                    if kind == "A":
                        S.dma("sp", lambda e, tsl=tsl: e.dma_start(out=rC[:], in_=ropeC[:, tsl]), "rC", w=["rC"])
                        S.dma("sp", lambda e, tsl=tsl: e.dma_start(out=rS[:], in_=ropeS[:, tsl]), "rS", w=["rS"])
                    yield 0
                    for (ty, c0, sc0, gi, sgi, qi) in tiles:
                        if ty == "q" and not own:
                            continue
                        mm8(c0, xb, XB)
                        yield 1
                        if ty == "v":
                            S.op("dve", lambda e: e.tensor_tensor(out=vtmp[:], in0=ps[:, 7, :], in1=ptm[0][:], op=ALU.mult),
                                 r=[P(0)], w=[PS(7), "vtmp"])
                            yield
                            for j in range(4):
                                S.op("pe", lambda e, j=j: e.transpose(out=ps7b[:, j * 128:(j + 1) * 128],
                                                                      in_=vtmp[:, j * 128:(j + 1) * 128],
                                                                      identity=identb[:]),
                                     r=["vtmp", "identb"], w=[PS(7)], sig=(j == 3))
                            yield 1
                            src = ps7b[:, 0:512].rearrange("p (j e) -> p j e", e=128)
                            if kind == "B":
                                S.op("dve", lambda e, t=t, src=src: e.tensor_copy(out=vaug[:, 4 * t:4 * t + 4, 0:128], in_=src),
                                     w=[PS(7), VA])
                            else:
                                S.op("dve", lambda e, t=t, src=src: e.tensor_copy(out=vaug[:, 4 * t:4 * t + 4, 0:64],
                                                                                  in_=src[:, :, 0:64]),
                                     w=[PS(7), VA])
                                S.op("dve", lambda e, t=t, src=src: e.tensor_copy(out=vaug[:, 4 * t:4 * t + 4, 128:192],
                                                                                  in_=src[:, :, 0:64]),
                                     w=[PS(7), VA])
                            yield
                            continue
                        S.op("dve", lambda e: e.tensor_copy(out=ptm[5][:], in_=ps[:, 7, :]), w=[PS(7), P(5)])
                        S.op("pool", lambda e: e.tensor_tensor(out=sqt_p[:], in0=ptm[5][:], in1=ptm[5][:], op=ALU.mult),
                             r=[P(5)], w=["sqt_p"])
                        yield 0
                        yield 0
                        S.op("pe", lambda e: e.matmul(ps[:, 7, :], lhsT=blk64[:], rhs=sqt_p[:], start=True, stop=True),
                             r=["blk64", "sqt_p"], w=[PS(7)])
                        yield 1
                        S.op("dve", lambda e: e.tensor_tensor(out=ptm[3][:], in0=ps[:, 7, :], in1=ptm[1][:], op=ALU.mult),
                             r=[P(1)], w=[PS(7), P(3)])
                        yield
                        S.op("act", lambda e: e.activation(out=ptm[4][:], in_=ptm[3][:], func=AF.Ln, bias=epsc[:], scale=1.0),
                             r=[P(3), "epsc"], w=[P(4)])
                        S.op("act", lambda e: e.activation(out=ptm[3][:], in_=ptm[4][:], func=AF.Exp, bias=zeroc[:], scale=-0.5),
                             r=[P(4), "zeroc"], w=[P(3)])
                        yield
                        S.op("dve", lambda e: e.tensor_tensor(out=ptm[4][:], in0=ptm[3][:], in1=ptm[0][:], op=ALU.mult),
                             r=[P(3), P(0)], w=[P(4)])
                        if ty == "q":
                            dst = qT[:, qi, tsl]
                            dkey = QT
                        else:
                            dst = kT[:, tsl]
                            dkey = KT
                        if sc0 is None:
                            S.op("dve", lambda e, gi=gi, dst=dst: e.scalar_tensor_tensor(
                                out=dst, in0=ptm[5][:], scalar=gc[:, gi:gi + 1], in1=ptm[4][:], op0=ALU.mult, op1=ALU.mult),
                                r=["gc", P(4), P(5)], w=[dkey])
                            yield
                        else:
                            S.op("dve", lambda e, gi=gi: e.scalar_tensor_tensor(
                                out=ptm[6][:], in0=ptm[5][:], scalar=gc[:, gi:gi + 1], in1=rC[:], op0=ALU.mult, op1=ALU.mult),
                                r=["gc", "rC", P(5)], w=[P(6)])
                            yield
                            mm8(sc0, xb, XB)
                            yield 1
                            S.op("dve", lambda e, sgi=sgi: e.scalar_tensor_tensor(
                                out=ptm[7][:], in0=ps[:, 7, :], scalar=gc[:, sgi:sgi + 1], in1=rS[:], op0=ALU.mult, op1=ALU.mult),
                                r=["gc", "rS"], w=[PS(7), P(7)])
                            yield
                            S.op("pool", lambda e: e.tensor_tensor(out=ptm[6][:], in0=ptm[6][:], in1=ptm[7][:], op=ALU.add),
                                 r=[P(7)], w=[P(6)])
                            S.op("pool", lambda e, dst=dst: e.tensor_tensor(out=dst, in0=ptm[6][:], in1=ptm[4][:], op=ALU.mult),
                                 r=[P(6), P(4)], w=[dkey])
                            yield

            epi_count = [0]

            def attention(kind, idx, ub, bg, every):
                kT, vaug, qT = kTs[ub], vaugs[ub], qTs[ub]
                KT, VA, QT = ("kT", ub), ("vaug", ub), ("qT", ub)
                if kind == "A":
                    passes = [(0, 2 * idx), (1, 2 * idx + 1)]
                else:
                    passes = [(0, 4 + idx)]
                    S.dma("sp", lambda e: e.dma_start(out=bt[:], in_=btoep[:, idx, :]), "bt", w=["bt"])
                    S.dma("sp", lambda e: e.dma_start(out=bx[:], in_=bcross[:, idx, :, :]), "bx", w=["bx"])
                h = idx
                its = []
                for (qi, chunk) in passes:
                    for qb in range(NQB):
                        for kt in range(NKT):
                            its.append((qi, chunk, qb, kt))
                state = {"live": 0, "hold": False}

                def bg_step():
                    if bg is not None:
                        v = next(bg, 0)
                        state["live"] = 1 if v else 0

                def emit_qk(n):
                    qi, chunk, qb, kt = its[n]
                    b0 = 2 * (n % 2)
                    ksl = slice(kt * 128, (kt + 1) * 128)
                    qsl = slice(qb * TT, (qb + 1) * TT)
                    S.op("pe", lambda e: e.matmul(ps[:, b0, :], lhsT=kT[0:64, ksl], rhs=qT[0:64, qi, qsl], start=True,
                                                  stop=True, tile_position=(0, 0)),
                         r=[KT, QT], w=[PS(b0)], sig=False)
                    S.op("pe", lambda e: e.matmul(ps[:, b0 + 1, :], lhsT=kT[64:128, ksl], rhs=qT[64:128, qi, qsl],
                                                  start=True, stop=True, tile_position=(64, 0)),
                         r=[KT, QT], w=[PS(b0 + 1)])
                    if kind == "B":
                        info = bias_info(qb, kt)
                        if info[0] != "far":
                            if info[0] == "toep":
                                bsrc = bt[:, info[1]:info[1] + 512]
                                bkey = "bt"
                            else:
                                bsrc = bx[:, info[1], :]
                                bkey = "bx"
                            for bb in (b0, b0 + 1):
                                S.op("dve", lambda e, bb=bb, bsrc=bsrc: e.scalar_tensor_tensor(
                                    out=ps[:, bb, :], in0=ps[:, bb, :], scalar=SCALE, in1=bsrc, op0=ALU.mult, op1=ALU.add),
                                    r=[bkey], w=[PS(bb)])

                def epi_gen(chunk, qb):
                    qsl = slice(qb * TT, (qb + 1) * TT)
                    mi = epi_count[0] % 2
                    epi_count[0] += 1
                    ms = mst[mi]
                    MS = ("mst", mi)
                    if kind == "B":
                        S.op("dve", lambda e: e.reciprocal(out=et[0][:], in_=ps[:, 6, :]), w=[PS(6), E(0)])
                        S.op("dve", lambda e: e.tensor_copy(out=et[1][:], in_=ps[:, 4, :]), w=[PS(4), E(1)])
                        S.op("dve", lambda e: e.tensor_copy(out=et[2][:], in_=ps[:, 5, :]), w=[PS(5), E(2)])
                        S.dma("sp", lambda e: e.dma_start(out=et[4][64:128, :], in_=et[0][0:64, :]), "er1", r=[E(0)], w=[E(4)])
                        S.dma("sp", lambda e: e.dma_start(out=et[5][0:64, :], in_=et[0][64:128, :]), "er2", r=[E(0)], w=[E(5)])
                        yield
                        yield
                        yield
                        yield
                        S.op("dve", lambda e: e.tensor_tensor(out=et[1][0:64, :], in0=et[1][0:64, :], in1=et[0][0:64, :],
                                                              op=ALU.mult), r=[E(0)], w=[E(1)])
                        S.op("dve", lambda e: e.tensor_tensor(out=et[1][64:128, :], in0=et[1][64:128, :], in1=et[4][64:128, :],
                                                              op=ALU.mult), r=[E(4)], w=[E(1)])
                        S.op("dve", lambda e: e.tensor_tensor(out=et[2][0:64, :], in0=et[2][0:64, :], in1=et[5][0:64, :],
                                                              op=ALU.mult), r=[E(5)], w=[E(2)])
                        S.op("dve", lambda e: e.tensor_tensor(out=et[2][64:128, :], in0=et[2][64:128, :], in1=et[0][64:128, :],
                                                              op=ALU.mult), r=[E(0)], w=[E(2)])
                        S.op("dve", lambda e: e.scalar_tensor_tensor(out=et[3][:], in0=et[2][:], scalar=neglam[:, 0:1],
                                                                     in1=et[1][:], op0=ALU.mult, op1=ALU.add),
                             r=[E(2), E(1), "neglam"], w=[E(3)])
                        S.op("pool", lambda e: e.tensor_tensor(out=sqt_e[:], in0=et[3][:], in1=et[3][:], op=ALU.mult),
                             r=[E(3)], w=["sqt_e"])
                        yield
                        yield
                        yield
                        yield
                        while state["live"]:
                            bg_step()
                        state["hold"] = True
                        S.op("pe", lambda e: e.matmul(ps[:, 7, :], lhsT=ones128[:], rhs=sqt_e[:], start=True, stop=True),
                             r=["ones128", "sqt_e"], w=[PS(7)])
                        yield
                        S.op("act", lambda e: e.activation(out=et[4][:], in_=ps[:, 7, :], func=AF.Ln, bias=epsc[:],
                                                           scale=1.0), r=["epsc"], w=[PS(7), E(4)])
                        S.op("act", lambda e: e.activation(out=et[0][:], in_=et[4][:], func=AF.Exp, bias=zeroc[:],
                                                           scale=-0.5), r=[E(4), "zeroc"], w=[E(0)])
                        state["hold"] = False
                        yield
                        yield
                        S.op("dve", lambda e: e.scalar_tensor_tensor(
                            out=ms[:], in0=et[3][:], scalar=gsub[:, 0:1], in1=et[0][:], op0=ALU.mult, op1=ALU.mult),
                            r=[E(3), E(0), "gsub"], w=[MS])
                    else:
                        S.op("dve", lambda e: e.tensor_copy(out=et[0][:], in_=ps[:, 4, :]), w=[PS(4), E(0)])
                        S.op("dve", lambda e: e.tensor_copy(out=et[1][:], in_=ps[:, 5, :]), w=[PS(5), E(1)])
                        S.dma("sp", lambda e: e.dma_start(out=et[2][0:64, :], in_=et[0][64:128, :]), "er1", r=[E(0)], w=[("e2", 0)])
                        S.dma("sp", lambda e: e.dma_start(out=et[2][64:128, :], in_=et[1][0:64, :]), "er2", r=[E(1)], w=[("e2", 1)])
                        yield
                        yield
                        yield
                        yield
                        S.op("dve", lambda e: e.reciprocal(out=et[3][:], in_=et[2][:]), r=[("e2", 0), ("e2", 1)], w=[E(3)])
                        S.op("dve", lambda e: e.tensor_tensor(out=ms[0:64, :], in0=et[0][0:64, :], in1=et[3][0:64, :],
                                                              op=ALU.mult), r=[E(0), E(3)], w=[MS])
                        S.op("dve", lambda e: e.tensor_tensor(out=ms[64:128, :], in0=et[1][64:128, :], in1=et[3][64:128, :],
                                                              op=ALU.mult), r=[E(1), E(3)], w=[MS])
                    S.dma("sp", lambda e: e.dma_start(out=mix_s[chunk][:, qsl], in_=ms[:]),
                          ("mso", mi), r=[MS], w=[("mix", chunk, qb)])

                epis = []

                def epi_step():
                    for g in list(epis):
                        try:
                            next(g)
                        except StopIteration:
                            epis.remove(g)

                emit_qk(0)
                emit_qk(1)
                for n in range(len(its)):
                    qi, chunk, qb, kt = its[n]
                    b0 = 2 * (n % 2)
                    pt = pT[n % 3]
                    PT = ("pT", n % 3)
                    scale = SCALE
                    bias_ap = zeroc[:]
                    bias_key = "zeroc"
                    if kind == "B":
                        info = bias_info(qb, kt)
                        if info[0] == "far":
                            ci = 3 * h + info[1]
                            bias_ap = cfar_t[:, ci:ci + 1]
                            bias_key = "cfar_t"
                        else:
                            scale = 1.0
                    S.op("act", lambda e, pt=pt, b0=b0, bias_ap=bias_ap, scale=scale: e.activation(
                        out=pt[:], in_=ps[:, b0:b0 + 2, :], func=AF.Exp, bias=bias_ap, scale=scale),
                        r=[bias_key], w=[PS(b0), PS(b0 + 1), PT])
                    if n + 2 < len(its):
                        emit_qk(n + 2)
                    st, sp_ = (kt == 0), (kt == NKT - 1)
                    if kind == "B":
                        for mp, bank in ((0, 4), (1, 5)):
                            for half in (0, 1):
                                lo = 64 * half
                                S.op("pe", lambda e, pt=pt, kt=kt, st=st, sp_=sp_, mp=mp, bank=bank, lo=lo: e.matmul(
                                    ps[lo:lo + 64, bank, :], lhsT=vaug[:, kt, lo:lo + 64], rhs=pt[:, mp, :], start=st,
                                    stop=sp_, tile_position=(0, lo)), r=[VA, PT], w=[PS(bank)], sig=False)
                        S.op("pe", lambda e, pt=pt, st=st, sp_=sp_: e.matmul(
                            ps[0:64, 6, :], lhsT=onesb[:, 0:64], rhs=pt[:, 0, :], start=st, stop=sp_, tile_position=(0, 0)),
                            r=["onesb", PT], w=[PS(6)], sig=False)
                        S.op("pe", lambda e, pt=pt, st=st, sp_=sp_: e.matmul(
                            ps[64:128, 6, :], lhsT=onesb[:, 0:64], rhs=pt[:, 1, :], start=st, stop=sp_, tile_position=(0, 64)),
                            r=["onesb", PT], w=[PS(6)])
                    else:
                        S.op("pe", lambda e, pt=pt, kt=kt, st=st, sp_=sp_: e.matmul(
                            ps[:, 4, :], lhsT=vaug[:, kt, 0:128], rhs=pt[:, 0, :], start=st, stop=sp_),
                            r=[VA, PT], w=[PS(4)], sig=False)
                        S.op("pe", lambda e, pt=pt, kt=kt, st=st, sp_=sp_: e.matmul(
                            ps[:, 5, :], lhsT=vaug[:, kt, 64:192], rhs=pt[:, 1, :], start=st, stop=sp_),
                            r=[VA, PT], w=[PS(5)])
                    epi_step()
                    if sp_:
                        g = epi_gen(chunk, qb)
                        epis.append(g)
                        next(g)
                    if bg is not None and (n % every) == 0 and not state["hold"]:
                        bg_step()
                while epis:
                    epi_step()
                if bg is not None:
                    for _ in bg:
                        pass

            units = [("B", 0, 1), ("A", 0, 0), ("B", 1, 1), ("A", 1, 0), ("B", 2, 1), ("B", 3, 0)]
            est = {"A": 520, "B": 360}
            pg = prologue_gen()
            for _ in proj_gen(*units[0]):
                next(pg, None)
            for ui, (kind, idx, ub) in enumerate(units):
                nits = 1024 if kind == "A" else 512
                if ui + 1 < len(units):
                    nk = units[ui + 1][0]
                    bg = itertools.chain(pg, proj_gen(*units[ui + 1]))
                    every = max(1, nits // est[nk])
                else:
                    bg, every = None, 1
                attention(kind, idx, ub, bg, every)
            S.barrier()
            S.replay()

        with contextlib.ExitStack() as fst:
            woutb = sb(fst, "woutb", [128, 8, 1024], BF16)
            xts = [sb(fst, "xt%d" % i, [128, 8, 512], F32) for i in range(2)]
            mxs = [sb(fst, "mx%d" % i, [128, 8, 512], BF16) for i in range(2)]
            h2 = sb(fst, "h2", [128, 8, 512], BF16)
            uT = sb(fst, "uT", [128, 32, 512], BF16)
            wsb = [sb(fst, "wsb%d" % i, [128, 4096], BF16) for i in range(4)]
            lnv = sb(fst, "lnv", [128, 512], F32)
            rstd = sb(fst, "rstd", [128, 512], F32)
            rl = [sb(fst, "rl%d" % i, [128, 512], F32) for i in range(2)]

            S.dma("pool", lambda e: e.dma_start(out=woutb[:], in_=wout), "woutb", w=["woutb"])

            loads = []
            for i in range(NQB):
                for g in range(8):
                    loads.append(("u", g))
                for m in range(8):
                    loads.append(("d", m))
            issued = [0]

            def issue_loads(upto):
                while issued[0] < min(upto, len(loads)):
                    k = issued[0]
                    ty, j = loads[k]
                    src = wup_s[j] if ty == "u" else wdn_s[j]
                    bi = k % 4
                    S.dma("sp", lambda e, src=src, bi=bi: e.dma_start(out=wsb[bi][:], in_=src), ("wsb", bi),
                          w=[("wsb", bi)])
                    issued[0] += 1

            def load_xt(i):
                S.dma("sp", lambda e: e.dma_start(out=xts[i % 2][:], in_=xT3[:, :, i * TT:(i + 1) * TT]), ("xt", i % 2),
                      w=[("xt", i % 2)])
                S.dma("sp", lambda e: e.dma_start(out=mxs[i % 2][:], in_=mix3[:, :, i * TT:(i + 1) * TT]), ("mx", i % 2),
                      w=[("mx", i % 2)])

            bankc = [0]

            def nb():
                b = bankc[0] % 8
                bankc[0] += 1
                return b

            load_xt(0)
            lk = 0
            for i in range(NQB):
                xt = xts[i % 2]
                XT = ("xt", i % 2)
                mx = mxs[i % 2]
                MX = ("mx", i % 2)
                tsl = slice(i * TT, (i + 1) * TT)
                issue_loads(lk + 2)
                for m in range(8):
                    b = nb()
                    for c in range(8):
                        S.op("pe", lambda e, b=b, c=c, m=m, mx=mx: e.matmul(
                            ps[:, b, :], lhsT=woutb[:, c, m * 128:(m + 1) * 128], rhs=mx[:, c, :], start=(c == 0),
                            stop=(c == 7)), r=["woutb", MX], w=[PS(b)], sig=(c == 7))
                    S.op("dve", lambda e, b=b, m=m, xt=xt: e.tensor_tensor(out=xt[:, m, :], in0=ps[:, b, :], in1=xt[:, m, :],
                                                                            op=ALU.add), w=[PS(b), XT])
                if i + 1 < NQB:
                    load_xt(i + 1)
                S.op("act", lambda e, xt=xt: e.activation(out=h2[:], in_=xt[:], func=AF.Square), r=[XT], w=["h2"])
                b = nb()
                for c in range(8):
                    S.op("pe", lambda e, b=b, c=c: e.matmul(ps[:, b, :], lhsT=ones1024[:], rhs=h2[:, c, :], start=(c == 0),
                                                            stop=(c == 7)), r=["ones1024", "h2"], w=[PS(b)], sig=(c == 7))
                S.op("act", lambda e, b=b: e.activation(out=lnv[:], in_=ps[:, b, :], func=AF.Ln, bias=epsc[:], scale=1.0),
                     r=["epsc"], w=[PS(b), "lnv"])
                S.op("act", lambda e: e.activation(out=rstd[:], in_=lnv[:], func=AF.Exp, bias=zeroc[:], scale=-0.5),
                     r=["lnv", "zeroc"], w=["rstd"])
                for c in range(8):
                    S.op("dve", lambda e, c=c, xt=xt: e.tensor_tensor(out=h2[:, c, :], in0=xt[:, c, :], in1=rstd[:],
                                                                      op=ALU.mult), r=[XT, "rstd"], w=["h2"])
                for g in range(8):
                    bi = lk % 4
                    issue_loads(lk + 3)
                    for jj in range(4):
                        j = 4 * g + jj
                        b = nb()
                        for c in range(8):
                            S.op("pe", lambda e, b=b, c=c, jj=jj, bi=bi: e.matmul(
                                ps[:, b, :], lhsT=wsb[bi][:, c * 512 + jj * 128:c * 512 + (jj + 1) * 128], rhs=h2[:, c, :],
                                start=(c == 0), stop=(c == 7)), r=[("wsb", bi), "h2"], w=[PS(b)], sig=(c == 7))
                        ri = j % 2
                        S.op("act", lambda e, b=b, ri=ri: e.activation(out=rl[ri][:], in_=ps[:, b, :], func=AF.Relu),
                             w=[PS(b), ("rl", ri)])
                        S.op("pool", lambda e, j=j, ri=ri: e.tensor_tensor(out=uT[:, j, :], in0=rl[ri][:], in1=rl[ri][:],
                                                                           op=ALU.mult), r=[("rl", ri)], w=["uT"])
                    lk += 1
                for m in range(8):
                    bi = lk % 4
                    issue_loads(lk + 3)
                    b = nb()
                    for j in range(32):
                        S.op("pe", lambda e, b=b, j=j, bi=bi: e.matmul(
                            ps[:, b, :], lhsT=wsb[bi][:, j * 128:(j + 1) * 128], rhs=uT[:, j, :], start=(j == 0),
                            stop=(j == 31)), r=[("wsb", bi), "uT"], w=[PS(b)], sig=(j == 31))
                    S.op("dve", lambda e, b=b, m=m, xt=xt: e.tensor_tensor(out=xt[:, m, :], in0=ps[:, b, :], in1=xt[:, m, :],
                                                                            op=ALU.add), w=[PS(b), XT])
                    lk += 1
                S.dma("sp", lambda e, xt=xt, tsl=tsl: e.dma_start(out=outT3[:, :, tsl], in_=xt[:]), ("out", i % 2),
                      r=[XT], w=[("outT", i)])
            S.wait_slots("sp", [("out", 0), ("out", 1)])
            S.barrier()
            S.replay()
    return nc


def _host_tables():
    import jax
    import jax.numpy as jnp
    cpu = jax.devices("cpu")[0]
    with jax.default_device(cpu):
        rows = SEQ // 64
        row = jnp.broadcast_to(jnp.arange(rows)[:, None], (rows, 64)).reshape(-1).astype(jnp.float32)
        col = jnp.broadcast_to(jnp.arange(64)[None, :], (rows, 64)).reshape(-1).astype(jnp.float32)
        half = HD // 2
        inv_freq = 1.0 / (10000.0 ** (jnp.arange(0, half, 2, dtype=jnp.float32) / half))
        ang_r = row[:, None] * inv_freq[None, :]
        ang_c = col[:, None] * inv_freq[None, :]
        cr, sr, cc, sc = (np.asarray(t.astype(jnp.float32)) for t in
                          (jnp.cos(ang_r), jnp.sin(ang_r), jnp.cos(ang_c), jnp.sin(ang_c)))

        def t5_bucket(rel):
            nb = 16
            max_exact = 8
            ret = (rel > 0).astype(jnp.int32) * nb
            n = jnp.abs(rel)
            nf = jnp.maximum(n, 1).astype(jnp.float32)
            large = max_exact + (jnp.log(nf / max_exact) / math.log(128 / max_exact) * (nb - max_exact)).astype(jnp.int32)
            large = jnp.minimum(large, nb - 1)
            return ret + jnp.where(n < max_exact, n, large)

        rel = jnp.arange(-SEQ, SEQ + 1, dtype=jnp.int32)
        bucket = np.asarray(t5_bucket(rel))
    C = np.empty((64, SEQ), np.float32)
    Sg = np.empty((64, SEQ), np.float32)
    for d in range(64):
        j = d % 16
        first = (d % 32) < 16
        if d < 32:
            c_, s_ = cr[:, j], sr[:, j]
        else:
            c_, s_ = cc[:, j], sc[:, j]
        C[d] = c_
        Sg[d] = -s_ if first else s_
    return C, Sg, bucket


def _partner(d):
    return d + 16 if (d % 32) < 16 else d - 16


def _prep(inputs):
    f = lambda k: np.asarray(inputs[k], dtype=np.float32)
    x = f("x")
    w_in = f("w_in")[0]
    C, Sg, bucket = _host_tables()
    rel_bias = f("rel_bias")
    dd = np.arange(64)
    pd = np.array([_partner(d) for d in range(64)])

    def chunked(w):
        return np.ascontiguousarray(w.reshape(8, 128, -1).transpose(1, 0, 2))

    wA = np.empty((2, 128, 8, 896), np.float32)
    for kv in range(2):
        cols = []
        for pair in range(2):
            hs = [4 * kv + 2 * pair, 4 * kv + 2 * pair + 1]
            cols.append(np.concatenate([h * 64 + dd for h in hs]))
            cols.append(np.concatenate([h * 64 + pd for h in hs]))
        kc = 512 + kv * 64
        cols.append(np.concatenate([kc + dd, kc + dd]))
        cols.append(np.concatenate([kc + pd, kc + pd]))
        vc = 640 + kv * 64
        cols.append(np.concatenate([vc + dd, vc + dd]))
        wA[kv] = chunked(w_in[:, np.concatenate(cols)])
    wB = np.empty((4, 128, 8, 384), np.float32)
    for h in range(4):
        cols = np.concatenate([768 + h * 128 + np.arange(128), 1280 + h * 128 + np.arange(128),
                               1792 + h * 128 + np.arange(128)])
        wB[h] = chunked(w_in[:, cols])

    gcol = np.zeros((128, 24), np.float32)
    gcol[:, 0:8] = f("attn_norm_g")[0].reshape(8, 128).T
    gcol[:, 8:16] = f("mlp_norm_g")[0].reshape(8, 128).T
    p64 = np.arange(128) % 64
    aq, ak, bq, bk = f("a_q_norm_g")[0], f("a_k_norm_g")[0], f("b_q_norm_g")[0], f("b_k_norm_g")[0]
    gcol[:, 16] = aq[p64]
    gcol[:, 17] = aq[pd[p64]]
    gcol[:, 18] = ak[p64]
    gcol[:, 19] = ak[pd[p64]]
    gcol[:, 20] = bq[p64]
    gcol[:, 21] = bk[p64]
    gcol[:, 22] = f("b_subln_g")[0]

    lamv = np.empty((128, 256), np.float32)
    lamv[:, 0:64] = f("lambda_q1")[0][None, :]
    lamv[:, 64:128] = f("lambda_k1")[0][None, :]
    lamv[:, 128:192] = f("lambda_q2")[0][None, :]
    lamv[:, 192:256] = f("lambda_k2")[0][None, :]

    cmat = np.zeros((128, 4, 128), np.float32)
    cmat[:, 0, :] = np.eye(128, dtype=np.float32)
    for m in range(128):
        cmat[(m + 64) % 128, 1, m] = 1.0
    cmat[0:64, 2, :] = 1.0 / 64.0
    cmat[64:128, 3, :] = 1.0 / 64.0

    wout = chunked(f("w_out")[0])
    w_up = f("w_up")[0]
    wup = np.ascontiguousarray(w_up.reshape(8, 128, 8, 512).transpose(2, 1, 0, 3))
    w_down = f("w_down")[0]
    wdn = np.ascontiguousarray(w_down.reshape(32, 128, 8, 128).transpose(2, 1, 0, 3))

    ii = np.arange(128)[:, None]
    uu = np.arange(1152)[None, :]
    btoep = np.ascontiguousarray(rel_bias[bucket[(ii - uu + 512) + SEQ]].transpose(0, 2, 1))

    common = dict(wA=wA, wB=wB, gcol=gcol, lamv=lamv, btoep=btoep, cmat=cmat, wout=wout, wup=wup, wdn=wdn)
    in_maps = []
    for core in range(8):
        b, qh = core // 2, core % 2
        order = np.concatenate([np.arange(qh * NQ, (qh + 1) * NQ), np.arange((1 - qh) * NQ, (2 - qh) * NQ)])
        xT = np.ascontiguousarray(x[b].T[:, order])
        ropeC = np.ascontiguousarray(np.concatenate([C, C], axis=0)[:, order])
        ropeS = np.ascontiguousarray(np.concatenate([Sg, Sg], axis=0)[:, order])
        pos = order
        jj = np.arange(512)[None, :]
        bcross = np.empty((128, 4, 2, 512), np.float32)
        r0 = pos[4096 + np.arange(128)][:, None] - pos[3584 + np.arange(512)][None, :]
        r1 = pos[8064 + np.arange(128)][:, None] - pos[0 + np.arange(512)][None, :]
        bcross[:, :, 0, :] = rel_bias[bucket[r0 + SEQ]].transpose(0, 2, 1)
        bcross[:, :, 1, :] = rel_bias[bucket[r1 + SEQ]].transpose(0, 2, 1)
        cfar = np.empty((128, 12), np.float32)
        for h in range(4):
            cfar[:, 3 * h + 0] = rel_bias[15, h]
            cfar[:, 3 * h + 1] = rel_bias[31, h]
            cfar[:, 3 * h + 2] = rel_bias[31, h] if qh == 0 else rel_bias[15, h]
        m = dict(common)
        m.update(xT=xT, ropeC=ropeC, ropeS=ropeS, bcross=bcross, cfar=cfar)
        in_maps.append(m)
    return in_maps


_NC_CACHE = {}


def kernel(**inputs):
    in_maps = _prep(inputs)
    if "nc" not in _NC_CACHE:
        _NC_CACHE["nc"] = build_program()
    nc = _NC_CACHE["nc"]
    res = run_bass_kernel_spmd(nc, in_maps, core_ids=list(range(8)))
    out = np.empty((4, SEQ, D_MODEL), np.float32)
    for core in range(8):
        b, qh = core // 2, core % 2
        out[b, qh * NQ:(qh + 1) * NQ, :] = res.results[core]["outT"].T
    return out
```

```python
import contextlib
import itertools
import math

import numpy as np
import concourse.bass as bass
import concourse.mybir as mybir
from concourse.bass_utils import run_bass_kernel_spmd

F32 = mybir.dt.float32
BF16 = mybir.dt.bfloat16
ALU = mybir.AluOpType
AF = mybir.ActivationFunctionType
AX = mybir.AxisListType

D_MODEL = 1024
SEQ = 8192
NQ = 4096
HD = 64
EPS = 1e-6
SCALE = HD ** -0.5
TT = 512
NT = SEQ // TT
NQB = NQ // TT
NKT = SEQ // 128
LAMBDA_INIT = 0.8 - 0.6 * math.exp(-0.3 * 0)


class Sched:
    EPOCH = 16000

    def __init__(self, nc, stack):
        self.nc = nc
        self.stack = stack
        self.names = ["pe", "act", "dve", "pool", "sp"]
        self.streams = {e: [] for e in self.names}
        self.esem = {}
        self.ecnt = {e: 0 for e in self.names}
        self.semobj = []
        self.slots = {}
        self.last_w = {}
        self.readers = {}
        self.waited = {e: {} for e in self.names}

    def _newsem(self, name):
        h = self.stack.enter_context(self.nc.semaphore(name))
        self.semobj.append(h)
        return len(self.semobj) - 1

    def _deps(self, eng, r, w):
        need = {}
        wt = self.waited[eng]

        def add(sid, val, teng):
            if eng == "pe" and teng == "pe":
                return
            if wt.get(sid, 0) >= val:
                return
            if need.get(sid, 0) < val:
                need[sid] = val

        for k in r:
            t = self.last_w.get(k)
            if t is not None:
                add(*t)
        for k in w:
            t = self.last_w.get(k)
            if t is not None:
                add(*t)
            rd = self.readers.get(k)
            if rd:
                for sid, (val, teng) in rd.items():
                    add(sid, val, teng)
        for sid, val in need.items():
            wt[sid] = val
            self.streams[eng].append(("w", sid, val))

    def _record(self, tok, r, w):
        for k in r:
            d = self.readers.setdefault(k, {})
            if d.get(tok[0], (0, None))[0] < tok[1]:
                d[tok[0]] = (tok[1], tok[2])
        for k in w:
            self.last_w[k] = tok
            self.readers[k] = {}

    def op(self, eng, fn, r=(), w=(), sig=True):
        self._deps(eng, r, w)
        if eng not in self.esem:
            self.esem[eng] = self._newsem("e_%s_%d" % (eng, len(self.semobj)))
        if sig:
            self.ecnt[eng] += 1
            tok = (self.esem[eng], self.ecnt[eng], eng)
            self.streams[eng].append(("o", fn, tok[0]))
            if self.ecnt[eng] >= self.EPOCH:
                self.esem[eng] = self._newsem("e_%s_%d" % (eng, len(self.semobj)))
                self.ecnt[eng] = 0
        else:
            tok = (self.esem[eng], self.ecnt[eng] + 1, eng)
            self.streams[eng].append(("o", fn, None))
        self._record(tok, r, w)

    def dma(self, q, fn, slot, r=(), w=()):
        self._deps(q, r, w)
        if slot not in self.slots or self.slots[slot][1] >= 30000:
            self.slots[slot] = [self._newsem("d_%d" % len(self.semobj)), 0]
        s = self.slots[slot]
        s[1] += 16
        tok = (s[0], s[1], None)
        self.streams[q].append(("d", fn, s[0]))
        self._record(tok, r, w)

    def barrier(self):
        toks = []
        for e, sid in self.esem.items():
            if self.ecnt[e] > 0:
                toks.append((sid, self.ecnt[e], e))
        for s in self.slots.values():
            if s[1] > 0:
                toks.append((s[0], s[1], None))
        for e in self.names:
            wt = self.waited[e]
            for sid, val, teng in toks:
                if e == "pe" and teng == "pe":
                    continue
                if wt.get(sid, 0) >= val:
                    continue
                wt[sid] = val
                self.streams[e].append(("w", sid, val))
        self.last_w.clear()
        self.readers.clear()

    def wait_slots(self, eng, slots):
        for sl in slots:
            s = self.slots[sl]
            self.streams[eng].append(("w", s[0], s[1]))

    def replay(self):
        nc = self.nc
        semobj = self.semobj
        streams = self.streams

        def run(name, e):
            for it in streams[name]:
                if it[0] == "w":
                    e.wait_ge(semobj[it[1]], it[2])
                elif it[0] == "o":
                    ins = it[1](e)
                    if it[2] is not None:
                        ins.then_inc(semobj[it[2]], 1)
                else:
                    it[1](e).then_inc(semobj[it[2]], 16)

        with nc.Block() as block:
            @block.tensor
            def _(e):
                run("pe", e)

            @block.scalar
            def _(e):
                run("act", e)

            @block.vector
            def _(e):
                run("dve", e)

            @block.gpsimd
            def _(e):
                run("pool", e)

            @block.sync
            def _(e):
                run("sp", e)
        for k in self.names:
            self.streams[k] = []


def bias_info(qb, kt):
    if kt < 32:
        o = 128 * kt - 512 * qb
        if o <= -256:
            return ("far", 0)
        if o >= 640:
            return ("far", 1)
        return ("toep", 512 - o)
    if qb == 7 and kt == 32:
        return ("cross", 0)
    if qb == 0 and kt == 63:
        return ("cross", 1)
    return ("far", 2)


def build_program():
    nc = bass.Bass("TRN2", target_bir_lowering=False)

    def din(name, shape):
        return nc.dram_tensor(name, list(shape), F32, kind="ExternalInput").ap()

    xT = din("xT", [1024, SEQ])
    wA = din("wA", [2, 128, 8, 896])
    wB = din("wB", [4, 128, 8, 384])
    gcol = din("gcol", [128, 24])
    lamv = din("lamv", [128, 256])
    ropeC = din("ropeC", [128, SEQ])
    ropeS = din("ropeS", [128, SEQ])
    btoep = din("btoep", [128, 4, 1152])
    bcross = din("bcross", [128, 4, 2, 512])
    cfar = din("cfar", [128, 12])
    cmat = din("cmat", [128, 4, 128])
    wout = din("wout", [128, 8, 1024])
    wup = din("wup", [8, 128, 8, 512])
    wdn = din("wdn", [8, 128, 32, 128])
    outT = nc.dram_tensor("outT", [1024, NQ], F32, kind="ExternalOutput").ap()
    wup_s = nc.dram_tensor("wup_s", [8, 128, 4096], BF16).ap()
    wdn_s = nc.dram_tensor("wdn_s", [8, 128, 4096], BF16).ap()
    mix_s = nc.dram_tensor("mix_s", [8, 128, NQ], BF16).ap()
    rstd_s = nc.dram_tensor("rstd_s", [NT, 128, 512], F32).ap()

    xT3 = xT.rearrange("(c p) t -> p c t", p=128)
    outT3 = outT.rearrange("(c p) t -> p c t", p=128)
    mix3 = mix_s.rearrange("c p t -> p c t")

    with contextlib.ExitStack() as top:
        S = Sched(nc, top)

        def sb(stack, name, shape, dt):
            return stack.enter_context(nc.sbuf_tensor(name, list(shape), dt))

        ps = top.enter_context(nc.psum_tensor("ps", [128, 8, 512], F32))
        ps7b = ps[:, 7, :].bitcast(BF16)

        def PS(b):
            return ("ps", b)

        identb = sb(top, "identb", [128, 128], BF16)
        swapf = sb(top, "swapf", [128, 128], F32)
        selA = sb(top, "selA", [128, 128], F32)
        selB = sb(top, "selB", [128, 128], F32)
        ones1024 = sb(top, "ones1024", [128, 128], BF16)
        blk64 = sb(top, "blk64", [128, 128], BF16)
        ones128 = sb(top, "ones128", [128, 128], BF16)
        onesb = sb(top, "onesb", [128, 128], BF16)
        gc = sb(top, "gc", [128, 24], F32)
        cfar_t = sb(top, "cfar_t", [128, 12], F32)
        epsc = sb(top, "epsc", [128, 1], F32)
        zeroc = sb(top, "zeroc", [128, 1], F32)
        neglam = sb(top, "neglam", [128, 1], F32)
        gsub = sb(top, "gsub", [128, 1], F32)
        lamt = sb(top, "lamt", [128, 256], F32)
        lamp = sb(top, "lamp", [128, 128], F32)
        lams = sb(top, "lams", [128, 4], F32)

        S.dma("pool", lambda e: e.dma_start(out=identb[:], in_=cmat[:, 0, :]), "c0", w=["identb"])
        S.dma("sp", lambda e: e.dma_start(out=swapf[:], in_=cmat[:, 1, :]), "c1", w=["swapf"])
        S.dma("sp", lambda e: e.dma_start(out=selA[:], in_=cmat[:, 2, :]), "c5", w=["selA"])
        S.dma("sp", lambda e: e.dma_start(out=selB[:], in_=cmat[:, 3, :]), "c6", w=["selB"])
        S.dma("sp", lambda e: e.dma_start(out=gc[:], in_=gcol), "c2", w=["gc"])
        S.dma("sp", lambda e: e.dma_start(out=cfar_t[:], in_=cfar), "c3", w=["cfar_t"])
        S.dma("sp", lambda e: e.dma_start(out=lamt[:], in_=lamv), "c4", w=["lamt"])
        S.op("pool", lambda e: e.memset(ones1024[:], 1.0 / 1024.0), w=["ones1024"])
        S.op("pool", lambda e: e.memset(ones128[:], 1.0 / 128.0), w=["ones128"])
        S.op("pool", lambda e: e.memset(onesb[:], 1.0), w=["onesb"])
        S.op("pool", lambda e: e.memset(blk64[:], 0.0), w=["blk64"])
        S.op("pool", lambda e: e.memset(blk64[0:64, 0:64], 1.0 / 64.0), w=["blk64"])
        S.op("pool", lambda e: e.memset(blk64[64:128, 64:128], 1.0 / 64.0), w=["blk64"])
        S.op("pool", lambda e: e.memset(epsc[:], EPS), w=["epsc"])
        S.op("pool", lambda e: e.memset(zeroc[:], 0.0), w=["zeroc"])

        S.op("dve", lambda e: e.tensor_tensor(out=lamp[:, 0:64], in0=lamt[:, 0:64], in1=lamt[:, 64:128], op=ALU.mult),
             r=["lamt"], w=["lamp"])
        S.op("dve", lambda e: e.tensor_tensor(out=lamp[:, 64:128], in0=lamt[:, 128:192], in1=lamt[:, 192:256], op=ALU.mult),
             r=["lamt"], w=["lamp"])
        S.op("dve", lambda e: e.tensor_reduce(out=lams[:, 0:1], in_=lamp[:, 0:64], axis=AX.X, op=ALU.add),
             r=["lamp"], w=["lams"])
        S.op("dve", lambda e: e.tensor_reduce(out=lams[:, 1:2], in_=lamp[:, 64:128], axis=AX.X, op=ALU.add),
             r=["lamp"], w=["lams"])
        S.op("act", lambda e: e.activation(out=lams[:, 2:4], in_=lams[:, 0:2], func=AF.Exp, bias=zeroc[:], scale=1.0),
             r=["lams", "zeroc"], w=["lams2"])
        S.op("dve", lambda e: e.tensor_tensor(out=lams[:, 0:1], in0=lams[:, 3:4], in1=lams[:, 2:3], op=ALU.subtract),
             r=["lams2"], w=["lams"])
        S.op("dve", lambda e: e.tensor_scalar(out=neglam[:], in0=lams[:, 0:1], scalar1=-LAMBDA_INIT, scalar2=None, op0=ALU.add),
             r=["lams"], w=["neglam"])
        S.op("dve", lambda e: e.tensor_scalar(out=gsub[:], in0=gc[:, 22:23], scalar1=1.0 - LAMBDA_INIT, scalar2=None, op0=ALU.mult),
             r=["gc"], w=["gsub"])

        with contextlib.ExitStack() as ast:
            kTs = [sb(ast, "kT%d" % i, [128, SEQ], BF16) for i in range(2)]
            vaugs = [sb(ast, "vaug0", [128, NKT, 192], BF16), sb(ast, "vaug1", [128, NKT, 128], BF16)]
            qTs = [sb(ast, "qT0", [128, 2, NQ], BF16), sb(ast, "qT1", [128, 1, NQ], BF16)]
            pT = [sb(ast, "pT%d" % i, [128, 2, 512], BF16) for i in range(3)]
            wbf = sb(ast, "wbf", [128, 8, 896], BF16)
            wstage = sb(ast, "wstage", [128, 896], F32)
            xbf = [sb(ast, "xbf%d" % i, [128, 8, 512], BF16) for i in range(2)]
            xsq = sb(ast, "xsq", [128, 8, 512], BF16)
            rC = sb(ast, "rC", [128, 512], F32)
            rS = sb(ast, "rS", [128, 512], F32)
            ptm = [sb(ast, "ptm%d" % i, [128, 512], F32) for i in range(8)]
            et = [sb(ast, "et%d" % i, [128, 512], F32) for i in range(6)]
            sqt_p = sb(ast, "sqt_p", [128, 512], BF16)
            sqt_e = sb(ast, "sqt_e", [128, 512], BF16)
            vtmp = sb(ast, "vtmp", [128, 512], BF16)
            bt = sb(ast, "bt", [128, 1152], F32)
            bx = sb(ast, "bx", [128, 2, 512], F32)
            mst = [sb(ast, "mst%d" % i, [128, 512], BF16) for i in range(2)]
            pst_f = [sb(ast, "pst_f%d" % i, [128, 512], F32) for i in range(2)]
            pst_b = [sb(ast, "pst_b%d" % i, [128, 512], BF16) for i in range(2)]
            pst_d = sb(ast, "pst_d", [128, 1024], BF16)

            def prologue_gen():
                for g in range(8):
                    for c in range(8):
                        i = c % 2
                        S.dma("sp", lambda e, g=g, c=c, i=i: e.dma_start(out=pst_f[i][:], in_=wup[g][:, c, :]),
                              ("pwst", i), w=[("pwst", i)])
                        S.op("dve", lambda e, c=c, i=i: e.tensor_scalar(
                            out=pst_b[i][:], in0=pst_f[i][:], scalar1=gc[:, 8 + c:9 + c], scalar2=None, op0=ALU.mult),
                            r=[("pwst", i), "gc"], w=[("pwbo", i)])
                        S.dma("sp", lambda e, g=g, c=c, i=i: e.dma_start(out=wup_s[g][:, c * 512:(c + 1) * 512],
                                                                         in_=pst_b[i][:]),
                              ("pwo", i), r=[("pwbo", i)], w=[("wup_s", g, c)])
                        yield
                for m in range(8):
                    for hh in range(4):
                        src = wdn[m].rearrange("p j e -> p (j e)")[:, hh * 1024:(hh + 1) * 1024]
                        S.dma("pool", lambda e, src=src: e.dma_start(out=pst_d[:], in_=src), "pwb2", w=["pwb2"])
                        S.dma("sp", lambda e, m=m, hh=hh: e.dma_start(out=wdn_s[m][:, hh * 1024:(hh + 1) * 1024],
                                                                      in_=pst_d[:]),
                              "pwo2", r=["pwb2"], w=[("wdn_s", m, hh)])
                        yield

            def P(i):
                return ("ptm", i)

            def E(i):
                return ("et", i)

            S.op("pool", lambda e: e.memset(vaugs[0][:, :, 64:128], 1.0), w=[("vaug", 0)])

            def proj_gen(kind, idx, ub, first=False):
                kT, vaug, qT = kTs[ub], vaugs[ub], qTs[ub]
                KT, VA, QT = ("kT", ub), ("vaug", ub), ("qT", ub)
                if kind == "A":
                    wsrc, ncol = wA[idx], 896
                    tiles = [("q", 0, 128, 16, 17, 0), ("q", 256, 384, 16, 17, 1),
                             ("k", 512, 640, 18, 19, None), ("v", 768, None, None, None, None)]
                else:
                    wsrc, ncol = wB[idx], 384
                    tiles = [("q", 0, None, 20, None, 0), ("k", 128, None, 21, None, None),
                             ("v", 256, None, None, None, None)]
                for c in range(8):
                    S.dma("sp", lambda e, c=c: e.dma_start(out=wstage[:, 0:ncol], in_=wsrc[:, c, :]), "wstage",
                          w=["wstage"])
                    S.op("dve", lambda e, c=c: e.tensor_scalar(out=wbf[:, c, 0:ncol], in0=wstage[:, 0:ncol],
                                                               scalar1=gc[:, c:c + 1], scalar2=None, op0=ALU.mult),
                         r=["wstage", "gc"], w=["wbf"])
                    yield

                def load_x(t):
                    i = t % 2
                    S.dma("pool", lambda e: e.dma_start(out=xbf[i][:], in_=xT3[:, :, t * TT:(t + 1) * TT]),
                          ("xbf", i), w=[("xbf", i)])

                def mm8(c0, xb, XB):
                    for c in range(8):
                        S.op("pe", lambda e, c=c: e.matmul(ps[:, 7, :], lhsT=wbf[:, c, c0:c0 + 128], rhs=xb[:, c, :],
                                                           start=(c == 0), stop=(c == 7)),
                             r=["wbf", XB], w=[PS(7)], sig=(c == 7))

                load_x(0)
                for t in range(NT):
                    own = t < NQB
                    xb = xbf[t % 2]
                    XB = ("xbf", t % 2)
                    if t + 1 < NT:
                        load_x(t + 1)
                    tsl = slice(t * TT, (t + 1) * TT)
                    if first:
                        for hh in range(2):
                            S.op("dve", lambda e, xb=xb, hh=hh: e.tensor_tensor(
                                out=xsq[:, 4 * hh:4 * hh + 4, :], in0=xb[:, 4 * hh:4 * hh + 4, :], in1=xb[:, 4 * hh:4 * hh + 4, :],
                                op=ALU.mult), r=[XB], w=[("xsq", hh)])
                            yield 0
                        for hh in range(2):
                            for c in range(4 * hh, 4 * hh + 4):
                                S.op("pe", lambda e, c=c: e.matmul(ps[:, 7, :], lhsT=ones1024[:], rhs=xsq[:, c, :],
                                                                   start=(c == 0), stop=(c == 7)),
                                     r=["ones1024", ("xsq", hh)], w=[PS(7)], sig=(c == 7 or c == 3))
                            yield 1
                        S.op("act", lambda e: e.activation(out=ptm[2][:], in_=ps[:, 7, :], func=AF.Ln, bias=epsc[:], scale=1.0),
                             r=["epsc"], w=[PS(7), P(2)])
                        S.op("act", lambda e: e.activation(out=ptm[0][:], in_=ptm[2][:], func=AF.Exp, bias=zeroc[:], scale=-0.5),
                             r=[P(2), "zeroc"], w=[P(0)])
                        yield 0
                        S.dma("sp", lambda e, t=t: e.dma_start(out=rstd_s[t], in_=ptm[0][:]), "rstd_o",
                              r=[P(0)], w=[("rstd_s", t)])
                    else:
                        S.dma("sp", lambda e, t=t: e.dma_start(out=ptm[0][:], in_=rstd_s[t]), "rstd_i",
                              r=[("rstd_s", t)], w=[P(0)])
                        yield 0
                        yield 0
                    S.op("dve", lambda e: e.tensor_tensor(out=ptm[1][:], in0=ptm[0][:], in1=ptm[0][:], op=ALU.mult),
                         r=[P(0)], w=[P(1)])
                    if kind == "A":
                        S.dma("sp", lambda e, tsl=tsl: e.dma_start(out=rC[:], in_=ropeC[:, tsl]), "rC", w=["rC"])
                        S.dma("sp", lambda e, tsl=tsl: e.dma_start(out=rS[:], in_=ropeS[:, tsl]), "rS", w=["rS"])
                    yield 0
                    for (ty, c0, sc0, gi, sgi, qi) in tiles:
                        if ty == "q" and not own:
                            continue
                        mm8(c0, xb, XB)
                        yield 1
                        if ty == "v":
                            S.op("dve", lambda e: e.tensor_tensor(out=vtmp[:], in0=ps[:, 7, :], in1=ptm[0][:], op=ALU.mult),
                                 r=[P(0)], w=[PS(7), "vtmp"])
                            yield
                            for j in range(4):
                                S.op("pe", lambda e, j=j: e.transpose(out=ps7b[:, j * 128:(j + 1) * 128],
                                                                      in_=vtmp[:, j * 128:(j + 1) * 128],
                                                                      identity=identb[:]),
                                     r=["vtmp", "identb"], w=[PS(7)], sig=(j == 3))
                            yield 1
                            src = ps7b[:, 0:512].rearrange("p (j e) -> p j e", e=128)
                            if kind == "B":
                                S.op("dve", lambda e, t=t, src=src: e.tensor_copy(out=vaug[:, 4 * t:4 * t + 4, 0:128], in_=src),
                                     w=[PS(7), VA])
                            else:
                                S.op("dve", lambda e, t=t, src=src: e.tensor_copy(out=vaug[:, 4 * t:4 * t + 4, 0:64],
                                                                                  in_=src[:, :, 0:64]),
                                     w=[PS(7), VA])
                                S.op("dve", lambda e, t=t, src=src: e.tensor_copy(out=vaug[:, 4 * t:4 * t + 4, 128:192],
                                                                                  in_=src[:, :, 0:64]),
                                     w=[PS(7), VA])
                            yield
                            continue
                        S.op("dve", lambda e: e.tensor_copy(out=ptm[5][:], in_=ps[:, 7, :]), w=[PS(7), P(5)])
                        S.op("pool", lambda e: e.tensor_tensor(out=sqt_p[:], in0=ptm[5][:], in1=ptm[5][:], op=ALU.mult),
                             r=[P(5)], w=["sqt_p"])
                        yield 0
                        yield 0
                        S.op("pe", lambda e: e.matmul(ps[:, 7, :], lhsT=blk64[:], rhs=sqt_p[:], start=True, stop=True),
                             r=["blk64", "sqt_p"], w=[PS(7)])
                        yield 1
                        S.op("dve", lambda e: e.tensor_tensor(out=ptm[3][:], in0=ps[:, 7, :], in1=ptm[1][:], op=ALU.mult),
                             r=[P(1)], w=[PS(7), P(3)])
                        yield
                        S.op("act", lambda e: e.activation(out=ptm[4][:], in_=ptm[3][:], func=AF.Ln, bias=epsc[:], scale=1.0),
                             r=[P(3), "epsc"], w=[P(4)])
                        S.op("act", lambda e: e.activation(out=ptm[3][:], in_=ptm[4][:], func=AF.Exp, bias=zeroc[:], scale=-0.5),
                             r=[P(4), "zeroc"], w=[P(3)])
                        yield
                        S.op("dve", lambda e: e.tensor_tensor(out=ptm[4][:], in0=ptm[3][:], in1=ptm[0][:], op=ALU.mult),
                             r=[P(3), P(0)], w=[P(4)])
                        if ty == "q":
                            dst = qT[:, qi, tsl]
                            dkey = QT
                        else:
                            dst = kT[:, tsl]
                            dkey = KT
                        if sc0 is None:
                            S.op("dve", lambda e, gi=gi, dst=dst: e.scalar_tensor_tensor(
                                out=dst, in0=ptm[5][:], scalar=gc[:, gi:gi + 1], in1=ptm[4][:], op0=ALU.mult, op1=ALU.mult),
                                r=["gc", P(4), P(5)], w=[dkey])
                            yield
                        else:
                            S.op("dve", lambda e, gi=gi: e.scalar_tensor_tensor(
                                out=ptm[6][:], in0=ptm[5][:], scalar=gc[:, gi:gi + 1], in1=rC[:], op0=ALU.mult, op1=ALU.mult),
                                r=["gc", "rC", P(5)], w=[P(6)])
                            yield
                            mm8(sc0, xb, XB)
                            yield 1
                            S.op("dve", lambda e, sgi=sgi: e.scalar_tensor_tensor(
                                out=ptm[7][:], in0=ps[:, 7, :], scalar=gc[:, sgi:sgi + 1], in1=rS[:], op0=ALU.mult, op1=ALU.mult),
                                r=["gc", "rS"], w=[PS(7), P(7)])
                            yield
                            S.op("pool", lambda e: e.tensor_tensor(out=ptm[6][:], in0=ptm[6][:], in1=ptm[7][:], op=ALU.add),
                                 r=[P(7)], w=[P(6)])
                            S.op("pool", lambda e, dst=dst: e.tensor_tensor(out=dst, in0=ptm[6][:], in1=ptm[4][:], op=ALU.mult),
                                 r=[P(6), P(4)], w=[dkey])
                            yield

            epi_count = [0]

            def attention(kind, idx, ub, bg, every):
                kT, vaug, qT = kTs[ub], vaugs[ub], qTs[ub]
                KT, VA, QT = ("kT", ub), ("vaug", ub), ("qT", ub)
                if kind == "A":
                    passes = [(0, 2 * idx), (1, 2 * idx + 1)]
                else:
                    passes = [(0, 4 + idx)]
                    S.dma("sp", lambda e: e.dma_start(out=bt[:], in_=btoep[:, idx, :]), "bt", w=["bt"])
                    S.dma("sp", lambda e: e.dma_start(out=bx[:], in_=bcross[:, idx, :, :]), "bx", w=["bx"])
                h = idx
                its = []
                for (qi, chunk) in passes:
                    for qb in range(NQB):
                        for kt in range(NKT):
                            its.append((qi, chunk, qb, kt))
                state = {"live": 0, "hold": False}

                def bg_step():
                    if bg is not None:
                        v = next(bg, 0)
                        state["live"] = 1 if v else 0

                def emit_qk(n):
                    qi, chunk, qb, kt = its[n]
                    b0 = 2 * (n % 2)
                    ksl = slice(kt * 128, (kt + 1) * 128)
                    qsl = slice(qb * TT, (qb + 1) * TT)
                    S.op("pe", lambda e: e.matmul(ps[:, b0, :], lhsT=kT[0:64, ksl], rhs=qT[0:64, qi, qsl], start=True,
                                                  stop=True, tile_position=(0, 0)),
                         r=[KT, QT], w=[PS(b0)], sig=False)
                    S.op("pe", lambda e: e.matmul(ps[:, b0 + 1, :], lhsT=kT[64:128, ksl], rhs=qT[64:128, qi, qsl],
                                                  start=True, stop=True, tile_position=(64, 0)),
                         r=[KT, QT], w=[PS(b0 + 1)])
                    if kind == "B":
                        info = bias_info(qb, kt)
                        if info[0] != "far":
                            if info[0] == "toep":
                                bsrc = bt[:, info[1]:info[1] + 512]
                                bkey = "bt"
                            else:
                                bsrc = bx[:, info[1], :]
                                bkey = "bx"
                            for bb in (b0, b0 + 1):
                                S.op("dve", lambda e, bb=bb, bsrc=bsrc: e.scalar_tensor_tensor(
                                    out=ps[:, bb, :], in0=ps[:, bb, :], scalar=SCALE, in1=bsrc, op0=ALU.mult, op1=ALU.add),
                                    r=[bkey], w=[PS(bb)])

                def epi_gen(chunk, qb):
                    qsl = slice(qb * TT, (qb + 1) * TT)
                    mi = epi_count[0] % 2
                    epi_count[0] += 1
                    ms = mst[mi]
                    MS = ("mst", mi)
                    if kind == "B":
                        S.op("dve", lambda e: e.reciprocal(out=et[0][:], in_=ps[:, 6, :]), w=[PS(6), E(0)])
                        S.op("dve", lambda e: e.tensor_copy(out=et[1][:], in_=ps[:, 4, :]), w=[PS(4), E(1)])
                        S.op("dve", lambda e: e.tensor_copy(out=et[2][:], in_=ps[:, 5, :]), w=[PS(5), E(2)])
                        S.dma("sp", lambda e: e.dma_start(out=et[4][64:128, :], in_=et[0][0:64, :]), "er1", r=[E(0)], w=[E(4)])
                        S.dma("sp", lambda e: e.dma_start(out=et[5][0:64, :], in_=et[0][64:128, :]), "er2", r=[E(0)], w=[E(5)])
                        yield
                        yield
                        yield
                        yield
                        S.op("dve", lambda e: e.tensor_tensor(out=et[1][0:64, :], in0=et[1][0:64, :], in1=et[0][0:64, :],
                                                              op=ALU.mult), r=[E(0)], w=[E(1)])
                        S.op("dve", lambda e: e.tensor_tensor(out=et[1][64:128, :], in0=et[1][64:128, :], in1=et[4][64:128, :],
                                                              op=ALU.mult), r=[E(4)], w=[E(1)])
                        S.op("dve", lambda e: e.tensor_tensor(out=et[2][0:64, :], in0=et[2][0:64, :], in1=et[5][0:64, :],
                                                              op=ALU.mult), r=[E(5)], w=[E(2)])
                        S.op("dve", lambda e: e.tensor_tensor(out=et[2][64:128, :], in0=et[2][64:128, :], in1=et[0][64:128, :],
                                                              op=ALU.mult), r=[E(0)], w=[E(2)])
                        S.op("dve", lambda e: e.scalar_tensor_tensor(out=et[3][:], in0=et[2][:], scalar=neglam[:, 0:1],
                                                                     in1=et[1][:], op0=ALU.mult, op1=ALU.add),
                             r=[E(2), E(1), "neglam"], w=[E(3)])
                        S.op("pool", lambda e: e.tensor_tensor(out=sqt_e[:], in0=et[3][:], in1=et[3][:], op=ALU.mult),
                             r=[E(3)], w=["sqt_e"])
                        yield
                        yield
                        yield
                        yield
                        while state["live"]:
                            bg_step()
                        state["hold"] = True
                        S.op("pe", lambda e: e.matmul(ps[:, 7, :], lhsT=ones128[:], rhs=sqt_e[:], start=True, stop=True),
                             r=["ones128", "sqt_e"], w=[PS(7)])
                        yield
                        S.op("act", lambda e: e.activation(out=et[4][:], in_=ps[:, 7, :], func=AF.Ln, bias=epsc[:],
                                                           scale=1.0), r=["epsc"], w=[PS(7), E(4)])
                        S.op("act", lambda e: e.activation(out=et[0][:], in_=et[4][:], func=AF.Exp, bias=zeroc[:],
                                                           scale=-0.5), r=[E(4), "zeroc"], w=[E(0)])
                        state["hold"] = False
                        yield
                        yield
                        S.op("dve", lambda e: e.scalar_tensor_tensor(
                            out=ms[:], in0=et[3][:], scalar=gsub[:, 0:1], in1=et[0][:], op0=ALU.mult, op1=ALU.mult),
                            r=[E(3), E(0), "gsub"], w=[MS])
                    else:
                        S.op("dve", lambda e: e.tensor_copy(out=et[0][:], in_=ps[:, 4, :]), w=[PS(4), E(0)])
                        S.op("dve", lambda e: e.tensor_copy(out=et[1][:], in_=ps[:, 5, :]), w=[PS(5), E(1)])
                        S.dma("sp", lambda e: e.dma_start(out=et[2][0:64, :], in_=et[0][64:128, :]), "er1", r=[E(0)], w=[("e2", 0)])
                        S.dma("sp", lambda e: e.dma_start(out=et[2][64:128, :], in_=et[1][0:64, :]), "er2", r=[E(1)], w=[("e2", 1)])
                        yield
                        yield
                        yield
                        yield
                        S.op("dve", lambda e: e.reciprocal(out=et[3][:], in_=et[2][:]), r=[("e2", 0), ("e2", 1)], w=[E(3)])
                        S.op("dve", lambda e: e.tensor_tensor(out=ms[0:64, :], in0=et[0][0:64, :], in1=et[3][0:64, :],
                                                              op=ALU.mult), r=[E(0), E(3)], w=[MS])
                        S.op("dve", lambda e: e.tensor_tensor(out=ms[64:128, :], in0=et[1][64:128, :], in1=et[3][64:128, :],
                                                              op=ALU.mult), r=[E(1), E(3)], w=[MS])
                    S.dma("sp", lambda e: e.dma_start(out=mix_s[chunk][:, qsl], in_=ms[:]),
                          ("mso", mi), r=[MS], w=[("mix", chunk, qb)])

                epis = []

                def epi_step():
                    for g in list(epis):
                        try:
                            next(g)
                        except StopIteration:
                            epis.remove(g)

                emit_qk(0)
                emit_qk(1)
                for n in range(len(its)):
                    qi, chunk, qb, kt = its[n]
                    b0 = 2 * (n % 2)
                    pt = pT[n % 3]
                    PT = ("pT", n % 3)
                    scale = SCALE
                    bias_ap = zeroc[:]
                    bias_key = "zeroc"
                    if kind == "B":
                        info = bias_info(qb, kt)
                        if info[0] == "far":
                            ci = 3 * h + info[1]
                            bias_ap = cfar_t[:, ci:ci + 1]
                            bias_key = "cfar_t"
                        else:
                            scale = 1.0
                    S.op("act", lambda e, pt=pt, b0=b0, bias_ap=bias_ap, scale=scale: e.activation(
                        out=pt[:], in_=ps[:, b0:b0 + 2, :], func=AF.Exp, bias=bias_ap, scale=scale),
                        r=[bias_key], w=[PS(b0), PS(b0 + 1), PT])
                    if n + 2 < len(its):
                        emit_qk(n + 2)
                    st, sp_ = (kt == 0), (kt == NKT - 1)
                    if kind == "B":
                        for mp, bank in ((0, 4), (1, 5)):
                            for half in (0, 1):
                                lo = 64 * half
                                S.op("pe", lambda e, pt=pt, kt=kt, st=st, sp_=sp_, mp=mp, bank=bank, lo=lo: e.matmul(
                                    ps[lo:lo + 64, bank, :], lhsT=vaug[:, kt, lo:lo + 64], rhs=pt[:, mp, :], start=st,
                                    stop=sp_, tile_position=(0, lo)), r=[VA, PT], w=[PS(bank)], sig=False)
                        S.op("pe", lambda e, pt=pt, st=st, sp_=sp_: e.matmul(
                            ps[0:64, 6, :], lhsT=onesb[:, 0:64], rhs=pt[:, 0, :], start=st, stop=sp_, tile_position=(0, 0)),
                            r=["onesb", PT], w=[PS(6)], sig=False)
                        S.op("pe", lambda e, pt=pt, st=st, sp_=sp_: e.matmul(
                            ps[64:128, 6, :], lhsT=onesb[:, 0:64], rhs=pt[:, 1, :], start=st, stop=sp_, tile_position=(0, 64)),
                            r=["onesb", PT], w=[PS(6)])
                    else:
                        S.op("pe", lambda e, pt=pt, kt=kt, st=st, sp_=sp_: e.matmul(
                            ps[:, 4, :], lhsT=vaug[:, kt, 0:128], rhs=pt[:, 0, :], start=st, stop=sp_),
                            r=[VA, PT], w=[PS(4)], sig=False)
                        S.op("pe", lambda e, pt=pt, kt=kt, st=st, sp_=sp_: e.matmul(
                            ps[:, 5, :], lhsT=vaug[:, kt, 64:192], rhs=pt[:, 1, :], start=st, stop=sp_),
                            r=[VA, PT], w=[PS(5)])
                    epi_step()
                    if sp_:
                        g = epi_gen(chunk, qb)
                        epis.append(g)
                        next(g)
                    if bg is not None and (n % every) == 0 and not state["hold"]:
                        bg_step()
                while epis:
                    epi_step()
                if bg is not None:
                    for _ in bg:
                        pass

            units = [("B", 0, 1), ("A", 0, 0), ("B", 1, 1), ("A", 1, 0), ("B", 2, 1), ("B", 3, 0)]
            est = {"A": 520, "B": 360}
            pg = prologue_gen()
            for _ in proj_gen(*units[0], first=True):
                next(pg, None)
            for ui, (kind, idx, ub) in enumerate(units):
                nits = 1024 if kind == "A" else 512
                if ui + 1 < len(units):
                    nk = units[ui + 1][0]
                    bg = itertools.chain(pg, proj_gen(*units[ui + 1]))
                    every = max(1, nits // est[nk])
                else:
                    bg, every = None, 1
                attention(kind, idx, ub, bg, every)
            S.barrier()
            S.replay()

        with contextlib.ExitStack() as fst:
            woutb = sb(fst, "woutb", [128, 8, 1024], BF16)
            xts = [sb(fst, "xt%d" % i, [128, 8, 512], F32) for i in range(2)]
            mxs = [sb(fst, "mx%d" % i, [128, 8, 512], BF16) for i in range(2)]
            h2 = sb(fst, "h2", [128, 8, 512], BF16)
            uT = sb(fst, "uT", [128, 32, 512], BF16)
            wsb = [sb(fst, "wsb%d" % i, [128, 4096], BF16) for i in range(4)]
            lnv = sb(fst, "lnv", [128, 512], F32)
            rstd = sb(fst, "rstd", [128, 512], F32)
            rl = [sb(fst, "rl%d" % i, [128, 512], F32) for i in range(2)]

            S.dma("pool", lambda e: e.dma_start(out=woutb[:], in_=wout), "woutb", w=["woutb"])

            loads = []
            for i in range(NQB):
                for g in range(8):
                    loads.append(("u", g))
                for m in range(8):
                    loads.append(("d", m))
            issued = [0]

            def issue_loads(upto):
                while issued[0] < min(upto, len(loads)):
                    k = issued[0]
                    ty, j = loads[k]
                    src = wup_s[j] if ty == "u" else wdn_s[j]
                    bi = k % 4
                    S.dma("sp", lambda e, src=src, bi=bi: e.dma_start(out=wsb[bi][:], in_=src), ("wsb", bi),
                          w=[("wsb", bi)])
                    issued[0] += 1

            def load_xt(i):
                S.dma("sp", lambda e: e.dma_start(out=xts[i % 2][:], in_=xT3[:, :, i * TT:(i + 1) * TT]), ("xt", i % 2),
                      w=[("xt", i % 2)])
                S.dma("sp", lambda e: e.dma_start(out=mxs[i % 2][:], in_=mix3[:, :, i * TT:(i + 1) * TT]), ("mx", i % 2),
                      w=[("mx", i % 2)])

            bankc = [0]

            def nb():
                b = bankc[0] % 8
                bankc[0] += 1
                return b

            load_xt(0)
            lk = 0
            for i in range(NQB):
                xt = xts[i % 2]
                XT = ("xt", i % 2)
                mx = mxs[i % 2]
                MX = ("mx", i % 2)
                tsl = slice(i * TT, (i + 1) * TT)
                issue_loads(lk + 2)
                for m in range(8):
                    b = nb()
                    for c in range(8):
                        S.op("pe", lambda e, b=b, c=c, m=m, mx=mx: e.matmul(
                            ps[:, b, :], lhsT=woutb[:, c, m * 128:(m + 1) * 128], rhs=mx[:, c, :], start=(c == 0),
                            stop=(c == 7)), r=["woutb", MX], w=[PS(b)], sig=(c == 7))
                    S.op("dve", lambda e, b=b, m=m, xt=xt: e.tensor_tensor(out=xt[:, m, :], in0=ps[:, b, :], in1=xt[:, m, :],
                                                                            op=ALU.add), w=[PS(b), XT])
                if i + 1 < NQB:
                    load_xt(i + 1)
                S.op("act", lambda e, xt=xt: e.activation(out=h2[:], in_=xt[:], func=AF.Square), r=[XT], w=["h2"])
                b = nb()
                for c in range(8):
                    S.op("pe", lambda e, b=b, c=c: e.matmul(ps[:, b, :], lhsT=ones1024[:], rhs=h2[:, c, :], start=(c == 0),
                                                            stop=(c == 7)), r=["ones1024", "h2"], w=[PS(b)], sig=(c == 7))
                S.op("act", lambda e, b=b: e.activation(out=lnv[:], in_=ps[:, b, :], func=AF.Ln, bias=epsc[:], scale=1.0),
                     r=["epsc"], w=[PS(b), "lnv"])
                S.op("act", lambda e: e.activation(out=rstd[:], in_=lnv[:], func=AF.Exp, bias=zeroc[:], scale=-0.5),
                     r=["lnv", "zeroc"], w=["rstd"])
                for c in range(8):
                    S.op("dve", lambda e, c=c, xt=xt: e.tensor_tensor(out=h2[:, c, :], in0=xt[:, c, :], in1=rstd[:],
                                                                      op=ALU.mult), r=[XT, "rstd"], w=["h2"])
                for g in range(8):
                    bi = lk % 4
                    issue_loads(lk + 3)
                    for jj in range(4):
                        j = 4 * g + jj
                        b = nb()
                        for c in range(8):
                            S.op("pe", lambda e, b=b, c=c, jj=jj, bi=bi: e.matmul(
                                ps[:, b, :], lhsT=wsb[bi][:, c * 512 + jj * 128:c * 512 + (jj + 1) * 128], rhs=h2[:, c, :],
                                start=(c == 0), stop=(c == 7)), r=[("wsb", bi), "h2"], w=[PS(b)], sig=(c == 7))
                        ri = j % 2
                        S.op("act", lambda e, b=b, ri=ri: e.activation(out=rl[ri][:], in_=ps[:, b, :], func=AF.Relu),
                             w=[PS(b), ("rl", ri)])
                        S.op("pool", lambda e, j=j, ri=ri: e.tensor_tensor(out=uT[:, j, :], in0=rl[ri][:], in1=rl[ri][:],
                                                                           op=ALU.mult), r=[("rl", ri)], w=["uT"])
                    lk += 1
                for m in range(8):
                    bi = lk % 4
                    issue_loads(lk + 3)
                    b = nb()
                    for j in range(32):
                        S.op("pe", lambda e, b=b, j=j, bi=bi: e.matmul(
                            ps[:, b, :], lhsT=wsb[bi][:, j * 128:(j + 1) * 128], rhs=uT[:, j, :], start=(j == 0),
                            stop=(j == 31)), r=[("wsb", bi), "uT"], w=[PS(b)], sig=(j == 31))
                    S.op("dve", lambda e, b=b, m=m, xt=xt: e.tensor_tensor(out=xt[:, m, :], in0=ps[:, b, :], in1=xt[:, m, :],
                                                                            op=ALU.add), w=[PS(b), XT])
                    lk += 1
                S.dma("sp", lambda e, xt=xt, tsl=tsl: e.dma_start(out=outT3[:, :, tsl], in_=xt[:]), ("out", i % 2),
                      r=[XT], w=[("outT", i)])
            S.wait_slots("sp", [("out", 0), ("out", 1)])
            S.barrier()
            S.replay()
    return nc


def _host_tables():
    import jax
    import jax.numpy as jnp
    cpu = jax.devices("cpu")[0]
    with jax.default_device(cpu):
        rows = SEQ // 64
        row = jnp.broadcast_to(jnp.arange(rows)[:, None], (rows, 64)).reshape(-1).astype(jnp.float32)
        col = jnp.broadcast_to(jnp.arange(64)[None, :], (rows, 64)).reshape(-1).astype(jnp.float32)
        half = HD // 2
        inv_freq = 1.0 / (10000.0 ** (jnp.arange(0, half, 2, dtype=jnp.float32) / half))
        ang_r = row[:, None] * inv_freq[None, :]
        ang_c = col[:, None] * inv_freq[None, :]
        cr, sr, cc, sc = (np.asarray(t.astype(jnp.float32)) for t in
                          (jnp.cos(ang_r), jnp.sin(ang_r), jnp.cos(ang_c), jnp.sin(ang_c)))

        def t5_bucket(rel):
            nb = 16
            max_exact = 8
            ret = (rel > 0).astype(jnp.int32) * nb
            n = jnp.abs(rel)
            nf = jnp.maximum(n, 1).astype(jnp.float32)
            large = max_exact + (jnp.log(nf / max_exact) / math.log(128 / max_exact) * (nb - max_exact)).astype(jnp.int32)
            large = jnp.minimum(large, nb - 1)
            return ret + jnp.where(n < max_exact, n, large)

        rel = jnp.arange(-SEQ, SEQ + 1, dtype=jnp.int32)
        bucket = np.asarray(t5_bucket(rel))
    C = np.empty((64, SEQ), np.float32)
    Sg = np.empty((64, SEQ), np.float32)
    for d in range(64):
        j = d % 16
        first = (d % 32) < 16
        if d < 32:
            c_, s_ = cr[:, j], sr[:, j]
        else:
            c_, s_ = cc[:, j], sc[:, j]
        C[d] = c_
        Sg[d] = -s_ if first else s_
    return C, Sg, bucket


def _partner(d):
    return d + 16 if (d % 32) < 16 else d - 16


def _prep(inputs):
    f = lambda k: np.asarray(inputs[k], dtype=np.float32)
    x = f("x")
    w_in = f("w_in")[0]
    C, Sg, bucket = _host_tables()
    rel_bias = f("rel_bias")
    dd = np.arange(64)
    pd = np.array([_partner(d) for d in range(64)])

    def chunked(w):
        return np.ascontiguousarray(w.reshape(8, 128, -1).transpose(1, 0, 2))

    wA = np.empty((2, 128, 8, 896), np.float32)
    for kv in range(2):
        cols = []
        for pair in range(2):
            hs = [4 * kv + 2 * pair, 4 * kv + 2 * pair + 1]
            cols.append(np.concatenate([h * 64 + dd for h in hs]))
            cols.append(np.concatenate([h * 64 + pd for h in hs]))
        kc = 512 + kv * 64
        cols.append(np.concatenate([kc + dd, kc + dd]))
        cols.append(np.concatenate([kc + pd, kc + pd]))
        vc = 640 + kv * 64
        cols.append(np.concatenate([vc + dd, vc + dd]))
        wA[kv] = chunked(w_in[:, np.concatenate(cols)])
    wB = np.empty((4, 128, 8, 384), np.float32)
    for h in range(4):
        cols = np.concatenate([768 + h * 128 + np.arange(128), 1280 + h * 128 + np.arange(128),
                               1792 + h * 128 + np.arange(128)])
        wB[h] = chunked(w_in[:, cols])

    gcol = np.zeros((128, 24), np.float32)
    gcol[:, 0:8] = f("attn_norm_g")[0].reshape(8, 128).T
    gcol[:, 8:16] = f("mlp_norm_g")[0].reshape(8, 128).T
    p64 = np.arange(128) % 64
    aq, ak, bq, bk = f("a_q_norm_g")[0], f("a_k_norm_g")[0], f("b_q_norm_g")[0], f("b_k_norm_g")[0]
    gcol[:, 16] = aq[p64]
    gcol[:, 17] = aq[pd[p64]]
    gcol[:, 18] = ak[p64]
    gcol[:, 19] = ak[pd[p64]]
    gcol[:, 20] = bq[p64]
    gcol[:, 21] = bk[p64]
    gcol[:, 22] = f("b_subln_g")[0]

    lamv = np.empty((128, 256), np.float32)
    lamv[:, 0:64] = f("lambda_q1")[0][None, :]
    lamv[:, 64:128] = f("lambda_k1")[0][None, :]
    lamv[:, 128:192] = f("lambda_q2")[0][None, :]
    lamv[:, 192:256] = f("lambda_k2")[0][None, :]

    cmat = np.zeros((128, 4, 128), np.float32)
    cmat[:, 0, :] = np.eye(128, dtype=np.float32)
    for m in range(128):
        cmat[(m + 64) % 128, 1, m] = 1.0
    cmat[0:64, 2, :] = 1.0 / 64.0
    cmat[64:128, 3, :] = 1.0 / 64.0

    wout = chunked(f("w_out")[0])
    w_up = f("w_up")[0]
    wup = np.ascontiguousarray(w_up.reshape(8, 128, 8, 512).transpose(2, 1, 0, 3))
    w_down = f("w_down")[0]
    wdn = np.ascontiguousarray(w_down.reshape(32, 128, 8, 128).transpose(2, 1, 0, 3))

    ii = np.arange(128)[:, None]
    uu = np.arange(1152)[None, :]
    btoep = np.ascontiguousarray(rel_bias[bucket[(ii - uu + 512) + SEQ]].transpose(0, 2, 1))

    common = dict(wA=wA, wB=wB, gcol=gcol, lamv=lamv, btoep=btoep, cmat=cmat, wout=wout, wup=wup, wdn=wdn)
    in_maps = []
    for core in range(8):
        b, qh = core // 2, core % 2
        order = np.concatenate([np.arange(qh * NQ, (qh + 1) * NQ), np.arange((1 - qh) * NQ, (2 - qh) * NQ)])
        xT = np.ascontiguousarray(x[b].T[:, order])
        ropeC = np.ascontiguousarray(np.concatenate([C, C], axis=0)[:, order])
        ropeS = np.ascontiguousarray(np.concatenate([Sg, Sg], axis=0)[:, order])
        pos = order
        jj = np.arange(512)[None, :]
        bcross = np.empty((128, 4, 2, 512), np.float32)
        r0 = pos[4096 + np.arange(128)][:, None] - pos[3584 + np.arange(512)][None, :]
        r1 = pos[8064 + np.arange(128)][:, None] - pos[0 + np.arange(512)][None, :]
        bcross[:, :, 0, :] = rel_bias[bucket[r0 + SEQ]].transpose(0, 2, 1)
        bcross[:, :, 1, :] = rel_bias[bucket[r1 + SEQ]].transpose(0, 2, 1)
        cfar = np.empty((128, 12), np.float32)
        for h in range(4):
            cfar[:, 3 * h + 0] = rel_bias[15, h]
            cfar[:, 3 * h + 1] = rel_bias[31, h]
            cfar[:, 3 * h + 2] = rel_bias[31, h] if qh == 0 else rel_bias[15, h]
        m = dict(common)
        m.update(xT=xT, ropeC=ropeC, ropeS=ropeS, bcross=bcross, cfar=cfar)
        in_maps.append(m)
    return in_maps


_NC_CACHE = {}


def kernel(**inputs):
    in_maps = _prep(inputs)
    if "nc" not in _NC_CACHE:
        _NC_CACHE["nc"] = build_program()
    nc = _NC_CACHE["nc"]
    res = run_bass_kernel_spmd(nc, in_maps, core_ids=list(range(8)))
    out = np.empty((4, SEQ, D_MODEL), np.float32)
    for core in range(8):
        b, qh = core // 2, core % 2
        out[b, qh * NQ:(qh + 1) * NQ, :] = res.results[core]["outT"].T
    return out
```
